# Optimizing a Trainium2 kernel written in Bass

```python
import jax, jax.numpy as jnp
from jax import lax
import numpy as np

D_MODEL = 1024
BATCH = 8
SEQ = 4096
DEPTH = 4

N_META = 16
BLOCK = 128
N_PAD = (-N_META) % BLOCK

SB_HEADS = 8
SB_HEAD_DIM = 64
MLA_HEADS = 8
MLA_NOPE = 64
MLA_ROPE = 32
MLA_V = 64
MLA_Q_LORA = 384
MLA_KV_LORA = 256
RET_HEADS = 4
RET_QK = 64
RET_V = 128

D_SB = SB_HEADS * SB_HEAD_DIM
D_MLA = MLA_HEADS * MLA_V
D_RET = RET_HEADS * RET_V
D_MIX = D_SB + D_MLA + D_RET
D_FF = 4 * D_MODEL

IN_SIZES = (D_SB, D_SB, D_SB,
            MLA_Q_LORA, MLA_KV_LORA, MLA_ROPE,
            RET_HEADS * RET_QK, RET_HEADS * RET_QK, D_RET, D_RET)
IN_SPLITS = tuple(int(s) for s in np.cumsum(IN_SIZES)[:-1])
N_IN = int(sum(IN_SIZES))

ROPE_THETA = 10000.0
LN_EPS = 1e-5
DN_ALPHA = (2 * DEPTH) ** 0.25
DN_BETA = (8 * DEPTH) ** -0.25
RET_GAMMA = tuple(1.0 - 2.0 ** (-5 - h) for h in range(RET_HEADS))

kernel_name = "hybrid_sb_mla_retention_deepnorm"


def layer_norm(x, g, b):
    x32 = x.astype(jnp.float32)
    mu = jnp.mean(x32, -1, keepdims=True)
    var = jnp.mean(jnp.square(x32 - mu), -1, keepdims=True)
    y = (x32 - mu) * lax.rsqrt(var + LN_EPS)
    return (y * g.astype(jnp.float32) + b.astype(jnp.float32)).astype(x.dtype)


def rms_norm(x, g):
    x32 = x.astype(jnp.float32)
    y = x32 * lax.rsqrt(jnp.mean(jnp.square(x32), -1, keepdims=True) + LN_EPS)
    return (y * g.astype(jnp.float32)).astype(x.dtype)


def head_norm(y):
    y32 = y.astype(jnp.float32)
    mu = jnp.mean(y32, -1, keepdims=True)
    var = jnp.mean(jnp.square(y32 - mu), -1, keepdims=True)
    return (y32 - mu) * lax.rsqrt(var + LN_EPS)


def apply_rope(x, pos):
    half = x.shape[-1] // 2
    inv = ROPE_THETA ** (-jnp.arange(half, dtype=jnp.float32) / half)
    ang = pos[:, None] * inv[None, :]
    cos = jnp.cos(ang)[:, None, :]
    sin = jnp.sin(ang)[:, None, :]
    x1, x2 = x[..., :half], x[..., half:]
    return jnp.concatenate([x1 * cos - x2 * sin, x1 * sin + x2 * cos], -1).astype(x.dtype)


def stick_breaking_attention(q, k, v, valid):
    L, d = q.shape[1], q.shape[-1]
    scale = d ** -0.5
    outs = []
    for i in range(L // BLOCK):
        q0, q1 = i * BLOCK, (i + 1) * BLOCK
        z = jnp.einsum("bqhd,bkhd->bhqk", q[:, q0:q1], k[:, :q1]).astype(jnp.float32) * scale
        t_idx = jnp.arange(q0, q1)[:, None]
        s_idx = jnp.arange(q1)[None, :]
        mask = (s_idx < t_idx) & valid[None, :q1]
        log_beta = jnp.where(mask, jax.nn.log_sigmoid(z), -jnp.inf)
        log_keep = jnp.where(mask, jax.nn.log_sigmoid(-z), 0.0)
        incl = lax.cumsum(log_keep, axis=3, reverse=True)
        excl = jnp.concatenate([incl[..., 1:], jnp.zeros_like(incl[..., :1])], axis=-1)
        w = jnp.exp(log_beta + excl)
        outs.append(jnp.einsum("bhqk,bkhd->bqhd", w.astype(v.dtype), v[:, :q1]))
    return jnp.concatenate(outs, axis=1)


def mla_attention(q_nope, q_rope, k_nope, k_rope, v, valid):
    L = q_nope.shape[1]
    scale = (MLA_NOPE + MLA_ROPE) ** -0.5
    outs = []
    for i in range(L // BLOCK):
        q0, q1 = i * BLOCK, (i + 1) * BLOCK
        s = (jnp.einsum("bqhd,bkhd->bhqk", q_nope[:, q0:q1], k_nope[:, :q1])
             + jnp.einsum("bqhd,bkd->bhqk", q_rope[:, q0:q1], k_rope[:, :q1])).astype(jnp.float32) * scale
        t_idx = jnp.arange(q0, q1)[:, None]
        s_idx = jnp.arange(q1)[None, :]
        mask = (s_idx <= t_idx) & (valid[None, :q1] | (s_idx == t_idx))
        p = jax.nn.softmax(jnp.where(mask, s, -jnp.inf), axis=-1)
        outs.append(jnp.einsum("bhqk,bkhd->bqhd", p.astype(v.dtype), v[:, :q1]))
    return jnp.concatenate(outs, axis=1)


def multiscale_retention(q, k, v):
    B, L, H, dk = q.shape
    dv = v.shape[-1]
    n = L // BLOCK
    log_g = jnp.log(jnp.array(RET_GAMMA, jnp.float32))
    idx = jnp.arange(BLOCK, dtype=jnp.float32)
    diff = idx[:, None] - idx[None, :]
    d_in = jnp.where(diff[None] >= 0, jnp.exp(jnp.maximum(diff, 0.0)[None] * log_g[:, None, None]), 0.0)
    q_decay = jnp.exp((idx[:, None] + 1.0) * log_g[None, :])
    k_decay = jnp.exp((BLOCK - 1.0 - idx[:, None]) * log_g[None, :])
    c_decay = jnp.exp(BLOCK * log_g)

    def to_chunks(a):
        return jnp.moveaxis(a.astype(jnp.float32).reshape(B, n, BLOCK, H, a.shape[-1]), 1, 0)

    def step(state, inp):
        qc, kc, vc = inp
        inner = jnp.einsum("bqhd,bkhd->bhqk", qc, kc) * d_in[None]
        y = (jnp.einsum("bhqk,bkhe->bqhe", inner, vc)
             + jnp.einsum("bqhd,bhde->bqhe", qc, state) * q_decay[None, :, :, None])
        state = (state * c_decay[None, :, None, None]
                 + jnp.einsum("bkhd,bkhe->bhde", kc * k_decay[None, :, :, None], vc))
        return state, y

    state0 = jnp.zeros((B, H, dk, dv), jnp.float32)
    _, ys = lax.scan(step, state0, (to_chunks(q), to_chunks(k), to_chunks(v)))
    return jnp.moveaxis(ys, 0, 1).reshape(B, L, H, dv)


def hybrid_mixer(h, w_in, q_norm_g, kv_norm_g, w_uq, w_ukv, w_out, pos, valid):
    B, L, _ = h.shape
    proj = h @ w_in
    sb_q, sb_k, sb_v, c_q, c_kv, k_r, r_q, r_k, r_v, r_g = jnp.split(proj, IN_SPLITS, axis=-1)

    hs = lambda a, nh: a.reshape(B, L, nh, -1)
    out_a = stick_breaking_attention(hs(sb_q, SB_HEADS), hs(sb_k, SB_HEADS), hs(sb_v, SB_HEADS), valid)

    q = (rms_norm(c_q, q_norm_g) @ w_uq).reshape(B, L, MLA_HEADS, MLA_NOPE + MLA_ROPE)
    q_nope, q_rope = q[..., :MLA_NOPE], apply_rope(q[..., MLA_NOPE:], pos)
    kv = (rms_norm(c_kv, kv_norm_g) @ w_ukv).reshape(B, L, MLA_HEADS, MLA_NOPE + MLA_V)
    k_nope, v_b = kv[..., :MLA_NOPE], kv[..., MLA_NOPE:]
    k_rope = apply_rope(k_r[:, :, None, :], pos)[:, :, 0]
    out_b = mla_attention(q_nope, q_rope, k_nope, k_rope, v_b, valid)

    rq = apply_rope(hs(r_q, RET_HEADS), pos)
    rk = apply_rope(hs(r_k, RET_HEADS), pos) * (RET_QK ** -0.5)
    rk = jnp.where(valid[None, :, None, None], rk, jnp.zeros_like(rk))
    y_c = head_norm(multiscale_retention(rq, rk, hs(r_v, RET_HEADS))).reshape(B, L, D_RET)
    out_c = (jax.nn.silu(r_g.astype(jnp.float32)) * y_c).astype(h.dtype)

    mixed = jnp.concatenate([out_a.reshape(B, L, D_SB), out_b.reshape(B, L, D_MLA), out_c], axis=-1)
    return mixed @ w_out


def squared_relu_mlp(h, w1, w2):
    return jnp.square(jax.nn.relu(h @ w1)) @ w2


def setup_inputs(seed: int = 0) -> dict:
    key = jax.random.key(seed)
    ks = jax.random.split(key, 16)
    f32 = jnp.float32

    def nrm(k, shape, std):
        return jax.random.normal(k, shape, f32) * std

    return {
        "x": nrm(ks[0], (BATCH, SEQ, D_MODEL), 1.0),
        "meta_tokens": nrm(ks[1], (N_META, D_MODEL), 1.0),
        "ln_emb_g": 1.0 + nrm(ks[2], (D_MODEL,), 0.02),
        "ln_emb_b": nrm(ks[3], (D_MODEL,), 0.02),
        "w_in": nrm(ks[4], (DEPTH, D_MODEL, N_IN), D_MODEL ** -0.5),
        "mla_q_norm": 1.0 + nrm(ks[5], (DEPTH, MLA_Q_LORA), 0.02),
        "mla_kv_norm": 1.0 + nrm(ks[6], (DEPTH, MLA_KV_LORA), 0.02),
        "w_uq": nrm(ks[7], (DEPTH, MLA_Q_LORA, MLA_HEADS * (MLA_NOPE + MLA_ROPE)), MLA_Q_LORA ** -0.5),
        "w_ukv": nrm(ks[8], (DEPTH, MLA_KV_LORA, MLA_HEADS * (MLA_NOPE + MLA_V)), MLA_KV_LORA ** -0.5),
        "w_out": nrm(ks[9], (DEPTH, D_MIX, D_MODEL), DN_BETA * D_MIX ** -0.5),
        "ln1_g": 1.0 + nrm(ks[10], (DEPTH, D_MODEL), 0.02),
        "ln1_b": nrm(ks[11], (DEPTH, D_MODEL), 0.02),
        "w_ff1": nrm(ks[12], (DEPTH, D_MODEL, D_FF), D_MODEL ** -0.5),
        "w_ff2": nrm(ks[13], (DEPTH, D_FF, D_MODEL), DN_BETA * D_FF ** -0.5),
        "ln2_g": 1.0 + nrm(ks[14], (DEPTH, D_MODEL), 0.02),
        "ln2_b": nrm(ks[15], (DEPTH, D_MODEL), 0.02),
    }


def reference(x, meta_tokens, ln_emb_g, ln_emb_b, w_in, mla_q_norm, mla_kv_norm, w_uq, w_ukv, w_out,
              ln1_g, ln1_b, w_ff1, w_ff2, ln2_g, ln2_b):
    B, S, _ = x.shape
    meta = jnp.broadcast_to(meta_tokens[None].astype(x.dtype), (B, N_META, D_MODEL))
    pad = jnp.zeros((B, N_PAD, D_MODEL), x.dtype)
    h = jnp.concatenate([pad, meta, x], axis=1)
    L = h.shape[1]
    pos_i = jnp.arange(L) - N_PAD
    valid = pos_i >= 0
    pos = pos_i.astype(jnp.float32)
    h = layer_norm(h, ln_emb_g, ln_emb_b)
    for l in range(DEPTH):
        mix = hybrid_mixer(h, w_in[l], mla_q_norm[l], mla_kv_norm[l], w_uq[l], w_ukv[l], w_out[l], pos, valid)
        h = layer_norm(DN_ALPHA * h + mix, ln1_g[l], ln1_b[l])
        h = layer_norm(DN_ALPHA * h + squared_relu_mlp(h, w_ff1[l], w_ff2[l]), ln2_g[l], ln2_b[l])
    return h[:, N_PAD + N_META:]
```

```python
import numpy as np
import ml_dtypes
from contextlib import ExitStack
import concourse.bass as bass
import concourse.mybir as mybir
from concourse.bass_utils import run_bass_kernel_spmd

F32 = mybir.dt.float32
BF16 = mybir.dt.bfloat16
ALU = mybir.AluOpType
AF = mybir.ActivationFunctionType

D = 1024
SEQ = 4096
NLAYERS = 4
T = 4224
NBLK = 33
TILES = [(0, 128)] + [(128 + 512 * i, 512) for i in range(8)]
LN_EPS = 1e-5
DN_ALPHA = (2 * NLAYERS) ** 0.25
RET_GAMMA = [1.0 - 2.0 ** (-5 - h) for h in range(4)]
MLA_SCALE = 96 ** -0.5

O_SBQ, O_SBK, O_SBV, O_CQ, O_CKV, O_KR, O_KRR = 0, 512, 1024, 1536, 1920, 2176, 2208
O_RQ, O_RQR, O_RK, O_RKR, O_RV, O_RG = 2240, 2496, 2752, 3008, 3264, 3776
NA = 4288
SEM_LIM = 12000


class Builder:
    def __init__(self, nc, es):
        self.nc = nc
        self.es = es
        self.E = {"pe": nc.tensor, "act": nc.scalar, "dve": nc.vector, "pool": nc.gpsimd, "sp": nc.sync}
        self.cnt = {e: 0 for e in self.E}
        self.waited = {e: {} for e in self.E}
        self.sems = {}
        self.st = {}
        self.dq = {}
        self.flip = 0

    def sem(self, key):
        if key not in self.sems:
            self.sems[key] = self.es.enter_context(self.nc.semaphore("s%d" % len(self.sems)))
        return self.sems[key]

    def _wait(self, eng, tok):
        sk, val = tok
        w = self.waited[eng]
        if w.get(sk, 0) >= val:
            return
        w[sk] = val
        self.E[eng].wait_ge(self.sem(sk), val)

    def _deps(self, eng, r, w):
        toks = set()
        for k in r:
            s = self.st.get(k)
            if s and s[0]:
                toks.add(s[0])
        for k in w:
            s = self.st.get(k)
            if s:
                if s[0] and s[0][0][0] != eng:
                    toks.add(s[0])
                for sk, v in s[1].items():
                    if sk[0] != eng:
                        toks.add((sk, v))
        for t in toks:
            self._wait(eng, t)

    def _update(self, tok, r, w):
        sk, val = tok
        for k in r:
            s = self.st.setdefault(k, [None, {}])
            if s[1].get(sk, 0) < val:
                s[1][sk] = val
        for k in w:
            self.st[k] = [tok, {}]

    def op(self, eng, fn, r=(), w=()):
        self._deps(eng, r, w)
        ins = fn(self.E[eng])
        self.cnt[eng] += 1
        i = self.cnt[eng]
        sk = (eng, (i - 1) // SEM_LIM)
        val = (i - 1) % SEM_LIM + 1
        ins.then_inc(self.sem(sk), 1)
        self._update((sk, val), r, w)

    def dma(self, out, in_, r=(), w=(), q="sp"):
        d = self.dq.setdefault(q, {"uses": [0] * 8, "next": 0})
        slot = d["next"]
        d["next"] = (slot + 1) % 8
        sk = ("dma", q, slot)
        if d["uses"][slot] > 0:
            self._wait(q, (sk, 16 * d["uses"][slot]))
        self._deps(q, r, w)
        self.E[q].dma_start(out=out, in_=in_).then_inc(self.sem(sk), 16)
        d["uses"][slot] += 1
        self._update((sk, 16 * d["uses"][slot]), r, w)

    def all_tokens(self):
        toks = []
        for e in ("pe", "act", "dve", "pool"):
            i = self.cnt[e]
            if i > 0:
                toks.append(((e, (i - 1) // SEM_LIM), (i - 1) % SEM_LIM + 1))
        for q, d in self.dq.items():
            for slot, u in enumerate(d["uses"]):
                if u > 0:
                    toks.append((("dma", q, slot), 16 * u))
        return toks

    def barrier(self):
        toks = self.all_tokens()
        for e in self.E:
            for t in toks:
                if t[0][0] == e:
                    continue
                self._wait(e, t)

    def finish(self):
        for t in self.all_tokens():
            if t[0][0] == "dma":
                self._wait("sp", t)

    def mm(self, out, lhsT, rhs, start, stop, r, w, **kw):
        self.op("pe", lambda e: e.matmul(out, lhsT, rhs, start=start, stop=stop, **kw), r=r, w=w)

    def tr(self, out, in_, ident, r, w):
        self.op("pe", lambda e: e.transpose(out, in_, ident), r=r, w=w)

    def evac(self, out, in_, r, w, scale=None, eng=None):
        if eng is None:
            self.flip ^= 1
            eng = "act" if self.flip else "dve"
        if eng == "act":
            if scale is None:
                self.op("act", lambda e: e.copy(out=out, in_=in_), r=r, w=w)
            else:
                self.op("act", lambda e: e.mul(out=out, in_=in_, mul=scale), r=r, w=w)
        else:
            if scale is None:
                self.op("dve", lambda e: e.tensor_copy(out=out, in_=in_), r=r, w=w)
            else:
                self.op("dve", lambda e: e.tensor_scalar(out=out, in0=in_, scalar1=scale, scalar2=None,
                                                         op0=ALU.mult), r=r, w=w)


def build_nc(NL, debug=False):
    nc = bass.Bass("TRN2", target_bir_lowering=False)

    def din(name, shape, dt=F32):
        return nc.dram_tensor(name, shape, dt, kind="ExternalInput").ap()

    def dscr(name, shape, dt):
        return nc.dram_tensor(name, shape, dt, kind="Internal").ap()

    x = din("x", [SEQ, D])
    meta = din("meta", [16, D])
    lnp = din("lnp", [128, 9, 2, 8])
    qn = din("qn", [128, NLAYERS, 3])
    kvn = din("kvn", [128, NLAYERS, 2])
    wa = din("wa", [NL, D, NA])
    wuq = din("wuq", [NL, 384, 1536])
    wukv = din("wukv", [NL, 256, 1024])
    wo = din("wo", [NL, 1536, D])
    w1 = din("w1", [NL, D, 4096])
    w2 = din("w2", [NL, 4096, D])
    c_ident = din("c_ident", [128, 128])
    c_uincl = din("c_uincl", [128, 128])
    c_ropeR = din("c_ropeR", [128, 4, T])
    c_ropeM = din("c_ropeM", [96, 2, T])
    c_qdec = din("c_qdec", [128, 2, 128])
    c_kdecT = din("c_kdecT", [128, 256])
    c_dt = din("c_dt", [128, 512])
    c_cvec = din("c_cvec", [128, 2])
    c_msb = din("c_msb", [128, 5, 512])
    c_mmla = din("c_mmla", [128, 5, 512])
    c_valid = din("c_valid", [128, 1])
    out = nc.dram_tensor("out", [SEQ, D], F32, kind="ExternalOutput").ap()

    hres = dscr("hres", [128, 8, T], F32)
    hT = dscr("hT", [128, 8, T], BF16)
    sbq = dscr("sbq", [128, 4, T], BF16)
    sbk = dscr("sbk", [128, 4, T], BF16)
    sbv = dscr("sbv", [128, NBLK, 512], BF16)
    mq = dscr("mq", [96, 8, T], BF16)
    mk = dscr("mk", [96, 8, T], BF16)
    mv = dscr("mv", [128, NBLK, 512], BF16)
    rq = dscr("rq", [128, 2, T], BF16)
    rqd = dscr("rqd", [128, 2, T], BF16)
    rk = dscr("rk", [128, 2, T], BF16)
    rkd = dscr("rkd", [128, NBLK, 256], BF16)
    rv = dscr("rv", [128, NBLK, 512], BF16)
    rg = dscr("rg", [128, 4, T], F32)
    mix = dscr("mix", [128, 12, T], BF16)

    with ExitStack() as es:
        B = Builder(nc, es)

        uniq = {"n": 0}

        def sb(stack, name, shape, dt):
            uniq["n"] += 1
            return stack.enter_context(nc.sbuf_tensor("%s_%d" % (name, uniq["n"]), shape, dt))

        ps = [es.enter_context(nc.psum_tensor("ps%d" % i, [128, 512], F32)) for i in range(8)]
        rot = {"i": 0}

        def bank(lo=0, hi=8):
            n = hi - lo
            b = lo + rot["i"] % n
            rot["i"] += 1
            return b

        identF = sb(es, "identF", [128, 128], F32)
        identB = sb(es, "identB", [128, 128], BF16)
        onesF = sb(es, "onesF", [128, 128], F32)
        onesB = sb(es, "onesB", [128, 64], BF16)
        lnps = sb(es, "lnps", [128, 9, 2, 8], F32)
        qns = sb(es, "qns", [128, NLAYERS, 3], F32)
        kvns = sb(es, "kvns", [128, NLAYERS, 2], F32)
        B.dma(identF[:], c_ident, w=["identF"])
        B.dma(identB[:], c_ident, w=["identB"], q="pool")
        B.dma(lnps[:], lnp, w=["lnps"])
        B.dma(qns[:], qn, w=["qns"])
        B.dma(kvns[:], kvn, w=["kvns"])
        B.op("pool", lambda e: e.memset(onesF[:], 1.0), w=["onesF"])
        B.op("pool", lambda e: e.memset(onesB[:], 1.0), w=["onesB"])

        def ln_f(ph, X, kx, SQ, ksq, R, XB, W, lnidx, t0, final):
            xs = X[:, :, 0:W]
            mp = bank()
            for c in range(8):
                B.mm(ps[mp][:, 0:W], onesF[:], X[:, c, 0:W], c == 0, c == 7, r=[kx, "onesF"], w=[("ps", mp)])
            B.op("dve", lambda e: e.scalar_tensor_tensor(
                out=xs, in0=ps[mp][:, 0:W].unsqueeze(1).broadcast_to([128, 8, W]), scalar=-1.0 / D, in1=xs,
                op0=ALU.mult, op1=ALU.add), r=[("ps", mp), kx], w=[kx])
            B.op("act", lambda e: e.activation(out=SQ[:, :, 0:W], in_=xs, func=AF.Square), r=[kx], w=ksq)
            vp = bank()
            for c in range(8):
                B.mm(ps[vp][:, 0:W], onesF[:], SQ[:, c, 0:W], c == 0, c == 7, r=ksq + ["onesF"], w=[("ps", vp)])
            B.op("act", lambda e: e.activation(out=R[:, 0:W], in_=ps[vp][:, 0:W], func=AF.Ln, scale=1.0 / D,
                                               bias=LN_EPS), r=[("ps", vp)], w=["R"])
            B.op("act", lambda e: e.activation(out=R[:, 0:W], in_=R[:, 0:W], func=AF.Exp, scale=-0.5),
                 r=["R"], w=["R"])
            B.op("dve", lambda e: e.tensor_tensor(out=xs, in0=xs, in1=R[:, 0:W].unsqueeze(1).broadcast_to([128, 8, W]),
                                                  op=ALU.mult), r=[kx, "R"], w=[kx])
            for c in range(8):
                B.op("act", lambda e, c=c: e.activation(out=X[:, c, 0:W], in_=X[:, c, 0:W], func=AF.Identity,
                                                        scale=lnps[:, lnidx, 0, c:c + 1],
                                                        bias=lnps[:, lnidx, 1, c:c + 1]),
                     r=[kx, "lnps"], w=[kx])
            B.op("pool", lambda e: e.tensor_copy(out=XB[:, :, 0:W], in_=xs), r=[kx], w=["XB"])
            hk = ("h", t0)
            B.dma(hres[:, :, t0:t0 + W], xs, r=[kx], w=[("hres", t0)])
            B.dma(hT[:, :, t0:t0 + W], XB[:, :, 0:W], r=["XB"], w=[("hT", t0)])
            if final and t0 >= 128:
                for s in range(W // 128):
                    OT = ph["OT"][s % 2]
                    ko = "OT%d" % (s % 2)
                    for half in range(2):
                        b = bank()
                        for cc in range(4):
                            c = half * 4 + cc
                            B.tr(ps[b][:, cc * 128:(cc + 1) * 128], X[:, c, s * 128:(s + 1) * 128], identF[:],
                                 r=[kx, "identF"], w=[("ps", b)])
                        B.evac(OT[:, half * 512:(half + 1) * 512], ps[b][:], r=[("ps", b)], w=[ko])
                    tok0 = t0 - 128 + s * 128
                    B.dma(out[tok0:tok0 + 128, :], OT[:], r=[ko], w=[("out", tok0)])

        def phase_embed():
            with ExitStack() as ph:
                xt = [sb(ph, "e_xt%d" % i, [128, 4, D], F32) for i in range(2)]
                X = [sb(ph, "e_X%d" % i, [128, 8, 512], F32) for i in range(2)]
                SQ = sb(ph, "e_SQ", [128, 8, 512], F32)
                R = sb(ph, "e_R", [128, 512], F32)
                XB = sb(ph, "e_XB", [128, 8, 512], BF16)
                for ti, (t0, W) in enumerate(TILES):
                    a = xt[ti % 2]
                    ka = "e_xt%d" % (ti % 2)
                    kx = "e_X%d" % (ti % 2)
                    nsub = W // 128
                    if ti == 0:
                        B.op("pool", lambda e: e.memset(a[:, 0, :], 0.0), w=[ka])
                        B.dma(a[112:128, 0, :], meta, w=[ka])
                    else:
                        B.dma(a[:], x[t0 - 128:t0 - 128 + 512, :].rearrange("(s p) d -> p s d", p=128), w=[ka])
                    for c in range(8):
                        b = bank()
                        for s in range(nsub):
                            B.tr(ps[b][:, s * 128:(s + 1) * 128], a[:, s, c * 128:(c + 1) * 128], identF[:],
                                 r=[ka, "identF"], w=[("ps", b)])
                        B.evac(X[ti % 2][:, c, 0:W], ps[b][:, 0:W], r=[("ps", b)], w=[kx])
                    ln_f(None, X[ti % 2], kx, SQ[:], ["SQ"], R, XB, W, 0, t0, False)
                B.barrier()

        def phase_A(l):
            with ExitStack() as ph:
                WA = sb(ph, "a_WA", [128, 8, NA], BF16)
                WUQ = sb(ph, "a_WUQ", [128, 3, 1536], BF16)
                WUKV = sb(ph, "a_WUKV", [128, 2, 1024], BF16)
                for c in range(8):
                    B.dma(WA[:, c, :], wa[l, c * 128:(c + 1) * 128, :], w=["WA"], q="pool")
                B.dma(WUQ[:], wuq[l].rearrange("(c p) n -> p c n", p=128), w=["WUQ"], q="pool")
                B.dma(WUKV[:], wukv[l].rearrange("(c p) n -> p c n", p=128), w=["WUKV"], q="pool")
                QDEC = sb(ph, "a_QDEC", [128, 2, 128], F32)
                KDEC = sb(ph, "a_KDEC", [128, 256], F32)
                B.dma(QDEC[:], c_qdec, w=["QDEC"])
                B.dma(KDEC[:], c_kdecT, w=["KDEC"])
                HT = [sb(ph, "a_HT%d" % i, [128, 8, 512], BF16) for i in range(2)]
                RT = [sb(ph, "a_RT%d" % i, [128, 4, 512], F32) for i in range(2)]
                MT = [sb(ph, "a_MT%d" % i, [96, 2, 512], F32) for i in range(2)]
                MK = [sb(ph, "a_MK%d" % i, [32, 2, 512], F32) for i in range(2)]
                CQ = sb(ph, "a_CQ", [128, 3, 512], F32)
                CKV = sb(ph, "a_CKV", [128, 2, 512], F32)
                SQ = sb(ph, "a_SQ", [128, 3, 512], F32)
                R = sb(ph, "a_R", [128, 512], F32)
                CQN = sb(ph, "a_CQN", [128, 3, 512], BF16)
                CKVN = sb(ph, "a_CKVN", [128, 2, 512], BF16)
                T1 = [sb(ph, "a_T1%d" % i, [128, 512], F32) for i in range(3)]
                T2 = [sb(ph, "a_T2%d" % i, [128, 512], F32) for i in range(3)]
                STG = [sb(ph, "a_STG%d" % i, [128, 512], BF16) for i in range(6)]
                RKB = [sb(ph, "a_RKB%d" % i, [128, 512], BF16) for i in range(2)]
                GS = [sb(ph, "a_GS0", [128, 4, 512], F32)] * 2
                KST = [sb(ph, "a_KST%d" % i, [128, 256], BF16) for i in range(2)]
                cn = {"stg": 0, "t": 0, "kst": 0}

                def stage():
                    i = cn["stg"] % 6
                    cn["stg"] += 1
                    return STG[i], "STG%d" % i

                def tt():
                    i = cn["t"] % 3
                    cn["t"] += 1
                    return T1[i], "T1%d" % i, T2[i], "T2%d" % i

                def load_tile(ti):
                    t0, W = TILES[ti]
                    p = ti % 2
                    B.dma(HT[p][:, :, 0:W], hT[:, :, t0:t0 + W], r=[("hT", t0)], w=["HT%d" % p])
                    B.dma(RT[p][:, :, 0:W], c_ropeR[:, :, t0:t0 + W], w=["RT%d" % p])
                    B.dma(MT[p][:, :, 0:W], c_ropeM[:, :, t0:t0 + W], w=["MT%d" % p])
                    B.dma(MK[p][:, :, 0:W], c_ropeM[64:96, :, t0:t0 + W], w=["MK%d" % p])

                load_tile(0)
                for ti, (t0, W) in enumerate(TILES):
                    p = ti % 2
                    if ti + 1 < len(TILES):
                        load_tile(ti + 1)
                    H = HT[p]
                    kh = "HT%d" % p
                    nsub = W // 128
                    blk0 = t0 // 128

                    def fgroup(col0, M, Wm=WA, kw="WA", rhs=None, krhs=None, nchunk=8):
                        b = bank()
                        for c in range(nchunk):
                            rr = H[:, c, 0:W] if rhs is None else rhs[:, c, 0:W]
                            B.mm(ps[b][0:M, 0:W], Wm[:, c, col0:col0 + M], rr, c == 0, c == nchunk - 1,
                                 r=[kw, kh if krhs is None else krhs], w=[("ps", b)])
                        return b

                    for m in range(4):
                        b = fgroup(O_SBQ + m * 128, 128)
                        S_, ks = stage()
                        B.evac(S_[:, 0:W], ps[b][:, 0:W], r=[("ps", b)], w=[ks], scale=0.125)
                        B.dma(sbq[:, m, t0:t0 + W], S_[:, 0:W], r=[ks], w=[("sbq", ti)])
                    for m in range(4):
                        b = fgroup(O_SBK + m * 128, 128)
                        S_, ks = stage()
                        B.evac(S_[:, 0:W], ps[b][:, 0:W], r=[("ps", b)], w=[ks])
                        B.dma(sbk[:, m, t0:t0 + W], S_[:, 0:W], r=[ks], w=[("sbk", ti)])
                    for (col0, dst, nm) in ((O_SBV, sbv, "sbv"), (O_RV, rv, "rv")):
                        for s in range(nsub):
                            b = bank()
                            for c in range(8):
                                B.mm(ps[b][:, :], H[:, c, s * 128:(s + 1) * 128], WA[:, c, col0:col0 + 512],
                                     c == 0, c == 7, r=["WA", kh], w=[("ps", b)])
                            S_, ks = stage()
                            B.evac(S_[:, :], ps[b][:, :], r=[("ps", b)], w=[ks])
                            B.dma(dst[:, blk0 + s, :], S_[:, :], r=[ks], w=[(nm, ti)])
                    for m in range(3):
                        b = fgroup(O_CQ + m * 128, 128)
                        B.evac(CQ[:, m, 0:W], ps[b][:, 0:W], r=[("ps", b)], w=["CQ"])
                    for m in range(2):
                        b = fgroup(O_CKV + m * 128, 128)
                        B.evac(CKV[:, m, 0:W], ps[b][:, 0:W], r=[("ps", b)], w=["CKV"])
                    ba = fgroup(O_KR, 32)
                    bb = fgroup(O_KRR, 32)
                    t1, k1, t2, k2 = tt()
                    B.op("dve", lambda e: e.tensor_tensor(out=t1[0:32, 0:W], in0=ps[ba][0:32, 0:W], in1=MK[p][:, 0, 0:W],
                                                          op=ALU.mult), r=[("ps", ba), "MK%d" % p], w=[k1])
                    B.op("dve", lambda e: e.tensor_tensor(out=t2[0:32, 0:W], in0=ps[bb][0:32, 0:W], in1=MK[p][:, 1, 0:W],
                                                          op=ALU.mult), r=[("ps", bb), "MK%d" % p], w=[k2])
                    S_, ks = stage()
                    B.op("pool", lambda e: e.tensor_tensor(out=S_[0:32, 0:W], in0=t1[0:32, 0:W], in1=t2[0:32, 0:W],
                                                           op=ALU.add), r=[k1, k2], w=[ks])
                    B.dma(mk[64:96, :, t0:t0 + W], S_[0:32, 0:W].unsqueeze(1).broadcast_to([32, 8, W]),
                          r=[ks], w=[("mk", ti)])
                    for pr in range(2):
                        ba = fgroup(O_RQ + pr * 128, 128)
                        bb = fgroup(O_RQR + pr * 128, 128)
                        t1, k1, t2, k2 = tt()
                        B.op("dve", lambda e: e.tensor_tensor(out=t1[:, 0:W], in0=ps[ba][:, 0:W], in1=RT[p][:, 0, 0:W],
                                                              op=ALU.mult), r=[("ps", ba), "RT%d" % p], w=[k1])
                        B.op("dve", lambda e: e.tensor_tensor(out=t2[:, 0:W], in0=ps[bb][:, 0:W], in1=RT[p][:, 1, 0:W],
                                                              op=ALU.mult), r=[("ps", bb), "RT%d" % p], w=[k2])
                        B.op("pool", lambda e: e.tensor_tensor(out=t1[:, 0:W], in0=t1[:, 0:W], in1=t2[:, 0:W],
                                                               op=ALU.add), r=[k1, k2], w=[k1])
                        S_, ks = stage()
                        B.op("act", lambda e: e.copy(out=S_[:, 0:W], in_=t1[:, 0:W]), r=[k1], w=[ks])
                        B.dma(rq[:, pr, t0:t0 + W], S_[:, 0:W], r=[ks], w=[("rq", ti)])
                        S2, ks2 = stage()
                        B.op("pool", lambda e: e.tensor_tensor(
                            out=S2[:, 0:W].rearrange("p (s n) -> p s n", n=128),
                            in0=t1[:, 0:W].rearrange("p (s n) -> p s n", n=128),
                            in1=QDEC[:, pr, :].unsqueeze(1).broadcast_to([128, nsub, 128]), op=ALU.mult),
                            r=[k1, "QDEC"], w=[ks2])
                        B.dma(rqd[:, pr, t0:t0 + W], S2[:, 0:W], r=[ks2], w=[("rqd", ti)])
                    for pr in range(2):
                        ba = fgroup(O_RK + pr * 128, 128)
                        bb = fgroup(O_RKR + pr * 128, 128)
                        t1, k1, t2, k2 = tt()
                        B.op("dve", lambda e: e.tensor_tensor(out=t1[:, 0:W], in0=ps[ba][:, 0:W], in1=RT[p][:, 2, 0:W],
                                                              op=ALU.mult), r=[("ps", ba), "RT%d" % p], w=[k1])
                        B.op("dve", lambda e: e.tensor_tensor(out=t2[:, 0:W], in0=ps[bb][:, 0:W], in1=RT[p][:, 3, 0:W],
                                                              op=ALU.mult), r=[("ps", bb), "RT%d" % p], w=[k2])
                        kb_ = "RKB%d" % pr
                        B.op("pool", lambda e: e.tensor_tensor(out=RKB[pr][:, 0:W], in0=t1[:, 0:W], in1=t2[:, 0:W],
                                                               op=ALU.add), r=[k1, k2], w=[kb_])
                        if ti == 0:
                            B.op("pool", lambda e: e.memset(RKB[pr][:, 0:112], 0.0), r=[kb_], w=[kb_])
                        B.dma(rk[:, pr, t0:t0 + W], RKB[pr][:, 0:W], r=[kb_], w=[("rk", ti)])
                    for s in range(nsub):
                        b = bank()
                        pb = ps[b][:].bitcast(BF16)
                        for pr in range(2):
                            B.tr(pb[:, pr * 128:(pr + 1) * 128], RKB[pr][:, s * 128:(s + 1) * 128], identB[:],
                                 r=["RKB%d" % pr, "identB"], w=[("ps", b)])
                        i = cn["kst"] % 2
                        cn["kst"] += 1
                        B.op("dve", lambda e: e.tensor_tensor(out=KST[i][:], in0=pb[:, 0:256], in1=KDEC[:], op=ALU.mult),
                             r=[("ps", b), "KDEC"], w=["KST%d" % i])
                        B.dma(rkd[:, blk0 + s, :], KST[i][:], r=["KST%d" % i], w=[("rkd", ti)])
                    G = GS[p]
                    for m in range(4):
                        b = fgroup(O_RG + m * 128, 128)
                        B.op("act", lambda e: e.activation(out=G[:, m, 0:W], in_=ps[b][:, 0:W], func=AF.Silu),
                             r=[("ps", b)], w=["GS"])
                    B.dma(rg[:, :, t0:t0 + W], G[:, :, 0:W], r=["GS"], w=[("rg", ti)])
                    for (src, ksrc, nch, dstn, kd, gam, inv) in ((CQ, "CQ", 3, CQN, "CQN", qns, 1.0 / 384),
                                                                   (CKV, "CKV", 2, CKVN, "CKVN", kvns, 1.0 / 256)):
                        B.op("act", lambda e: e.activation(out=SQ[:, 0:nch, 0:W], in_=src[:, :, 0:W], func=AF.Square),
                             r=[ksrc], w=["SQ"])
                        b = bank()
                        for m in range(nch):
                            B.mm(ps[b][:, 0:W], onesF[:], SQ[:, m, 0:W], m == 0, m == nch - 1, r=["SQ", "onesF"],
                                 w=[("ps", b)])
                        B.op("act", lambda e: e.activation(out=R[:, 0:W], in_=ps[b][:, 0:W], func=AF.Ln, scale=inv,
                                                           bias=LN_EPS), r=[("ps", b)], w=["R"])
                        B.op("act", lambda e: e.activation(out=R[:, 0:W], in_=R[:, 0:W], func=AF.Exp, scale=-0.5),
                             r=["R"], w=["R"])
                        for m in range(nch):
                            B.op("dve", lambda e, m=m: e.scalar_tensor_tensor(
                                out=dstn[:, m, 0:W], in0=src[:, m, 0:W], scalar=gam[:, l, m:m + 1], in1=R[:, 0:W],
                                op0=ALU.mult, op1=ALU.mult), r=[ksrc, "R", "qns", "kvns"], w=[kd])
                    for h in range(8):
                        ba = fgroup((2 * h) * 96, 96, Wm=WUQ, kw="WUQ", rhs=CQN, krhs="CQN", nchunk=3)
                        bb = fgroup((2 * h + 1) * 96, 96, Wm=WUQ, kw="WUQ", rhs=CQN, krhs="CQN", nchunk=3)
                        t1, k1, t2, k2 = tt()
                        B.op("dve", lambda e: e.tensor_tensor(out=t1[0:96, 0:W], in0=ps[ba][0:96, 0:W], in1=MT[p][:, 0, 0:W],
                                                              op=ALU.mult), r=[("ps", ba), "MT%d" % p], w=[k1])
                        B.op("dve", lambda e: e.tensor_tensor(out=t2[0:96, 0:W], in0=ps[bb][0:96, 0:W], in1=MT[p][:, 1, 0:W],
                                                              op=ALU.mult), r=[("ps", bb), "MT%d" % p], w=[k2])
                        S_, ks = stage()
                        B.op("pool", lambda e: e.tensor_tensor(out=S_[0:96, 0:W], in0=t1[0:96, 0:W], in1=t2[0:96, 0:W],
                                                               op=ALU.add), r=[k1, k2], w=[ks])
                        B.dma(mq[:, h, t0:t0 + W], S_[0:96, 0:W], r=[ks], w=[("mq", ti)])
                    for m in range(4):
                        b = fgroup(m * 128, 128, Wm=WUKV, kw="WUKV", rhs=CKVN, krhs="CKVN", nchunk=2)
                        S_, ks = stage()
                        B.evac(S_[:, 0:W], ps[b][:, 0:W], r=[("ps", b)], w=[ks])
                        B.dma(mk[0:64, 2 * m, t0:t0 + W], S_[0:64, 0:W], r=[ks], w=[("mk", ti)])
                        B.dma(mk[0:64, 2 * m + 1, t0:t0 + W], S_[64:128, 0:W], r=[ks], w=[("mk", ti)])
                    for s in range(nsub):
                        b = bank()
                        for c in range(2):
                            B.mm(ps[b][:, :], CKVN[:, c, s * 128:(s + 1) * 128], WUKV[:, c, 512:1024], c == 0, c == 1,
                                 r=["WUKV", "CKVN"], w=[("ps", b)])
                        S_, ks = stage()
                        B.evac(S_[:, :], ps[b][:, :], r=[("ps", b)], w=[ks])
                        B.dma(mv[:, blk0 + s, :], S_[:, :], r=[ks], w=[("mv", ti)])
                B.barrier()

        def phase_B1():
            with ExitStack() as ph:
                UI = sb(ph, "b_UI", [128, 128], F32)
                MS = sb(ph, "b_MS", [128, 5, 512], F32)
                VAL = sb(ph, "b_VAL", [128, 1], F32)
                B.dma(UI[:], c_uincl, w=["UI"])
                B.dma(MS[:], c_msb, w=["MS"])
                B.dma(VAL[:], c_valid, w=["VAL"])
                QT = [sb(ph, "b_QT%d" % i, [64, T], BF16) for i in range(2)]
                KT = [sb(ph, "b_KT%d" % i, [64, T], BF16) for i in range(2)]
                NK = [sb(ph, "b_NK%d" % i, [64, T], BF16) for i in range(2)]
                VV = [sb(ph, "b_VV%d" % i, [128, NBLK, 64], BF16) for i in range(2)]
                NR = 3
                EB = [sb(ph, "b_E%d" % i, [128, 512], F32) for i in range(NR)]
                LK = [sb(ph, "b_LK%d" % i, [128, 512], F32) for i in range(NR)]
                SS = [sb(ph, "b_S%d" % i, [128, 512], F32) for i in range(NR)]
                WW = [sb(ph, "b_W%d" % i, [128, 512], BF16) for i in range(NR)]
                CC = [sb(ph, "b_C%d" % i, [128, 512], F32) for i in range(2)]
                OS = [sb(ph, "b_OS%d" % i, [64, 512], BF16) for i in range(2)]

                def load_head(h):
                    p = h % 2
                    r0 = (h % 2) * 64
                    B.dma(QT[p][:], sbq[r0:r0 + 64, h // 2, :], r=[("sbq", i) for i in range(9)], w=["QT%d" % p])
                    B.dma(KT[p][:], sbk[r0:r0 + 64, h // 2, :], r=[("sbk", i) for i in range(9)], w=["KT%d" % p])
                    B.dma(VV[p][:], sbv[:, :, h * 64:(h + 1) * 64], r=[("sbv", i) for i in range(9)], w=["VV%d" % p])
                    B.op("pool", lambda e: e.tensor_scalar(out=NK[p][:], in0=KT[p][:], scalar1=-1.0, scalar2=None,
                                                           op0=ALU.mult), r=["KT%d" % p], w=["NK%d" % p])

                jobs = []
                for h in range(8):
                    for ti, (t0, W) in enumerate(TILES):
                        qb0 = t0 // 128
                        nkb = (t0 + W) // 128
                        for kb in reversed(range(nkb)):
                            jobs.append((h, ti, kb, kb == nkb - 1, kb == 0))
                njobs = len(jobs)

                def stage1(ji):
                    h, ti, kb, first, last = jobs[ji]
                    t0, W = TILES[ti]
                    p = h % 2
                    i = ji % NR
                    zb = ji % 2
                    B.mm(ps[zb][:, 0:W], KT[p][:, kb * 128:(kb + 1) * 128], QT[p][:, t0:t0 + W], True, True,
                         r=["KT%d" % p, "QT%d" % p], w=[("ps", zb)])
                    B.op("act", lambda e: e.activation(out=EB[i][:, 0:W], in_=ps[zb][:, 0:W], func=AF.Exp),
                         r=[("ps", zb)], w=["E%d" % i])
                    B.op("act", lambda e: e.activation(out=LK[i][:, 0:W], in_=EB[i][:, 0:W], func=AF.Ln, bias=1.0),
                         r=["E%d" % i], w=["LK%d" % i])
                    c = kb - t0 // 128
                    if ti == 0:
                        B.op("pool", lambda e: e.tensor_tensor(out=LK[i][:, 0:W], in0=LK[i][:, 0:W], in1=MS[:, 4, 0:W],
                                                               op=ALU.mult), r=["LK%d" % i, "MS"], w=["LK%d" % i])
                    elif c >= 0:
                        B.op("pool", lambda e: e.tensor_tensor(out=LK[i][:, 0:W], in0=LK[i][:, 0:W], in1=MS[:, c, 0:W],
                                                               op=ALU.mult), r=["LK%d" % i, "MS"], w=["LK%d" % i])
                    elif kb == 0:
                        B.op("pool", lambda e: e.tensor_scalar(out=LK[i][:, 0:W], in0=LK[i][:, 0:W], scalar1=VAL[:, 0:1],
                                                               scalar2=None, op0=ALU.mult),
                             r=["LK%d" % i, "VAL"], w=["LK%d" % i])

                def stage2(ji):
                    h, ti, kb, first, last = jobs[ji]
                    t0, W = TILES[ti]
                    p = h % 2
                    i = ji % NR
                    eb = 2 + ji % 2
                    tb = 4 + ji % 2
                    cp = (h * 9 + ti) % 2
                    Cc, kc = CC[cp], "C%d" % cp
                    B.mm(ps[eb][:, 0:W], UI[:], LK[i][:, 0:W], True, False, r=["UI", "LK%d" % i], w=[("ps", eb)])
                    B.mm(ps[eb][:, 0:W], NK[p][:, kb * 128:(kb + 1) * 128], QT[p][:, t0:t0 + W], False, True,
                         r=["NK%d" % p, "QT%d" % p], w=[("ps", eb)])
                    if not last:
                        B.mm(ps[tb][:, 0:W], onesF[:], LK[i][:, 0:W], True, True, r=["onesF", "LK%d" % i],
                             w=[("ps", tb)])
                    if first:
                        B.op("act", lambda e: e.activation(out=WW[i][:, 0:W], in_=ps[eb][:, 0:W], func=AF.Exp, scale=-1.0),
                             r=[("ps", eb)], w=["W%d" % i])
                        if not last:
                            B.op("dve", lambda e: e.tensor_copy(out=Cc[:, 0:W], in_=ps[tb][:, 0:W]),
                                 r=[("ps", tb)], w=[kc])
                    else:
                        B.op("dve", lambda e: e.tensor_tensor(out=SS[i][:, 0:W], in0=ps[eb][:, 0:W], in1=Cc[:, 0:W],
                                                              op=ALU.add), r=[("ps", eb), kc], w=["S%d" % i])
                        B.op("act", lambda e: e.activation(out=WW[i][:, 0:W], in_=SS[i][:, 0:W], func=AF.Exp, scale=-1.0),
                             r=["S%d" % i], w=["W%d" % i])
                        if not last:
                            B.op("dve", lambda e: e.tensor_tensor(out=Cc[:, 0:W], in0=ps[tb][:, 0:W], in1=Cc[:, 0:W],
                                                                  op=ALU.add), r=[("ps", tb), kc], w=[kc])
                    c = kb - t0 // 128
                    if ti == 0:
                        B.op("pool", lambda e: e.tensor_tensor(out=WW[i][:, 0:W], in0=WW[i][:, 0:W], in1=MS[:, 4, 0:W],
                                                               op=ALU.mult), r=["W%d" % i, "MS"], w=["W%d" % i])
                    elif c >= 0:
                        B.op("pool", lambda e: e.tensor_tensor(out=WW[i][:, 0:W], in0=WW[i][:, 0:W], in1=MS[:, c, 0:W],
                                                               op=ALU.mult), r=["W%d" % i, "MS"], w=["W%d" % i])
                    elif kb == 0:
                        B.op("pool", lambda e: e.tensor_scalar(out=WW[i][:, 0:W], in0=WW[i][:, 0:W], scalar1=VAL[:, 0:1],
                                                               scalar2=None, op0=ALU.mult),
                             r=["W%d" % i, "VAL"], w=["W%d" % i])

                def stage3(ji):
                    h, ti, kb, first, last = jobs[ji]
                    t0, W = TILES[ti]
                    p = h % 2
                    i = ji % NR
                    ob = 6 + (h * 9 + ti) % 2
                    if ti == 0 and first and h + 1 < 8:
                        load_head(h + 1)
                    B.mm(ps[ob][0:64, 0:W], VV[p][:, kb, :], WW[i][:, 0:W], first, last,
                         r=["VV%d" % p, "W%d" % i], w=[("ps", ob)])
                    if last:
                        oi = (h * 9 + ti) % 2
                        B.evac(OS[oi][:, 0:W], ps[ob][0:64, 0:W], r=[("ps", ob)], w=["OS%d" % oi])
                        r0 = (h % 2) * 64
                        B.dma(mix[r0:r0 + 64, h // 2, t0:t0 + W], OS[oi][:, 0:W], r=["OS%d" % oi], w=[("mix", ti)])

                load_head(0)
                for step in range(njobs + 2):
                    if step < njobs:
                        stage1(step)
                    if 0 <= step - 1 < njobs:
                        stage2(step - 1)
                    if 0 <= step - 2 < njobs:
                        stage3(step - 2)
                B.barrier()

        def phase_B2():
            with ExitStack() as ph:
                MM = sb(ph, "m_MM", [128, 5, 512], F32)
                VAL = sb(ph, "m_VAL", [128, 1], F32)
                B.dma(MM[:], c_mmla, w=["MM"])
                B.dma(VAL[:], c_valid, w=["VALm"])
                QT = [sb(ph, "m_QT%d" % i, [96, T], BF16) for i in range(2)]
                KT = [sb(ph, "m_KT%d" % i, [96, T], BF16) for i in range(2)]
                VV = [sb(ph, "m_VV%d" % i, [128, NBLK, 64], BF16) for i in range(2)]
                NR = 3
                PP = [sb(ph, "m_P%d" % i, [128, 512], BF16) for i in range(NR)]
                RC = [sb(ph, "m_RC%d" % i, [64, 512], F32) for i in range(2)]
                OS = [sb(ph, "m_OS%d" % i, [64, 512], BF16) for i in range(2)]

                def load_head(h):
                    p = h % 2
                    B.dma(QT[p][:], mq[:, h, :], r=[("mq", i) for i in range(9)], w=["mQT%d" % p])
                    B.dma(KT[p][:], mk[:, h, :], r=[("mk", i) for i in range(9)], w=["mKT%d" % p])
                    B.dma(VV[p][:], mv[:, :, h * 64:(h + 1) * 64], r=[("mv", i) for i in range(9)], w=["mVV%d" % p])

                jobs = []
                for h in range(8):
                    for ti, (t0, W) in enumerate(TILES):
                        nkb = (t0 + W) // 128
                        for kb in reversed(range(nkb)):
                            jobs.append((h, ti, kb, kb == nkb - 1, kb == 0))
                njobs = len(jobs)

                def stage1(ji):
                    h, ti, kb, first, last = jobs[ji]
                    t0, W = TILES[ti]
                    p = h % 2
                    i = ji % NR
                    sbk_ = ji % 3
                    B.mm(ps[sbk_][:, 0:W], KT[p][:, kb * 128:(kb + 1) * 128], QT[p][:, t0:t0 + W], True, True,
                         r=["mKT%d" % p, "mQT%d" % p], w=[("ps", sbk_)])
                    B.op("act", lambda e: e.activation(out=PP[i][:, 0:W], in_=ps[sbk_][:, 0:W], func=AF.Exp,
                                                       scale=MLA_SCALE), r=[("ps", sbk_)], w=["P%d" % i])
                    c = kb - t0 // 128
                    if ti == 0:
                        B.op("pool", lambda e: e.tensor_tensor(out=PP[i][:, 0:W], in0=PP[i][:, 0:W], in1=MM[:, 4, 0:W],
                                                               op=ALU.mult), r=["P%d" % i, "MM"], w=["P%d" % i])
                    elif c >= 0:
                        B.op("pool", lambda e: e.tensor_tensor(out=PP[i][:, 0:W], in0=PP[i][:, 0:W], in1=MM[:, c, 0:W],
                                                               op=ALU.mult), r=["P%d" % i, "MM"], w=["P%d" % i])
                    elif kb == 0:
                        B.op("pool", lambda e: e.tensor_scalar(out=PP[i][:, 0:W], in0=PP[i][:, 0:W], scalar1=VAL[:, 0:1],
                                                               scalar2=None, op0=ALU.mult),
                             r=["P%d" % i, "VALm"], w=["P%d" % i])

                def stage2(ji):
                    h, ti, kb, first, last = jobs[ji]
                    t0, W = TILES[ti]
                    p = h % 2
                    i = ji % NR
                    par = (h * 9 + ti) % 2
                    ob = 4 + par
                    db = 6 + par
                    if ti == 0 and first and h + 1 < 8:
                        load_head(h + 1)
                    B.mm(ps[ob][0:64, 0:W], VV[p][:, kb, :], PP[i][:, 0:W], first, last,
                         r=["mVV%d" % p, "P%d" % i], w=[("ps", ob)])
                    B.mm(ps[db][0:64, 0:W], onesB[:], PP[i][:, 0:W], first, last,
                         r=["onesB", "P%d" % i], w=[("ps", db)])
                    if last:
                        B.op("dve", lambda e: e.reciprocal(out=RC[par][:, 0:W], in_=ps[db][0:64, 0:W]),
                             r=[("ps", db)], w=["RC%d" % par])
                        B.op("dve", lambda e: e.tensor_tensor(out=OS[par][:, 0:W], in0=ps[ob][0:64, 0:W],
                                                              in1=RC[par][:, 0:W], op=ALU.mult),
                             r=[("ps", ob), "RC%d" % par], w=["mOS%d" % par])
                        r0 = (h % 2) * 64
                        B.dma(mix[r0:r0 + 64, 4 + h // 2, t0:t0 + W], OS[par][:, 0:W], r=["mOS%d" % par],
                              w=[("mix", ti)])

                load_head(0)
                for step in range(njobs + 1):
                    if step < njobs:
                        stage1(step)
                    if 0 <= step - 1 < njobs:
                        stage2(step - 1)
                B.barrier()

        def phase_B3():
            with ExitStack() as ph:
                RQ = sb(ph, "r_RQ", [64, 4, T], BF16)
                RQD = sb(ph, "r_RQD", [64, 4, T], BF16)
                RK = sb(ph, "r_RK", [64, 4, T], BF16)
                RV = sb(ph, "r_RV", [128, NBLK, 512], BF16)
                RKD = sb(ph, "r_RKD", [128, NBLK, 256], BF16)
                DT = sb(ph, "r_DT", [128, 512], F32)
                allt = list(range(9))
                for h in range(4):
                    r0 = (h % 2) * 64
                    B.dma(RQ[:, h, :], rq[r0:r0 + 64, h // 2, :], r=[("rq", i) for i in allt], w=["RQ"])
                    B.dma(RQD[:, h, :], rqd[r0:r0 + 64, h // 2, :], r=[("rqd", i) for i in allt], w=["RQD"])
                    B.dma(RK[:, h, :], rk[r0:r0 + 64, h // 2, :], r=[("rk", i) for i in allt], w=["RK"])
                B.dma(RV[:], rv, r=[("rv", i) for i in allt], w=["RV"])
                B.dma(RKD[:], rkd, r=[("rkd", i) for i in allt], w=["RKD"])
                B.dma(DT[:], c_dt, w=["DT"])
                ST = sb(ph, "r_ST", [64, 4, 128], F32)
                STB = sb(ph, "r_STB", [64, 4, 128], BF16)
                B.op("pool", lambda e: e.memset(ST[:], 0.0), w=["ST"])
                B.op("pool", lambda e: e.memset(STB[:], 0.0), w=["STB"])
                IND = [sb(ph, "r_IND%d" % i, [128, 512], BF16) for i in range(2)]
                YS = [sb(ph, "r_YS%d" % i, [128, 512], F32) for i in range(2)]
                SQ = [sb(ph, "r_SQ%d" % i, [128, 512], F32) for i in range(2)]
                R = [sb(ph, "r_R%d" % i, [128, 512], F32) for i in range(2)]
                GG = [sb(ph, "r_GG%d" % i, [128, 4, 128], F32) for i in range(2)]
                OC = [sb(ph, "r_OC%d" % i, [128, 4, 128], BF16) for i in range(2)]
                cdec = [float(np.float64(g) ** 128.0) for g in RET_GAMMA]
                for n in range(NBLK):
                    p = n % 2
                    tsl = slice(n * 128, (n + 1) * 128)
                    ti = 0 if n == 0 else 1 + (n - 1) // 4
                    B.dma(GG[p][:], rg[:, :, tsl], r=[("rg", ti)], w=["GG%d" % p])
                    bi = bank()
                    for h in range(4):
                        B.mm(ps[bi][:, h * 128:(h + 1) * 128], RK[:, h, tsl], RQ[:, h, tsl],
                             True, True, r=["RK", "RQ"], w=[("ps", bi)])
                    B.op("dve", lambda e: e.tensor_tensor(out=IND[p][:], in0=ps[bi][:], in1=DT[:], op=ALU.mult),
                         r=[("ps", bi), "DT"], w=["IND%d" % p])
                    by = bank()
                    for h in range(4):
                        B.mm(ps[by][:, h * 128:(h + 1) * 128], RV[:, n, h * 128:(h + 1) * 128],
                             IND[p][:, h * 128:(h + 1) * 128], True, False, r=["RV", "IND%d" % p], w=[("ps", by)])
                        B.mm(ps[by][:, h * 128:(h + 1) * 128], STB[:, h, :], RQD[:, h, tsl],
                             False, True, r=["STB", "RQD"], w=[("ps", by)])
                    if n + 1 < NBLK:
                        bs = bank()
                        for h in range(4):
                            B.mm(ps[bs][0:64, h * 128:(h + 1) * 128], RKD[:, n, h * 64:(h + 1) * 64],
                                 RV[:, n, h * 128:(h + 1) * 128], True, True, r=["RKD", "RV"], w=[("ps", bs)])
                        for h in range(4):
                            B.op("dve", lambda e: e.scalar_tensor_tensor(
                                out=ST[:, h, :], in0=ST[:, h, :], scalar=cdec[h],
                                in1=ps[bs][0:64, h * 128:(h + 1) * 128], op0=ALU.mult, op1=ALU.add),
                                r=["ST", ("ps", bs)], w=["ST"])
                        B.op("pool", lambda e: e.tensor_copy(out=STB[:], in_=ST[:]), r=["ST"], w=["STB"])
                    B.op("act", lambda e: e.copy(out=YS[p][:], in_=ps[by][:]), r=[("ps", by)], w=["YS%d" % p])
                    bm = bank()
                    B.mm(ps[bm][:], onesF[:], YS[p][:], True, True, r=["onesF", "YS%d" % p], w=[("ps", bm)])
                    B.op("dve", lambda e: e.scalar_tensor_tensor(out=YS[p][:], in0=ps[bm][:], scalar=-1.0 / 128,
                                                                 in1=YS[p][:], op0=ALU.mult, op1=ALU.add),
                         r=[("ps", bm), "YS%d" % p], w=["YS%d" % p])
                    B.op("act", lambda e: e.activation(out=SQ[p][:], in_=YS[p][:], func=AF.Square),
                         r=["YS%d" % p], w=["rSQ%d" % p])
                    bv = bank()
                    B.mm(ps[bv][:], onesF[:], SQ[p][:], True, True, r=["onesF", "rSQ%d" % p], w=[("ps", bv)])
                    B.op("act", lambda e: e.activation(out=R[p][:], in_=ps[bv][:], func=AF.Ln, scale=1.0 / 128,
                                                       bias=LN_EPS), r=[("ps", bv)], w=["rR%d" % p])
                    B.op("act", lambda e: e.activation(out=R[p][:], in_=R[p][:], func=AF.Exp, scale=-0.5),
                         r=["rR%d" % p], w=["rR%d" % p])
                    B.op("dve", lambda e: e.tensor_tensor(out=YS[p][:], in0=YS[p][:], in1=R[p][:], op=ALU.mult),
                         r=["YS%d" % p, "rR%d" % p], w=["YS%d" % p])
                    B.op("pool", lambda e: e.tensor_tensor(out=OC[p][:], in0=YS[p][:].rearrange("p (h n) -> p h n", h=4),
                                                           in1=GG[p][:], op=ALU.mult),
                         r=["YS%d" % p, "GG%d" % p], w=["OC%d" % p])
                    B.dma(mix[:, 8:12, tsl], OC[p][:], r=["OC%d" % p], w=[("mix", ti)])
                B.barrier()

        def phase_C(l):
            with ExitStack() as ph:
                WO = sb(ph, "c_WO", [128, 12, D], BF16)
                B.dma(WO[:], wo[l].rearrange("(c p) n -> p c n", p=128), w=["WO"], q="pool")
                MX = [sb(ph, "c_MX%d" % i, [128, 12, 512], BF16) for i in range(2)]
                X = [sb(ph, "c_X%d" % i, [128, 8, 512], F32) for i in range(2)]
                SQ = sb(ph, "c_SQ", [128, 8, 512], F32)
                R = sb(ph, "c_R", [128, 512], F32)
                XB = sb(ph, "c_XB", [128, 8, 512], BF16)

                def load_tile(ti):
                    t0, W = TILES[ti]
                    p = ti % 2
                    B.dma(MX[p][:, :, 0:W], mix[:, :, t0:t0 + W], r=[("mix", ti)], w=["MX%d" % p])
                    B.dma(X[p][:, :, 0:W], hres[:, :, t0:t0 + W], r=[("hres", t0)], w=["cX%d" % p])

                load_tile(0)
                for ti, (t0, W) in enumerate(TILES):
                    p = ti % 2
                    if ti + 1 < len(TILES):
                        load_tile(ti + 1)
                    for dc in range(8):
                        b = bank()
                        for m in range(12):
                            B.mm(ps[b][:, 0:W], WO[:, m, dc * 128:(dc + 1) * 128], MX[p][:, m, 0:W], m == 0, m == 11,
                                 r=["WO", "MX%d" % p], w=[("ps", b)])
                        B.op("dve", lambda e, dc=dc, b=b: e.scalar_tensor_tensor(
                            out=X[p][:, dc, 0:W], in0=X[p][:, dc, 0:W], scalar=DN_ALPHA, in1=ps[b][:, 0:W],
                            op0=ALU.mult, op1=ALU.add), r=["cX%d" % p, ("ps", b)], w=["cX%d" % p])
                    ln_f(None, X[p], "cX%d" % p, SQ[:], ["SQ"], R, XB, W, 1 + 2 * l, t0, False)
                B.barrier()

        def phase_D(l, final):
            with ExitStack() as ph:
                W1 = sb(ph, "d_W1", [128, 8, 4096], BF16)
                W2 = sb(ph, "d_W2", [128, 32, D], BF16)
                for c in range(8):
                    B.dma(W1[:, c, :], w1[l, c * 128:(c + 1) * 128, :], w=["W1"], q="pool")
                for c in range(4):
                    B.dma(W2[:, c * 8:(c + 1) * 8, :], w2[l, c * 1024:(c + 1) * 1024, :].rearrange("(c p) n -> p c n", p=128),
                          w=["W2"], q="pool")
                HB = [sb(ph, "d_HB%d" % i, [128, 8, 512], BF16) for i in range(2)]
                X = sb(ph, "d_X", [128, 8, 512], F32)
                R = sb(ph, "d_R", [128, 512], F32)
                XB = sb(ph, "d_XB", [128, 8, 512], BF16)
                UT = sb(ph, "d_UT", [128, 16, 512], BF16)
                SQ = UT[:].rearrange("p a n -> p (a n)").bitcast(F32).rearrange("p (c n) -> p c n", c=8)
                kut = [("UT", fc) for fc in range(16)]
                RL = [sb(ph, "d_RL%d" % i, [128, 512], F32) for i in range(3)]
                phd = {}
                if final:
                    phd["OT"] = [sb(ph, "d_OT%d" % i, [128, D], F32) for i in range(2)]

                def load_tile(ti):
                    t0, W = TILES[ti]
                    B.dma(HB[ti % 2][:, :, 0:W], hT[:, :, t0:t0 + W], r=[("hT", t0)], w=["HB%d" % (ti % 2)])

                load_tile(0)
                for ti, (t0, W) in enumerate(TILES):
                    p = ti % 2
                    if ti + 1 < len(TILES):
                        load_tile(ti + 1)
                    B.dma(X[:, :, 0:W], hres[:, :, t0:t0 + W], r=[("hres", t0)], w=["dX"])
                    for half in range(2):
                        for f in range(16):
                            fc = half * 16 + f
                            b = bank()
                            for c in range(8):
                                B.mm(ps[b][:, 0:W], W1[:, c, fc * 128:(fc + 1) * 128], HB[p][:, c, 0:W], c == 0, c == 7,
                                     r=["W1", "HB%d" % p], w=[("ps", b)])
                            i = fc % 3
                            B.op("act", lambda e: e.activation(out=RL[i][:, 0:W], in_=ps[b][:, 0:W], func=AF.Relu),
                                 r=[("ps", b)], w=["RL%d" % i])
                            eng = "pool" if fc % 2 == 0 else "dve"
                            B.op(eng, lambda e: e.tensor_tensor(out=UT[:, f, 0:W], in0=RL[i][:, 0:W],
                                                                in1=RL[i][:, 0:W], op=ALU.mult),
                                 r=["RL%d" % i], w=[("UT", f)])
                        for dc in range(8):
                            b = bank()
                            for f in range(16):
                                fc = half * 16 + f
                                B.mm(ps[b][:, 0:W], W2[:, fc, dc * 128:(dc + 1) * 128], UT[:, f, 0:W], f == 0, f == 15,
                                     r=["W2", ("UT", f)], w=[("ps", b)])
                            B.op("dve", lambda e: e.scalar_tensor_tensor(
                                out=X[:, dc, 0:W], in0=X[:, dc, 0:W], scalar=(DN_ALPHA if half == 0 else 1.0),
                                in1=ps[b][:, 0:W], op0=ALU.mult, op1=ALU.add), r=["dX", ("ps", b)], w=["dX"])
                    ln_f(phd, X, "dX", SQ, kut, R, XB, W, 2 + 2 * l, t0, final)
                B.barrier()

        import os as _os
        PH = _os.environ.get("KPH", "E,A,B1,B2,B3,C,D").split(",")
        if "E" in PH:
            phase_embed()
        for l in range(NL):
            if "A" in PH:
                phase_A(l)
            if "B1" in PH:
                phase_B1()
            if "B2" in PH:
                phase_B2()
            if "B3" in PH:
                phase_B3()
            if "C" in PH:
                phase_C(l)
            if "D" in PH:
                phase_D(l, l == NL - 1)
        B.finish()
        print("instr counts", B.cnt, "sems", len(B.sems), flush=True)
    return nc


def _consts():
    c = {}
    c["c_ident"] = np.eye(128, dtype=np.float32)
    j = np.arange(128)
    c["c_uincl"] = (j[:, None] >= j[None, :]).astype(np.float32)
    pos = (np.arange(T) - 112).astype(np.float32)

    def tables(half):
        inv = (np.float32(10000.0) ** (-np.arange(half, dtype=np.float32) / np.float32(half))).astype(np.float32)
        ang = (pos[None, :] * inv[:, None]).astype(np.float32)
        return np.cos(ang).astype(np.float32), np.sin(ang).astype(np.float32)

    c32, s32 = tables(32)
    C = np.concatenate([c32, c32, c32, c32], 0)
    S = np.concatenate([-s32, s32, -s32, s32], 0)
    c["c_ropeR"] = np.ascontiguousarray(np.stack([C, S, 0.125 * C, 0.125 * S], 1).astype(np.float32))
    c16, s16 = tables(16)
    CM = np.concatenate([np.ones((64, T), np.float32), c16, c16], 0)
    SM = np.concatenate([np.zeros((64, T), np.float32), -s16, s16], 0)
    c["c_ropeM"] = np.ascontiguousarray(np.stack([CM, SM], 1).astype(np.float32))
    g = np.array(RET_GAMMA, np.float64)
    n = np.arange(128, dtype=np.float64)
    qd = np.zeros((128, 2, 128), np.float64)
    for h in range(4):
        qd[(h % 2) * 64:(h % 2) * 64 + 64, h // 2, :] = (g[h] ** (n + 1.0))[None, :]
    c["c_qdec"] = qd.astype(np.float32)
    kd = np.zeros((128, 256), np.float64)
    for h in range(4):
        kd[:, h * 64:(h + 1) * 64] = (g[h] ** (127.0 - n))[:, None]
    c["c_kdecT"] = kd.astype(np.float32)
    dt = np.zeros((128, 4, 128), np.float64)
    diff = n[None, :] - n[:, None]
    for h in range(4):
        dt[:, h, :] = np.where(diff >= 0, g[h] ** np.maximum(diff, 0.0), 0.0)
    c["c_dt"] = dt.reshape(128, 512).astype(np.float32)
    cv = np.zeros((128, 2), np.float64)
    for h in range(4):
        cv[(h % 2) * 64:(h % 2) * 64 + 64, h // 2] = g[h] ** 128.0
    c["c_cvec"] = cv.astype(np.float32)
    s_ = np.arange(128)[:, None]
    t_ = np.arange(512)[None, :]
    msb = np.zeros((128, 5, 512), np.float32)
    mml = np.zeros((128, 5, 512), np.float32)
    for cc in range(4):
        msb[:, cc, :] = ((cc * 128 + s_) < t_)
        mml[:, cc, :] = ((cc * 128 + s_) <= t_)
    valid = (s_ >= 112)
    msb[:, 4, :] = ((s_ < t_) & valid)
    mml[:, 4, :] = ((s_ <= t_) & (valid | (s_ == t_)))
    c["c_msb"] = msb
    c["c_mmla"] = mml
    c["c_valid"] = valid.astype(np.float32).reshape(128, 1)
    return c


def _rot_idx(nheads, d):
    half = d // 2
    idx = []
    for h in range(nheads):
        idx += [h * d + ((i + half) % d) for i in range(d)]
    return np.array(idx)


def _prep_weights(inp, NL):
    w_in = np.asarray(inp["w_in"])[:NL]
    b_sbq, b_sbk, b_sbv, b_cq, b_ckv, b_kr, b_rq, b_rk, b_rv, b_rg = 0, 512, 1024, 1536, 1920, 2176, 2208, 2464, 2720, 3232
    cols = np.concatenate([
        np.arange(b_sbq, b_sbq + 512), np.arange(b_sbk, b_sbk + 512), np.arange(b_sbv, b_sbv + 512),
        np.arange(b_cq, b_cq + 384), np.arange(b_ckv, b_ckv + 256),
        np.arange(b_kr, b_kr + 32), b_kr + _rot_idx(1, 32),
        np.arange(b_rq, b_rq + 256), b_rq + _rot_idx(4, 64),
        np.arange(b_rk, b_rk + 256), b_rk + _rot_idx(4, 64),
        np.arange(b_rv, b_rv + 512), np.arange(b_rg, b_rg + 512)])
    assert cols.size == NA
    wa = np.ascontiguousarray(np.take(w_in, cols, axis=2))
    w_uq = np.asarray(inp["w_uq"])[:NL]
    ucols = []
    for h in range(8):
        base = h * 96
        xc = list(range(base, base + 96))
        rc = list(range(base, base + 64)) + [base + 64 + ((i + 16) % 32) for i in range(32)]
        ucols += xc + rc
    wuq = np.ascontiguousarray(np.take(w_uq, np.array(ucols), axis=2))
    w_ukv = np.asarray(inp["w_ukv"])[:NL]
    kc = np.concatenate([np.arange(h * 128, h * 128 + 64) for h in range(8)] +
                        [np.arange(h * 128 + 64, h * 128 + 128) for h in range(8)])
    wukv = np.ascontiguousarray(np.take(w_ukv, kc, axis=2))
    lnp = np.zeros((9, 2, 1024), np.float32)
    lnp[0, 0] = np.asarray(inp["ln_emb_g"])
    lnp[0, 1] = np.asarray(inp["ln_emb_b"])
    for l in range(NLAYERS):
        lnp[1 + 2 * l, 0] = np.asarray(inp["ln1_g"])[l]
        lnp[1 + 2 * l, 1] = np.asarray(inp["ln1_b"])[l]
        lnp[2 + 2 * l, 0] = np.asarray(inp["ln2_g"])[l]
        lnp[2 + 2 * l, 1] = np.asarray(inp["ln2_b"])[l]
    lnp = np.ascontiguousarray(lnp.reshape(9, 2, 8, 128).transpose(3, 0, 1, 2))
    qn = np.ascontiguousarray(np.asarray(inp["mla_q_norm"]).reshape(NLAYERS, 3, 128).transpose(2, 0, 1))
    kvn = np.ascontiguousarray(np.asarray(inp["mla_kv_norm"]).reshape(NLAYERS, 2, 128).transpose(2, 0, 1))
    shared = {
        "meta": np.ascontiguousarray(np.asarray(inp["meta_tokens"], dtype=np.float32)),
        "lnp": lnp.astype(np.float32), "qn": qn.astype(np.float32), "kvn": kvn.astype(np.float32),
        "wa": wa.astype(np.float32), "wuq": wuq.astype(np.float32), "wukv": wukv.astype(np.float32),
        "wo": np.ascontiguousarray(np.asarray(inp["w_out"], dtype=np.float32)[:NL]),
        "w1": np.ascontiguousarray(np.asarray(inp["w_ff1"], dtype=np.float32)[:NL]),
        "w2": np.ascontiguousarray(np.asarray(inp["w_ff2"], dtype=np.float32)[:NL]),
    }
    shared.update(_consts())
    return shared


def run(inp, NL=NLAYERS, cores=8):
    shared = _prep_weights(inp, NL)
    x = np.asarray(inp["x"], dtype=np.float32)
    nc = build_nc(NL)
    in_maps = []
    for b in range(cores):
        m = dict(shared)
        m["x"] = np.ascontiguousarray(x[b])
        in_maps.append(m)
    res = run_bass_kernel_spmd(nc, in_maps, core_ids=list(range(cores)))
    return np.stack([np.asarray(r["out"]) for r in res.results], axis=0).astype(np.float32)


def kernel(**inputs):
    return run(inputs, NLAYERS, 8)
```

```python
import numpy as np
import ml_dtypes
from contextlib import ExitStack
import concourse.bass as bass
import concourse.mybir as mybir
from concourse.bass_utils import run_bass_kernel_spmd

F32 = mybir.dt.float32
BF16 = mybir.dt.bfloat16
ALU = mybir.AluOpType
AF = mybir.ActivationFunctionType

D = 1024
SEQ = 4096
NLAYERS = 4
T = 4224
NBLK = 33
TILES = [(0, 128)] + [(128 + 512 * i, 512) for i in range(8)]
LN_EPS = 1e-5
DN_ALPHA = (2 * NLAYERS) ** 0.25
RET_GAMMA = [1.0 - 2.0 ** (-5 - h) for h in range(4)]
MLA_SCALE = 96 ** -0.5

O_SBQ, O_SBK, O_SBV, O_CQ, O_CKV, O_KR, O_KRR = 0, 512, 1024, 1536, 1920, 2176, 2208
O_RQ, O_RQR, O_RK, O_RKR, O_RV, O_RG = 2240, 2496, 2752, 3008, 3264, 3776
NA = 4288
SEM_LIM = 12000


class Builder:
    def __init__(self, nc, es):
        self.nc = nc
        self.es = es
        self.E = {"pe": nc.tensor, "act": nc.scalar, "dve": nc.vector, "pool": nc.gpsimd, "sp": nc.sync}
        self.cnt = {e: 0 for e in self.E}
        self.waited = {e: {} for e in self.E}
        self.sems = {}
        self.st = {}
        self.dq = {}
        self.flip = 0

    def sem(self, key):
        if key not in self.sems:
            self.sems[key] = self.es.enter_context(self.nc.semaphore("s%d" % len(self.sems)))
        return self.sems[key]

    def _wait(self, eng, tok):
        sk, val = tok
        w = self.waited[eng]
        if w.get(sk, 0) >= val:
            return
        w[sk] = val
        self.E[eng].wait_ge(self.sem(sk), val)

    def _deps(self, eng, r, w):
        toks = set()
        for k in r:
            s = self.st.get(k)
            if s and s[0]:
                toks.add(s[0])
        for k in w:
            s = self.st.get(k)
            if s:
                if s[0] and s[0][0][0] != eng:
                    toks.add(s[0])
                for sk, v in s[1].items():
                    if sk[0] != eng:
                        toks.add((sk, v))
        for t in toks:
            self._wait(eng, t)

    def _update(self, tok, r, w):
        sk, val = tok
        for k in r:
            s = self.st.setdefault(k, [None, {}])
            if s[1].get(sk, 0) < val:
                s[1][sk] = val
        for k in w:
            self.st[k] = [tok, {}]

    def op(self, eng, fn, r=(), w=()):
        self._deps(eng, r, w)
        ins = fn(self.E[eng])
        self.cnt[eng] += 1
        i = self.cnt[eng]
        sk = (eng, (i - 1) // SEM_LIM)
        val = (i - 1) % SEM_LIM + 1
        ins.then_inc(self.sem(sk), 1)
        self._update((sk, val), r, w)

    def dma(self, out, in_, r=(), w=(), q="sp"):
        d = self.dq.setdefault(q, {"uses": [0] * 8, "next": 0})
        slot = d["next"]
        d["next"] = (slot + 1) % 8
        sk = ("dma", q, slot)
        if d["uses"][slot] > 0:
            self._wait(q, (sk, 16 * d["uses"][slot]))
        self._deps(q, r, w)
        self.E[q].dma_start(out=out, in_=in_).then_inc(self.sem(sk), 16)
        d["uses"][slot] += 1
        self._update((sk, 16 * d["uses"][slot]), r, w)

    def all_tokens(self):
        toks = []
        for e in ("pe", "act", "dve", "pool"):
            i = self.cnt[e]
            if i > 0:
                toks.append(((e, (i - 1) // SEM_LIM), (i - 1) % SEM_LIM + 1))
        for q, d in self.dq.items():
            for slot, u in enumerate(d["uses"]):
                if u > 0:
                    toks.append((("dma", q, slot), 16 * u))
        return toks

    def barrier(self):
        toks = self.all_tokens()
        for e in self.E:
            for t in toks:
                if t[0][0] == e:
                    continue
                self._wait(e, t)

    def finish(self):
        for t in self.all_tokens():
            if t[0][0] == "dma":
                self._wait("sp", t)

    def mm(self, out, lhsT, rhs, start, stop, r, w, **kw):
        self.op("pe", lambda e: e.matmul(out, lhsT, rhs, start=start, stop=stop, **kw), r=r, w=w)

    def tr(self, out, in_, ident, r, w):
        self.op("pe", lambda e: e.transpose(out, in_, ident), r=r, w=w)

    def evac(self, out, in_, r, w, scale=None, eng=None):
        if eng is None:
            self.flip ^= 1
            eng = "act" if self.flip else "dve"
        if eng == "act":
            if scale is None:
                self.op("act", lambda e: e.copy(out=out, in_=in_), r=r, w=w)
            else:
                self.op("act", lambda e: e.mul(out=out, in_=in_, mul=scale), r=r, w=w)
        else:
            if scale is None:
                self.op("dve", lambda e: e.tensor_copy(out=out, in_=in_), r=r, w=w)
            else:
                self.op("dve", lambda e: e.tensor_scalar(out=out, in0=in_, scalar1=scale, scalar2=None,
                                                         op0=ALU.mult), r=r, w=w)


def build_nc(NL, debug=False):
    nc = bass.Bass("TRN2", target_bir_lowering=False)

    def din(name, shape, dt=F32):
        return nc.dram_tensor(name, shape, dt, kind="ExternalInput").ap()

    def dscr(name, shape, dt):
        return nc.dram_tensor(name, shape, dt, kind="Internal").ap()

    x = din("x", [SEQ, D])
    meta = din("meta", [16, D])
    lnp = din("lnp", [128, 9, 2, 8])
    qn = din("qn", [128, NLAYERS, 3])
    kvn = din("kvn", [128, NLAYERS, 2])
    wa = din("wa", [NL, D, NA])
    wuq = din("wuq", [NL, 384, 1536])
    wukv = din("wukv", [NL, 256, 1024])
    wo = din("wo", [NL, 1536, D])
    w1 = din("w1", [NL, D, 4096])
    w2 = din("w2", [NL, 4096, D])
    c_ident = din("c_ident", [128, 128])
    c_nuincl = din("c_nuincl", [128, 128])
    c_ropeR = din("c_ropeR", [128, 4, T])
    c_ropeM = din("c_ropeM", [96, 2, T])
    c_qdec = din("c_qdec", [128, 2, 128])
    c_kdecT = din("c_kdecT", [128, 256])
    c_dt = din("c_dt", [128, 512])
    c_cvec = din("c_cvec", [128, 2])
    c_msb = din("c_msb", [128, 6, 512])
    c_mmla = din("c_mmla", [128, 5, 512])
    c_valid = din("c_valid", [128, 1])
    out = nc.dram_tensor("out", [SEQ, D], F32, kind="ExternalOutput").ap()

    hres = dscr("hres", [128, 8, T], F32)
    hT = dscr("hT", [128, 8, T], BF16)
    sbq = dscr("sbq", [128, 4, T], BF16)
    sbk = dscr("sbk", [128, 4, T], BF16)
    sbv = dscr("sbv", [128, NBLK, 512], BF16)
    mq = dscr("mq", [96, 8, T], BF16)
    mk = dscr("mk", [96, 8, T], BF16)
    mv = dscr("mv", [128, NBLK, 512], BF16)
    rq = dscr("rq", [128, 2, T], BF16)
    rqd = dscr("rqd", [128, 2, T], BF16)
    rk = dscr("rk", [128, 2, T], BF16)
    rkd = dscr("rkd", [128, NBLK, 256], BF16)
    rv = dscr("rv", [128, NBLK, 512], BF16)
    rg = dscr("rg", [128, 4, T], F32)
    mix = dscr("mix", [128, 12, T], BF16)

    with ExitStack() as es:
        B = Builder(nc, es)

        uniq = {"n": 0}

        def sb(stack, name, shape, dt):
            uniq["n"] += 1
            return stack.enter_context(nc.sbuf_tensor("%s_%d" % (name, uniq["n"]), shape, dt))

        ps = [es.enter_context(nc.psum_tensor("ps%d" % i, [128, 512], F32)) for i in range(8)]
        rot = {"i": 0}

        def bank(lo=0, hi=8):
            n = hi - lo
            b = lo + rot["i"] % n
            rot["i"] += 1
            return b

        identF = sb(es, "identF", [128, 128], F32)
        identB = sb(es, "identB", [128, 128], BF16)
        onesF = sb(es, "onesF", [128, 128], F32)
        onesB = sb(es, "onesB", [128, 64], BF16)
        lnps = sb(es, "lnps", [128, 9, 2, 8], F32)
        qns = sb(es, "qns", [128, NLAYERS, 3], F32)
        kvns = sb(es, "kvns", [128, NLAYERS, 2], F32)
        B.dma(identF[:], c_ident, w=["identF"])
        B.dma(identB[:], c_ident, w=["identB"], q="pool")
        B.dma(lnps[:], lnp, w=["lnps"])
        B.dma(qns[:], qn, w=["qns"])
        B.dma(kvns[:], kvn, w=["kvns"])
        B.op("pool", lambda e: e.memset(onesF[:], 1.0), w=["onesF"])
        B.op("pool", lambda e: e.memset(onesB[:], 1.0), w=["onesB"])

        def ln_f(ph, X, kx, SQ, ksq, R, XB, W, lnidx, t0, final):
            xs = X[:, :, 0:W]
            mp = bank()
            for c in range(8):
                B.mm(ps[mp][:, 0:W], onesF[:], X[:, c, 0:W], c == 0, c == 7, r=[kx, "onesF"], w=[("ps", mp)])
            B.op("dve", lambda e: e.scalar_tensor_tensor(
                out=xs, in0=ps[mp][:, 0:W].unsqueeze(1).broadcast_to([128, 8, W]), scalar=-1.0 / D, in1=xs,
                op0=ALU.mult, op1=ALU.add), r=[("ps", mp), kx], w=[kx])
            B.op("act", lambda e: e.activation(out=SQ[:, :, 0:W], in_=xs, func=AF.Square), r=[kx], w=ksq)
            vp = bank()
            for c in range(8):
                B.mm(ps[vp][:, 0:W], onesF[:], SQ[:, c, 0:W], c == 0, c == 7, r=ksq + ["onesF"], w=[("ps", vp)])
            B.op("act", lambda e: e.activation(out=R[:, 0:W], in_=ps[vp][:, 0:W], func=AF.Ln, scale=1.0 / D,
                                               bias=LN_EPS), r=[("ps", vp)], w=["R"])
            B.op("act", lambda e: e.activation(out=R[:, 0:W], in_=R[:, 0:W], func=AF.Exp, scale=-0.5),
                 r=["R"], w=["R"])
            B.op("dve", lambda e: e.tensor_tensor(out=xs, in0=xs, in1=R[:, 0:W].unsqueeze(1).broadcast_to([128, 8, W]),
                                                  op=ALU.mult), r=[kx, "R"], w=[kx])
            for c in range(8):
                B.op("act", lambda e, c=c: e.activation(out=X[:, c, 0:W], in_=X[:, c, 0:W], func=AF.Identity,
                                                        scale=lnps[:, lnidx, 0, c:c + 1],
                                                        bias=lnps[:, lnidx, 1, c:c + 1]),
                     r=[kx, "lnps"], w=[kx])
            B.op("pool", lambda e: e.tensor_copy(out=XB[:, :, 0:W], in_=xs), r=[kx], w=["XB"])
            hk = ("h", t0)
            B.dma(hres[:, :, t0:t0 + W], xs, r=[kx], w=[("hres", t0)])
            B.dma(hT[:, :, t0:t0 + W], XB[:, :, 0:W], r=["XB"], w=[("hT", t0)])
            if final and t0 >= 128:
                for s in range(W // 128):
                    OT = ph["OT"][s % 2]
                    ko = "OT%d" % (s % 2)
                    for half in range(2):
                        b = bank()
                        for cc in range(4):
                            c = half * 4 + cc
                            B.tr(ps[b][:, cc * 128:(cc + 1) * 128], X[:, c, s * 128:(s + 1) * 128], identF[:],
                                 r=[kx, "identF"], w=[("ps", b)])
                        B.evac(OT[:, half * 512:(half + 1) * 512], ps[b][:], r=[("ps", b)], w=[ko])
                    tok0 = t0 - 128 + s * 128
                    B.dma(out[tok0:tok0 + 128, :], OT[:], r=[ko], w=[("out", tok0)])

        def phase_embed():
            with ExitStack() as ph:
                xt = [sb(ph, "e_xt%d" % i, [128, 4, D], F32) for i in range(2)]
                X = [sb(ph, "e_X%d" % i, [128, 8, 512], F32) for i in range(2)]
                SQ = sb(ph, "e_SQ", [128, 8, 512], F32)
                R = sb(ph, "e_R", [128, 512], F32)
                XB = sb(ph, "e_XB", [128, 8, 512], BF16)
                for ti, (t0, W) in enumerate(TILES):
                    a = xt[ti % 2]
                    ka = "e_xt%d" % (ti % 2)
                    kx = "e_X%d" % (ti % 2)
                    nsub = W // 128
                    if ti == 0:
                        B.op("pool", lambda e: e.memset(a[:, 0, :], 0.0), w=[ka])
                        B.dma(a[112:128, 0, :], meta, w=[ka])
                    else:
                        B.dma(a[:], x[t0 - 128:t0 - 128 + 512, :].rearrange("(s p) d -> p s d", p=128), w=[ka])
                    for c in range(8):
                        b = bank()
                        for s in range(nsub):
                            B.tr(ps[b][:, s * 128:(s + 1) * 128], a[:, s, c * 128:(c + 1) * 128], identF[:],
                                 r=[ka, "identF"], w=[("ps", b)])
                        B.evac(X[ti % 2][:, c, 0:W], ps[b][:, 0:W], r=[("ps", b)], w=[kx])
                    ln_f(None, X[ti % 2], kx, SQ[:], ["SQ"], R, XB, W, 0, t0, False)
                B.barrier()

        def phase_A(l):
            with ExitStack() as ph:
                WA = sb(ph, "a_WA", [128, 8, NA], BF16)
                WUQ = sb(ph, "a_WUQ", [128, 3, 1536], BF16)
                WUKV = sb(ph, "a_WUKV", [128, 2, 1024], BF16)
                for c in range(8):
                    B.dma(WA[:, c, :], wa[l, c * 128:(c + 1) * 128, :], w=["WA"], q="pool")
                B.dma(WUQ[:], wuq[l].rearrange("(c p) n -> p c n", p=128), w=["WUQ"], q="pool")
                B.dma(WUKV[:], wukv[l].rearrange("(c p) n -> p c n", p=128), w=["WUKV"], q="pool")
                QDEC = sb(ph, "a_QDEC", [128, 2, 128], F32)
                KDEC = sb(ph, "a_KDEC", [128, 256], F32)
                B.dma(QDEC[:], c_qdec, w=["QDEC"])
                B.dma(KDEC[:], c_kdecT, w=["KDEC"])
                HT = [sb(ph, "a_HT%d" % i, [128, 8, 512], BF16) for i in range(2)]
                RT = [sb(ph, "a_RT%d" % i, [128, 4, 512], F32) for i in range(2)]
                MT = [sb(ph, "a_MT%d" % i, [96, 2, 512], F32) for i in range(2)]
                MK = [sb(ph, "a_MK%d" % i, [32, 2, 512], F32) for i in range(2)]
                CQ = sb(ph, "a_CQ", [128, 3, 512], F32)
                CKV = sb(ph, "a_CKV", [128, 2, 512], F32)
                SQ = sb(ph, "a_SQ", [128, 3, 512], F32)
                R = sb(ph, "a_R", [128, 512], F32)
                CQN = sb(ph, "a_CQN", [128, 3, 512], BF16)
                CKVN = sb(ph, "a_CKVN", [128, 2, 512], BF16)
                T1 = [sb(ph, "a_T1%d" % i, [128, 512], F32) for i in range(3)]
                T2 = [sb(ph, "a_T2%d" % i, [128, 512], F32) for i in range(3)]
                STG = [sb(ph, "a_STG%d" % i, [128, 512], BF16) for i in range(6)]
                RKB = [sb(ph, "a_RKB%d" % i, [128, 512], BF16) for i in range(2)]
                GS = [sb(ph, "a_GS0", [128, 4, 512], F32)] * 2
                KST = [sb(ph, "a_KST%d" % i, [128, 256], BF16) for i in range(2)]
                cn = {"stg": 0, "t": 0, "kst": 0}

                def stage():
                    i = cn["stg"] % 6
                    cn["stg"] += 1
                    return STG[i], "STG%d" % i

                def tt():
                    i = cn["t"] % 3
                    cn["t"] += 1
                    return T1[i], "T1%d" % i, T2[i], "T2%d" % i

                def load_tile(ti):
                    t0, W = TILES[ti]
                    p = ti % 2
                    B.dma(HT[p][:, :, 0:W], hT[:, :, t0:t0 + W], r=[("hT", t0)], w=["HT%d" % p])
                    B.dma(RT[p][:, :, 0:W], c_ropeR[:, :, t0:t0 + W], w=["RT%d" % p])
                    B.dma(MT[p][:, :, 0:W], c_ropeM[:, :, t0:t0 + W], w=["MT%d" % p])
                    B.dma(MK[p][:, :, 0:W], c_ropeM[64:96, :, t0:t0 + W], w=["MK%d" % p])

                load_tile(0)
                for ti, (t0, W) in enumerate(TILES):
                    p = ti % 2
                    if ti + 1 < len(TILES):
                        load_tile(ti + 1)
                    H = HT[p]
                    kh = "HT%d" % p
                    nsub = W // 128
                    blk0 = t0 // 128

                    def fgroup(col0, M, Wm=WA, kw="WA", rhs=None, krhs=None, nchunk=8):
                        b = bank()
                        for c in range(nchunk):
                            rr = H[:, c, 0:W] if rhs is None else rhs[:, c, 0:W]
                            B.mm(ps[b][0:M, 0:W], Wm[:, c, col0:col0 + M], rr, c == 0, c == nchunk - 1,
                                 r=[kw, kh if krhs is None else krhs], w=[("ps", b)])
                        return b

                    for m in range(4):
                        b = fgroup(O_SBQ + m * 128, 128)
                        S_, ks = stage()
                        B.evac(S_[:, 0:W], ps[b][:, 0:W], r=[("ps", b)], w=[ks], scale=0.125)
                        B.dma(sbq[:, m, t0:t0 + W], S_[:, 0:W], r=[ks], w=[("sbq", ti)])
                    for m in range(4):
                        b = fgroup(O_SBK + m * 128, 128)
                        S_, ks = stage()
                        B.evac(S_[:, 0:W], ps[b][:, 0:W], r=[("ps", b)], w=[ks])
                        B.dma(sbk[:, m, t0:t0 + W], S_[:, 0:W], r=[ks], w=[("sbk", ti)])
                    for (col0, dst, nm) in ((O_SBV, sbv, "sbv"), (O_RV, rv, "rv")):
                        for s in range(nsub):
                            b = bank()
                            for c in range(8):
                                B.mm(ps[b][:, :], H[:, c, s * 128:(s + 1) * 128], WA[:, c, col0:col0 + 512],
                                     c == 0, c == 7, r=["WA", kh], w=[("ps", b)])
                            S_, ks = stage()
                            B.evac(S_[:, :], ps[b][:, :], r=[("ps", b)], w=[ks])
                            B.dma(dst[:, blk0 + s, :], S_[:, :], r=[ks], w=[(nm, ti)])
                    for m in range(3):
                        b = fgroup(O_CQ + m * 128, 128)
                        B.evac(CQ[:, m, 0:W], ps[b][:, 0:W], r=[("ps", b)], w=["CQ"])
                    for m in range(2):
                        b = fgroup(O_CKV + m * 128, 128)
                        B.evac(CKV[:, m, 0:W], ps[b][:, 0:W], r=[("ps", b)], w=["CKV"])
                    ba = fgroup(O_KR, 32)
                    bb = fgroup(O_KRR, 32)
                    t1, k1, t2, k2 = tt()
                    B.op("dve", lambda e: e.tensor_tensor(out=t1[0:32, 0:W], in0=ps[ba][0:32, 0:W], in1=MK[p][:, 0, 0:W],
                                                          op=ALU.mult), r=[("ps", ba), "MK%d" % p], w=[k1])
                    B.op("dve", lambda e: e.tensor_tensor(out=t2[0:32, 0:W], in0=ps[bb][0:32, 0:W], in1=MK[p][:, 1, 0:W],
                                                          op=ALU.mult), r=[("ps", bb), "MK%d" % p], w=[k2])
                    S_, ks = stage()
                    B.op("pool", lambda e: e.tensor_tensor(out=S_[0:32, 0:W], in0=t1[0:32, 0:W], in1=t2[0:32, 0:W],
                                                           op=ALU.add), r=[k1, k2], w=[ks])
                    B.dma(mk[64:96, :, t0:t0 + W], S_[0:32, 0:W].unsqueeze(1).broadcast_to([32, 8, W]),
                          r=[ks], w=[("mk", ti)])
                    for pr in range(2):
                        ba = fgroup(O_RQ + pr * 128, 128)
                        bb = fgroup(O_RQR + pr * 128, 128)
                        t1, k1, t2, k2 = tt()
                        B.op("dve", lambda e: e.tensor_tensor(out=t1[:, 0:W], in0=ps[ba][:, 0:W], in1=RT[p][:, 0, 0:W],
                                                              op=ALU.mult), r=[("ps", ba), "RT%d" % p], w=[k1])
                        B.op("dve", lambda e: e.tensor_tensor(out=t2[:, 0:W], in0=ps[bb][:, 0:W], in1=RT[p][:, 1, 0:W],
                                                              op=ALU.mult), r=[("ps", bb), "RT%d" % p], w=[k2])
                        B.op("pool", lambda e: e.tensor_tensor(out=t1[:, 0:W], in0=t1[:, 0:W], in1=t2[:, 0:W],
                                                               op=ALU.add), r=[k1, k2], w=[k1])
                        S_, ks = stage()
                        B.op("act", lambda e: e.copy(out=S_[:, 0:W], in_=t1[:, 0:W]), r=[k1], w=[ks])
                        B.dma(rq[:, pr, t0:t0 + W], S_[:, 0:W], r=[ks], w=[("rq", ti)])
                        S2, ks2 = stage()
                        B.op("pool", lambda e: e.tensor_tensor(
                            out=S2[:, 0:W].rearrange("p (s n) -> p s n", n=128),
                            in0=t1[:, 0:W].rearrange("p (s n) -> p s n", n=128),
                            in1=QDEC[:, pr, :].unsqueeze(1).broadcast_to([128, nsub, 128]), op=ALU.mult),
                            r=[k1, "QDEC"], w=[ks2])
                        B.dma(rqd[:, pr, t0:t0 + W], S2[:, 0:W], r=[ks2], w=[("rqd", ti)])
                    for pr in range(2):
                        ba = fgroup(O_RK + pr * 128, 128)
                        bb = fgroup(O_RKR + pr * 128, 128)
                        t1, k1, t2, k2 = tt()
                        B.op("dve", lambda e: e.tensor_tensor(out=t1[:, 0:W], in0=ps[ba][:, 0:W], in1=RT[p][:, 2, 0:W],
                                                              op=ALU.mult), r=[("ps", ba), "RT%d" % p], w=[k1])
                        B.op("dve", lambda e: e.tensor_tensor(out=t2[:, 0:W], in0=ps[bb][:, 0:W], in1=RT[p][:, 3, 0:W],
                                                              op=ALU.mult), r=[("ps", bb), "RT%d" % p], w=[k2])
                        kb_ = "RKB%d" % pr
                        B.op("pool", lambda e: e.tensor_tensor(out=RKB[pr][:, 0:W], in0=t1[:, 0:W], in1=t2[:, 0:W],
                                                               op=ALU.add), r=[k1, k2], w=[kb_])
                        if ti == 0:
                            B.op("pool", lambda e: e.memset(RKB[pr][:, 0:112], 0.0), r=[kb_], w=[kb_])
                        B.dma(rk[:, pr, t0:t0 + W], RKB[pr][:, 0:W], r=[kb_], w=[("rk", ti)])
                    for s in range(nsub):
                        b = bank()
                        pb = ps[b][:].bitcast(BF16)
                        for pr in range(2):
                            B.tr(pb[:, pr * 128:(pr + 1) * 128], RKB[pr][:, s * 128:(s + 1) * 128], identB[:],
                                 r=["RKB%d" % pr, "identB"], w=[("ps", b)])
                        i = cn["kst"] % 2
                        cn["kst"] += 1
                        B.op("dve", lambda e: e.tensor_tensor(out=KST[i][:], in0=pb[:, 0:256], in1=KDEC[:], op=ALU.mult),
                             r=[("ps", b), "KDEC"], w=["KST%d" % i])
                        B.dma(rkd[:, blk0 + s, :], KST[i][:], r=["KST%d" % i], w=[("rkd", ti)])
                    G = GS[p]
                    for m in range(4):
                        b = fgroup(O_RG + m * 128, 128)
                        B.op("act", lambda e: e.activation(out=G[:, m, 0:W], in_=ps[b][:, 0:W], func=AF.Silu),
                             r=[("ps", b)], w=["GS"])
                    B.dma(rg[:, :, t0:t0 + W], G[:, :, 0:W], r=["GS"], w=[("rg", ti)])
                    for (src, ksrc, nch, dstn, kd, gam, inv) in ((CQ, "CQ", 3, CQN, "CQN", qns, 1.0 / 384),
                                                                   (CKV, "CKV", 2, CKVN, "CKVN", kvns, 1.0 / 256)):
                        B.op("act", lambda e: e.activation(out=SQ[:, 0:nch, 0:W], in_=src[:, :, 0:W], func=AF.Square),
                             r=[ksrc], w=["SQ"])
                        b = bank()
                        for m in range(nch):
                            B.mm(ps[b][:, 0:W], onesF[:], SQ[:, m, 0:W], m == 0, m == nch - 1, r=["SQ", "onesF"],
                                 w=[("ps", b)])
                        B.op("act", lambda e: e.activation(out=R[:, 0:W], in_=ps[b][:, 0:W], func=AF.Ln, scale=inv,
                                                           bias=LN_EPS), r=[("ps", b)], w=["R"])
                        B.op("act", lambda e: e.activation(out=R[:, 0:W], in_=R[:, 0:W], func=AF.Exp, scale=-0.5),
                             r=["R"], w=["R"])
                        for m in range(nch):
                            B.op("dve", lambda e, m=m: e.scalar_tensor_tensor(
                                out=dstn[:, m, 0:W], in0=src[:, m, 0:W], scalar=gam[:, l, m:m + 1], in1=R[:, 0:W],
                                op0=ALU.mult, op1=ALU.mult), r=[ksrc, "R", "qns", "kvns"], w=[kd])
                    for h in range(8):
                        ba = fgroup((2 * h) * 96, 96, Wm=WUQ, kw="WUQ", rhs=CQN, krhs="CQN", nchunk=3)
                        bb = fgroup((2 * h + 1) * 96, 96, Wm=WUQ, kw="WUQ", rhs=CQN, krhs="CQN", nchunk=3)
                        t1, k1, t2, k2 = tt()
                        B.op("dve", lambda e: e.tensor_tensor(out=t1[0:96, 0:W], in0=ps[ba][0:96, 0:W], in1=MT[p][:, 0, 0:W],
                                                              op=ALU.mult), r=[("ps", ba), "MT%d" % p], w=[k1])
                        B.op("dve", lambda e: e.tensor_tensor(out=t2[0:96, 0:W], in0=ps[bb][0:96, 0:W], in1=MT[p][:, 1, 0:W],
                                                              op=ALU.mult), r=[("ps", bb), "MT%d" % p], w=[k2])
                        S_, ks = stage()
                        B.op("pool", lambda e: e.tensor_tensor(out=S_[0:96, 0:W], in0=t1[0:96, 0:W], in1=t2[0:96, 0:W],
                                                               op=ALU.add), r=[k1, k2], w=[ks])
                        B.dma(mq[:, h, t0:t0 + W], S_[0:96, 0:W], r=[ks], w=[("mq", ti)])
                    for m in range(4):
                        b = fgroup(m * 128, 128, Wm=WUKV, kw="WUKV", rhs=CKVN, krhs="CKVN", nchunk=2)
                        S_, ks = stage()
                        B.evac(S_[:, 0:W], ps[b][:, 0:W], r=[("ps", b)], w=[ks])
                        B.dma(mk[0:64, 2 * m, t0:t0 + W], S_[0:64, 0:W], r=[ks], w=[("mk", ti)])
                        B.dma(mk[0:64, 2 * m + 1, t0:t0 + W], S_[64:128, 0:W], r=[ks], w=[("mk", ti)])
                    for s in range(nsub):
                        b = bank()
                        for c in range(2):
                            B.mm(ps[b][:, :], CKVN[:, c, s * 128:(s + 1) * 128], WUKV[:, c, 512:1024], c == 0, c == 1,
                                 r=["WUKV", "CKVN"], w=[("ps", b)])
                        S_, ks = stage()
                        B.evac(S_[:, :], ps[b][:, :], r=[("ps", b)], w=[ks])
                        B.dma(mv[:, blk0 + s, :], S_[:, :], r=[ks], w=[("mv", ti)])
                B.barrier()

        def phase_B1():
            with ExitStack() as ph:
                UIb = sb(ph, "b_UIb", [128, 128], BF16)
                ONb = sb(ph, "b_ONb", [128, 128], BF16)
                MS = sb(ph, "b_MS", [128, 6, 512], F32)
                B.dma(UIb[:], c_nuincl, w=["UIb"], q="pool")
                B.op("pool", lambda e: e.memset(ONb[:], -1.0), w=["ONb"])
                B.dma(MS[:], c_msb, w=["MS"])
                QP = [sb(ph, "b_QP%d" % i, [128, T], BF16) for i in range(2)]
                KA = [sb(ph, "b_KA%d" % i, [128, T], BF16) for i in range(2)]
                KB = [sb(ph, "b_KB%d" % i, [128, T], BF16) for i in range(2)]
                VP = [sb(ph, "b_VP%d" % i, [128, NBLK, 128], BF16) for i in range(2)]
                for i in range(2):
                    B.op("pool", lambda e: e.memset(KA[i][64:128, :], 0.0), w=["KA%d" % i])
                    B.op("pool", lambda e: e.memset(KB[i][0:64, :], 0.0), w=["KB%d" % i])
                NR = 3
                EB = [sb(ph, "b_E%d" % i, [128, 512], F32) for i in range(NR)]
                LK = [sb(ph, "b_LK%d" % i, [128, 512], F32) for i in range(NR)]
                HI = [sb(ph, "b_HI%d" % i, [128, 512], BF16) for i in range(NR)]
                LO = [sb(ph, "b_LO%d" % i, [128, 512], BF16) for i in range(NR)]
                SS = [sb(ph, "b_S%d" % i, [128, 512], F32) for i in range(NR)]
                WW = [sb(ph, "b_W%d" % i, [128, 512], BF16) for i in range(NR)]
                CC = [sb(ph, "b_C%d" % i, [128, 512], F32) for i in range(2)]
                OS = [sb(ph, "b_OS%d" % i, [128, 512], BF16) for i in range(2)]
                allt = list(range(9))

                def load_pair(m):
                    p = m % 2
                    B.dma(QP[p][:], sbq[:, m, :], r=[("sbq", i) for i in allt], w=["QP%d" % p])
                    B.dma(KA[p][0:64, :], sbk[0:64, m, :], r=[("sbk", i) for i in allt], w=["KA%d" % p])
                    B.dma(KB[p][64:128, :], sbk[64:128, m, :], r=[("sbk", i) for i in allt], w=["KB%d" % p])
                    B.dma(VP[p][:], sbv[:, :, m * 128:(m + 1) * 128], r=[("sbv", i) for i in allt], w=["VP%d" % p])

                jobs = []
                for h in range(8):
                    for ti, (t0, W) in enumerate(TILES):
                        nkb = (t0 + W) // 128
                        for kb in reversed(range(nkb)):
                            jobs.append((h, ti, kb, kb == nkb - 1, kb == 0))
                njobs = len(jobs)

                def kop(h):
                    p = (h // 2) % 2
                    return (KA[p], "KA%d" % p) if h % 2 == 0 else (KB[p], "KB%d" % p)

                def mask_idx(ti, kb, t0):
                    c = kb - t0 // 128
                    if ti == 0:
                        return 4
                    if c >= 0:
                        return c
                    if kb == 0:
                        return 5
                    return None

                def stage1(ji):
                    h, ti, kb, first, last = jobs[ji]
                    t0, W = TILES[ti]
                    p = (h // 2) % 2
                    Kx, kk = kop(h)
                    i = ji % NR
                    zb = ji % 2
                    B.mm(ps[zb][:, 0:W], Kx[:, kb * 128:(kb + 1) * 128], QP[p][:, t0:t0 + W], True, True,
                         r=[kk, "QP%d" % p], w=[("ps", zb)])
                    B.op("act", lambda e: e.activation(out=EB[i][:, 0:W], in_=ps[zb][:, 0:W], func=AF.Exp),
                         r=[("ps", zb)], w=["E%d" % i])
                    B.op("act", lambda e: e.activation(out=LK[i][:, 0:W], in_=EB[i][:, 0:W], func=AF.Ln, bias=1.0),
                         r=["E%d" % i], w=["LK%d" % i])
                    mi = mask_idx(ti, kb, t0)
                    if mi is not None:
                        B.op("pool", lambda e: e.tensor_tensor(out=LK[i][:, 0:W], in0=LK[i][:, 0:W], in1=MS[:, mi, 0:W],
                                                               op=ALU.mult), r=["LK%d" % i, "MS"], w=["LK%d" % i])
                    B.op("dve", lambda e: e.tensor_copy(out=HI[i][:, 0:W], in_=LK[i][:, 0:W]),
                         r=["LK%d" % i], w=["HI%d" % i])
                    B.op("pool", lambda e: e.tensor_tensor(out=LO[i][:, 0:W], in0=LK[i][:, 0:W], in1=HI[i][:, 0:W],
                                                           op=ALU.subtract),
                         r=["LK%d" % i, "HI%d" % i], w=["LO%d" % i])

                def stage2(ji):
                    h, ti, kb, first, last = jobs[ji]
                    t0, W = TILES[ti]
                    p = (h // 2) % 2
                    Kx, kk = kop(h)
                    i = ji % NR
                    eb = 2 + ji % 2
                    tb = 4 + ji % 2
                    cp = (h * 9 + ti) % 2
                    Cc, kc = CC[cp], "C%d" % cp
                    B.mm(ps[eb][:, 0:W], UIb[:], HI[i][:, 0:W], True, False, r=["UIb", "HI%d" % i], w=[("ps", eb)])
                    B.mm(ps[eb][:, 0:W], UIb[:], LO[i][:, 0:W], False, False, r=["UIb", "LO%d" % i], w=[("ps", eb)])
                    B.mm(ps[eb][:, 0:W], Kx[:, kb * 128:(kb + 1) * 128], QP[p][:, t0:t0 + W], False, True,
                         r=[kk, "QP%d" % p], w=[("ps", eb)])
                    if not last:
                        B.mm(ps[tb][:, 0:W], ONb[:], HI[i][:, 0:W], True, False, r=["ONb", "HI%d" % i], w=[("ps", tb)])
                        B.mm(ps[tb][:, 0:W], ONb[:], LO[i][:, 0:W], False, True, r=["ONb", "LO%d" % i], w=[("ps", tb)])
                    if first:
                        B.op("act", lambda e: e.activation(out=WW[i][:, 0:W], in_=ps[eb][:, 0:W], func=AF.Exp),
                             r=[("ps", eb)], w=["W%d" % i])
                        if not last:
                            B.op("dve", lambda e: e.tensor_copy(out=Cc[:, 0:W], in_=ps[tb][:, 0:W]),
                                 r=[("ps", tb)], w=[kc])
                    else:
                        B.op("dve", lambda e: e.tensor_tensor(out=SS[i][:, 0:W], in0=Cc[:, 0:W], in1=ps[eb][:, 0:W],
                                                              op=ALU.add), r=[("ps", eb), kc], w=["S%d" % i])
                        B.op("act", lambda e: e.activation(out=WW[i][:, 0:W], in_=SS[i][:, 0:W], func=AF.Exp),
                             r=["S%d" % i], w=["W%d" % i])
                        if not last:
                            B.op("dve", lambda e: e.tensor_tensor(out=Cc[:, 0:W], in0=Cc[:, 0:W], in1=ps[tb][:, 0:W],
                                                                  op=ALU.add), r=[("ps", tb), kc], w=[kc])
                    mi = mask_idx(ti, kb, t0)
                    if mi is not None:
                        B.op("pool", lambda e: e.tensor_tensor(out=WW[i][:, 0:W], in0=WW[i][:, 0:W], in1=MS[:, mi, 0:W],
                                                               op=ALU.mult), r=["W%d" % i, "MS"], w=["W%d" % i])

                def stage3(ji):
                    h, ti, kb, first, last = jobs[ji]
                    t0, W = TILES[ti]
                    p = (h // 2) % 2
                    i = ji % NR
                    ob = 6 + (h * 9 + ti) % 2
                    if ti == 0 and first and h % 2 == 0 and h + 2 < 8:
                        load_pair(h // 2 + 1)
                    B.mm(ps[ob][:, 0:W], VP[p][:, kb, :], WW[i][:, 0:W], first, last,
                         r=["VP%d" % p, "W%d" % i], w=[("ps", ob)])
                    if last:
                        oi = (h * 9 + ti) % 2
                        r0 = (h % 2) * 64
                        B.evac(OS[oi][r0:r0 + 64, 0:W], ps[ob][r0:r0 + 64, 0:W], r=[("ps", ob)], w=["OS%d" % oi])
                        B.dma(mix[r0:r0 + 64, h // 2, t0:t0 + W], OS[oi][r0:r0 + 64, 0:W], r=["OS%d" % oi],
                              w=[("mix", ti)])

                load_pair(0)
                for step in range(njobs + 2):
                    if step < njobs:
                        stage1(step)
                    if 0 <= step - 1 < njobs:
                        stage2(step - 1)
                    if 0 <= step - 2 < njobs:
                        stage3(step - 2)
                B.barrier()

        def phase_B2():
            with ExitStack() as ph:
                MM = sb(ph, "m_MM", [128, 5, 512], F32)
                VAL = sb(ph, "m_VAL", [128, 1], F32)
                B.dma(MM[:], c_mmla, w=["MM"])
                B.dma(VAL[:], c_valid, w=["VALm"])
                QT = [sb(ph, "m_QT%d" % i, [96, T], BF16) for i in range(2)]
                KT = [sb(ph, "m_KT%d" % i, [96, T], BF16) for i in range(2)]
                VV = [sb(ph, "m_VV%d" % i, [128, NBLK, 64], BF16) for i in range(2)]
                NR = 3
                PP = [sb(ph, "m_P%d" % i, [128, 512], BF16) for i in range(NR)]
                RC = [sb(ph, "m_RC%d" % i, [64, 512], F32) for i in range(2)]
                OS = [sb(ph, "m_OS%d" % i, [64, 512], BF16) for i in range(2)]

                def load_head(h):
                    p = h % 2
                    B.dma(QT[p][:], mq[:, h, :], r=[("mq", i) for i in range(9)], w=["mQT%d" % p])
                    B.dma(KT[p][:], mk[:, h, :], r=[("mk", i) for i in range(9)], w=["mKT%d" % p])
                    B.dma(VV[p][:], mv[:, :, h * 64:(h + 1) * 64], r=[("mv", i) for i in range(9)], w=["mVV%d" % p])

                jobs = []
                for h in range(8):
                    for ti, (t0, W) in enumerate(TILES):
                        nkb = (t0 + W) // 128
                        for kb in reversed(range(nkb)):
                            jobs.append((h, ti, kb, kb == nkb - 1, kb == 0))
                njobs = len(jobs)

                def stage1(ji):
                    h, ti, kb, first, last = jobs[ji]
                    t0, W = TILES[ti]
                    p = h % 2
                    i = ji % NR
                    sbk_ = ji % 3
                    B.mm(ps[sbk_][:, 0:W], KT[p][:, kb * 128:(kb + 1) * 128], QT[p][:, t0:t0 + W], True, True,
                         r=["mKT%d" % p, "mQT%d" % p], w=[("ps", sbk_)])
                    B.op("act", lambda e: e.activation(out=PP[i][:, 0:W], in_=ps[sbk_][:, 0:W], func=AF.Exp,
                                                       scale=MLA_SCALE), r=[("ps", sbk_)], w=["P%d" % i])
                    c = kb - t0 // 128
                    if ti == 0:
                        B.op("pool", lambda e: e.tensor_tensor(out=PP[i][:, 0:W], in0=PP[i][:, 0:W], in1=MM[:, 4, 0:W],
                                                               op=ALU.mult), r=["P%d" % i, "MM"], w=["P%d" % i])
                    elif c >= 0:
                        B.op("pool", lambda e: e.tensor_tensor(out=PP[i][:, 0:W], in0=PP[i][:, 0:W], in1=MM[:, c, 0:W],
                                                               op=ALU.mult), r=["P%d" % i, "MM"], w=["P%d" % i])
                    elif kb == 0:
                        B.op("pool", lambda e: e.tensor_scalar(out=PP[i][:, 0:W], in0=PP[i][:, 0:W], scalar1=VAL[:, 0:1],
                                                               scalar2=None, op0=ALU.mult),
                             r=["P%d" % i, "VALm"], w=["P%d" % i])

                def stage2(ji):
                    h, ti, kb, first, last = jobs[ji]
                    t0, W = TILES[ti]
                    p = h % 2
                    i = ji % NR
                    par = (h * 9 + ti) % 2
                    ob = 4 + par
                    db = 6 + par
                    if ti == 0 and first and h + 1 < 8:
                        load_head(h + 1)
                    B.mm(ps[ob][0:64, 0:W], VV[p][:, kb, :], PP[i][:, 0:W], first, last,
                         r=["mVV%d" % p, "P%d" % i], w=[("ps", ob)])
                    B.mm(ps[db][0:64, 0:W], onesB[:], PP[i][:, 0:W], first, last,
                         r=["onesB", "P%d" % i], w=[("ps", db)])
                    if last:
                        B.op("dve", lambda e: e.reciprocal(out=RC[par][:, 0:W], in_=ps[db][0:64, 0:W]),
                             r=[("ps", db)], w=["RC%d" % par])
                        B.op("dve", lambda e: e.tensor_tensor(out=OS[par][:, 0:W], in0=ps[ob][0:64, 0:W],
                                                              in1=RC[par][:, 0:W], op=ALU.mult),
                             r=[("ps", ob), "RC%d" % par], w=["mOS%d" % par])
                        r0 = (h % 2) * 64
                        B.dma(mix[r0:r0 + 64, 4 + h // 2, t0:t0 + W], OS[par][:, 0:W], r=["mOS%d" % par],
                              w=[("mix", ti)])

                load_head(0)
                for step in range(njobs + 1):
                    if step < njobs:
                        stage1(step)
                    if 0 <= step - 1 < njobs:
                        stage2(step - 1)
                B.barrier()

        def phase_B3():
            with ExitStack() as ph:
                RQ = sb(ph, "r_RQ", [64, 4, T], BF16)
                RQD = sb(ph, "r_RQD", [64, 4, T], BF16)
                RK = sb(ph, "r_RK", [64, 4, T], BF16)
                RV = sb(ph, "r_RV", [128, NBLK, 512], BF16)
                RKD = sb(ph, "r_RKD", [128, NBLK, 256], BF16)
                DT = sb(ph, "r_DT", [128, 512], F32)
                allt = list(range(9))
                for h in range(4):
                    r0 = (h % 2) * 64
                    B.dma(RQ[:, h, :], rq[r0:r0 + 64, h // 2, :], r=[("rq", i) for i in allt], w=["RQ"])
                    B.dma(RQD[:, h, :], rqd[r0:r0 + 64, h // 2, :], r=[("rqd", i) for i in allt], w=["RQD"])
                    B.dma(RK[:, h, :], rk[r0:r0 + 64, h // 2, :], r=[("rk", i) for i in allt], w=["RK"])
                B.dma(RV[:], rv, r=[("rv", i) for i in allt], w=["RV"])
                B.dma(RKD[:], rkd, r=[("rkd", i) for i in allt], w=["RKD"])
                B.dma(DT[:], c_dt, w=["DT"])
                ST = sb(ph, "r_ST", [64, 4, 128], F32)
                STB = sb(ph, "r_STB", [64, 4, 128], BF16)
                B.op("pool", lambda e: e.memset(ST[:], 0.0), w=["ST"])
                B.op("pool", lambda e: e.memset(STB[:], 0.0), w=["STB"])
                IND = [sb(ph, "r_IND%d" % i, [128, 512], BF16) for i in range(2)]
                YS = [sb(ph, "r_YS%d" % i, [128, 512], F32) for i in range(2)]
                SQ = [sb(ph, "r_SQ%d" % i, [128, 512], F32) for i in range(2)]
                R = [sb(ph, "r_R%d" % i, [128, 512], F32) for i in range(2)]
                GG = [sb(ph, "r_GG%d" % i, [128, 4, 128], F32) for i in range(2)]
                OC = [sb(ph, "r_OC%d" % i, [128, 4, 128], BF16) for i in range(2)]
                cdec = [float(np.float64(g) ** 128.0) for g in RET_GAMMA]
                for n in range(NBLK):
                    p = n % 2
                    tsl = slice(n * 128, (n + 1) * 128)
                    ti = 0 if n == 0 else 1 + (n - 1) // 4
                    B.dma(GG[p][:], rg[:, :, tsl], r=[("rg", ti)], w=["GG%d" % p])
                    bi = bank()
                    for h in range(4):
                        B.mm(ps[bi][:, h * 128:(h + 1) * 128], RK[:, h, tsl], RQ[:, h, tsl],
                             True, True, r=["RK", "RQ"], w=[("ps", bi)])
                    B.op("dve", lambda e: e.tensor_tensor(out=IND[p][:], in0=ps[bi][:], in1=DT[:], op=ALU.mult),
                         r=[("ps", bi), "DT"], w=["IND%d" % p])
                    by = bank()
                    for h in range(4):
                        B.mm(ps[by][:, h * 128:(h + 1) * 128], RV[:, n, h * 128:(h + 1) * 128],
                             IND[p][:, h * 128:(h + 1) * 128], True, False, r=["RV", "IND%d" % p], w=[("ps", by)])
                        B.mm(ps[by][:, h * 128:(h + 1) * 128], STB[:, h, :], RQD[:, h, tsl],
                             False, True, r=["STB", "RQD"], w=[("ps", by)])
                    if n + 1 < NBLK:
                        bs = bank()
                        for h in range(4):
                            B.mm(ps[bs][0:64, h * 128:(h + 1) * 128], RKD[:, n, h * 64:(h + 1) * 64],
                                 RV[:, n, h * 128:(h + 1) * 128], True, True, r=["RKD", "RV"], w=[("ps", bs)])
                        for h in range(4):
                            B.op("dve", lambda e: e.scalar_tensor_tensor(
                                out=ST[:, h, :], in0=ST[:, h, :], scalar=cdec[h],
                                in1=ps[bs][0:64, h * 128:(h + 1) * 128], op0=ALU.mult, op1=ALU.add),
                                r=["ST", ("ps", bs)], w=["ST"])
                        B.op("pool", lambda e: e.tensor_copy(out=STB[:], in_=ST[:]), r=["ST"], w=["STB"])
                    B.op("act", lambda e: e.copy(out=YS[p][:], in_=ps[by][:]), r=[("ps", by)], w=["YS%d" % p])
                    bm = bank()
                    B.mm(ps[bm][:], onesF[:], YS[p][:], True, True, r=["onesF", "YS%d" % p], w=[("ps", bm)])
                    B.op("dve", lambda e: e.scalar_tensor_tensor(out=YS[p][:], in0=ps[bm][:], scalar=-1.0 / 128,
                                                                 in1=YS[p][:], op0=ALU.mult, op1=ALU.add),
                         r=[("ps", bm), "YS%d" % p], w=["YS%d" % p])
                    B.op("act", lambda e: e.activation(out=SQ[p][:], in_=YS[p][:], func=AF.Square),
                         r=["YS%d" % p], w=["rSQ%d" % p])
                    bv = bank()
                    B.mm(ps[bv][:], onesF[:], SQ[p][:], True, True, r=["onesF", "rSQ%d" % p], w=[("ps", bv)])
                    B.op("act", lambda e: e.activation(out=R[p][:], in_=ps[bv][:], func=AF.Ln, scale=1.0 / 128,
                                                       bias=LN_EPS), r=[("ps", bv)], w=["rR%d" % p])
                    B.op("act", lambda e: e.activation(out=R[p][:], in_=R[p][:], func=AF.Exp, scale=-0.5),
                         r=["rR%d" % p], w=["rR%d" % p])
                    B.op("dve", lambda e: e.tensor_tensor(out=YS[p][:], in0=YS[p][:], in1=R[p][:], op=ALU.mult),
                         r=["YS%d" % p, "rR%d" % p], w=["YS%d" % p])
                    B.op("pool", lambda e: e.tensor_tensor(out=OC[p][:], in0=YS[p][:].rearrange("p (h n) -> p h n", h=4),
                                                           in1=GG[p][:], op=ALU.mult),
                         r=["YS%d" % p, "GG%d" % p], w=["OC%d" % p])
                    B.dma(mix[:, 8:12, tsl], OC[p][:], r=["OC%d" % p], w=[("mix", ti)])
                B.barrier()

        def phase_C(l):
            with ExitStack() as ph:
                WO = sb(ph, "c_WO", [128, 12, D], BF16)
                B.dma(WO[:], wo[l].rearrange("(c p) n -> p c n", p=128), w=["WO"], q="pool")
                MX = [sb(ph, "c_MX%d" % i, [128, 12, 512], BF16) for i in range(2)]
                X = [sb(ph, "c_X%d" % i, [128, 8, 512], F32) for i in range(2)]
                SQ = sb(ph, "c_SQ", [128, 8, 512], F32)
                R = sb(ph, "c_R", [128, 512], F32)
                XB = sb(ph, "c_XB", [128, 8, 512], BF16)

                def load_tile(ti):
                    t0, W = TILES[ti]
                    p = ti % 2
                    B.dma(MX[p][:, :, 0:W], mix[:, :, t0:t0 + W], r=[("mix", ti)], w=["MX%d" % p])
                    B.dma(X[p][:, :, 0:W], hres[:, :, t0:t0 + W], r=[("hres", t0)], w=["cX%d" % p])

                load_tile(0)
                for ti, (t0, W) in enumerate(TILES):
                    p = ti % 2
                    if ti + 1 < len(TILES):
                        load_tile(ti + 1)
                    for dc in range(8):
                        b = bank()
                        for m in range(12):
                            B.mm(ps[b][:, 0:W], WO[:, m, dc * 128:(dc + 1) * 128], MX[p][:, m, 0:W], m == 0, m == 11,
                                 r=["WO", "MX%d" % p], w=[("ps", b)])
                        B.op("dve", lambda e, dc=dc, b=b: e.scalar_tensor_tensor(
                            out=X[p][:, dc, 0:W], in0=X[p][:, dc, 0:W], scalar=DN_ALPHA, in1=ps[b][:, 0:W],
                            op0=ALU.mult, op1=ALU.add), r=["cX%d" % p, ("ps", b)], w=["cX%d" % p])
                    ln_f(None, X[p], "cX%d" % p, SQ[:], ["SQ"], R, XB, W, 1 + 2 * l, t0, False)
                B.barrier()

        def phase_D(l, final):
            with ExitStack() as ph:
                W1 = sb(ph, "d_W1", [128, 8, 4096], BF16)
                W2 = sb(ph, "d_W2", [128, 32, D], BF16)
                for c in range(8):
                    B.dma(W1[:, c, :], w1[l, c * 128:(c + 1) * 128, :], w=["W1"], q="pool")
                for c in range(4):
                    B.dma(W2[:, c * 8:(c + 1) * 8, :], w2[l, c * 1024:(c + 1) * 1024, :].rearrange("(c p) n -> p c n", p=128),
                          w=["W2"], q="pool")
                HB = [sb(ph, "d_HB%d" % i, [128, 8, 512], BF16) for i in range(2)]
                X = sb(ph, "d_X", [128, 8, 512], F32)
                R = sb(ph, "d_R", [128, 512], F32)
                XB = sb(ph, "d_XB", [128, 8, 512], BF16)
                UT = sb(ph, "d_UT", [128, 16, 512], BF16)
                SQ = UT[:].rearrange("p a n -> p (a n)").bitcast(F32).rearrange("p (c n) -> p c n", c=8)
                kut = [("UT", fc) for fc in range(16)]
                RL = [sb(ph, "d_RL%d" % i, [128, 512], F32) for i in range(3)]
                phd = {}
                if final:
                    phd["OT"] = [sb(ph, "d_OT%d" % i, [128, D], F32) for i in range(2)]

                def load_tile(ti):
                    t0, W = TILES[ti]
                    B.dma(HB[ti % 2][:, :, 0:W], hT[:, :, t0:t0 + W], r=[("hT", t0)], w=["HB%d" % (ti % 2)])

                load_tile(0)
                for ti, (t0, W) in enumerate(TILES):
                    p = ti % 2
                    if ti + 1 < len(TILES):
                        load_tile(ti + 1)
                    B.dma(X[:, :, 0:W], hres[:, :, t0:t0 + W], r=[("hres", t0)], w=["dX"])
                    for half in range(2):
                        for f in range(16):
                            fc = half * 16 + f
                            b = bank()
                            for c in range(8):
                                B.mm(ps[b][:, 0:W], W1[:, c, fc * 128:(fc + 1) * 128], HB[p][:, c, 0:W], c == 0, c == 7,
                                     r=["W1", "HB%d" % p], w=[("ps", b)])
                            i = fc % 3
                            B.op("act", lambda e: e.activation(out=RL[i][:, 0:W], in_=ps[b][:, 0:W], func=AF.Relu),
                                 r=[("ps", b)], w=["RL%d" % i])
                            eng = "pool" if fc % 2 == 0 else "dve"
                            B.op(eng, lambda e: e.tensor_tensor(out=UT[:, f, 0:W], in0=RL[i][:, 0:W],
                                                                in1=RL[i][:, 0:W], op=ALU.mult),
                                 r=["RL%d" % i], w=[("UT", f)])
                        for dc in range(8):
                            b = bank()
                            for f in range(16):
                                fc = half * 16 + f
                                B.mm(ps[b][:, 0:W], W2[:, fc, dc * 128:(dc + 1) * 128], UT[:, f, 0:W], f == 0, f == 15,
                                     r=["W2", ("UT", f)], w=[("ps", b)])
                            B.op("dve", lambda e: e.scalar_tensor_tensor(
                                out=X[:, dc, 0:W], in0=X[:, dc, 0:W], scalar=(DN_ALPHA if half == 0 else 1.0),
                                in1=ps[b][:, 0:W], op0=ALU.mult, op1=ALU.add), r=["dX", ("ps", b)], w=["dX"])
                    ln_f(phd, X, "dX", SQ, kut, R, XB, W, 2 + 2 * l, t0, final)
                B.barrier()

        import os as _os
        PH = _os.environ.get("KPH", "E,A,B1,B2,B3,C,D").split(",")
        if "E" in PH:
            phase_embed()
        for l in range(NL):
            if "A" in PH:
                phase_A(l)
            if "B1" in PH:
                phase_B1()
            if "B2" in PH:
                phase_B2()
            if "B3" in PH:
                phase_B3()
            if "C" in PH:
                phase_C(l)
            if "D" in PH:
                phase_D(l, l == NL - 1)
        B.finish()
        print("instr counts", B.cnt, "sems", len(B.sems), flush=True)
    return nc


def _consts():
    c = {}
    c["c_ident"] = np.eye(128, dtype=np.float32)
    j = np.arange(128)
    c["c_nuincl"] = -(j[:, None] >= j[None, :]).astype(np.float32)
    pos = (np.arange(T) - 112).astype(np.float32)

    def tables(half):
        inv = (np.float32(10000.0) ** (-np.arange(half, dtype=np.float32) / np.float32(half))).astype(np.float32)
        ang = (pos[None, :] * inv[:, None]).astype(np.float32)
        return np.cos(ang).astype(np.float32), np.sin(ang).astype(np.float32)

    c32, s32 = tables(32)
    C = np.concatenate([c32, c32, c32, c32], 0)
    S = np.concatenate([-s32, s32, -s32, s32], 0)
    c["c_ropeR"] = np.ascontiguousarray(np.stack([C, S, 0.125 * C, 0.125 * S], 1).astype(np.float32))
    c16, s16 = tables(16)
    CM = np.concatenate([np.ones((64, T), np.float32), c16, c16], 0)
    SM = np.concatenate([np.zeros((64, T), np.float32), -s16, s16], 0)
    c["c_ropeM"] = np.ascontiguousarray(np.stack([CM, SM], 1).astype(np.float32))
    g = np.array(RET_GAMMA, np.float64)
    n = np.arange(128, dtype=np.float64)
    qd = np.zeros((128, 2, 128), np.float64)
    for h in range(4):
        qd[(h % 2) * 64:(h % 2) * 64 + 64, h // 2, :] = (g[h] ** (n + 1.0))[None, :]
    c["c_qdec"] = qd.astype(np.float32)
    kd = np.zeros((128, 256), np.float64)
    for h in range(4):
        kd[:, h * 64:(h + 1) * 64] = (g[h] ** (127.0 - n))[:, None]
    c["c_kdecT"] = kd.astype(np.float32)
    dt = np.zeros((128, 4, 128), np.float64)
    diff = n[None, :] - n[:, None]
    for h in range(4):
        dt[:, h, :] = np.where(diff >= 0, g[h] ** np.maximum(diff, 0.0), 0.0)
    c["c_dt"] = dt.reshape(128, 512).astype(np.float32)
    cv = np.zeros((128, 2), np.float64)
    for h in range(4):
        cv[(h % 2) * 64:(h % 2) * 64 + 64, h // 2] = g[h] ** 128.0
    c["c_cvec"] = cv.astype(np.float32)
    s_ = np.arange(128)[:, None]
    t_ = np.arange(512)[None, :]
    msb = np.zeros((128, 6, 512), np.float32)
    mml = np.zeros((128, 5, 512), np.float32)
    for cc in range(4):
        msb[:, cc, :] = ((cc * 128 + s_) < t_)
        mml[:, cc, :] = ((cc * 128 + s_) <= t_)
    valid = (s_ >= 112)
    msb[:, 4, :] = ((s_ < t_) & valid)
    msb[:, 5, :] = np.broadcast_to(valid, (128, 512))
    mml[:, 4, :] = ((s_ <= t_) & (valid | (s_ == t_)))
    c["c_msb"] = msb
    c["c_mmla"] = mml
    c["c_valid"] = valid.astype(np.float32).reshape(128, 1)
    return c


def _rot_idx(nheads, d):
    half = d // 2
    idx = []
    for h in range(nheads):
        idx += [h * d + ((i + half) % d) for i in range(d)]
    return np.array(idx)


def _prep_weights(inp, NL):
    w_in = np.asarray(inp["w_in"])[:NL]
    b_sbq, b_sbk, b_sbv, b_cq, b_ckv, b_kr, b_rq, b_rk, b_rv, b_rg = 0, 512, 1024, 1536, 1920, 2176, 2208, 2464, 2720, 3232
    cols = np.concatenate([
        np.arange(b_sbq, b_sbq + 512), np.arange(b_sbk, b_sbk + 512), np.arange(b_sbv, b_sbv + 512),
        np.arange(b_cq, b_cq + 384), np.arange(b_ckv, b_ckv + 256),
        np.arange(b_kr, b_kr + 32), b_kr + _rot_idx(1, 32),
        np.arange(b_rq, b_rq + 256), b_rq + _rot_idx(4, 64),
        np.arange(b_rk, b_rk + 256), b_rk + _rot_idx(4, 64),
        np.arange(b_rv, b_rv + 512), np.arange(b_rg, b_rg + 512)])
    assert cols.size == NA
    wa = np.ascontiguousarray(np.take(w_in, cols, axis=2))
    w_uq = np.asarray(inp["w_uq"])[:NL]
    ucols = []
    for h in range(8):
        base = h * 96
        xc = list(range(base, base + 96))
        rc = list(range(base, base + 64)) + [base + 64 + ((i + 16) % 32) for i in range(32)]
        ucols += xc + rc
    wuq = np.ascontiguousarray(np.take(w_uq, np.array(ucols), axis=2))
    w_ukv = np.asarray(inp["w_ukv"])[:NL]
    kc = np.concatenate([np.arange(h * 128, h * 128 + 64) for h in range(8)] +
                        [np.arange(h * 128 + 64, h * 128 + 128) for h in range(8)])
    wukv = np.ascontiguousarray(np.take(w_ukv, kc, axis=2))
    lnp = np.zeros((9, 2, 1024), np.float32)
    lnp[0, 0] = np.asarray(inp["ln_emb_g"])
    lnp[0, 1] = np.asarray(inp["ln_emb_b"])
    for l in range(NLAYERS):
        lnp[1 + 2 * l, 0] = np.asarray(inp["ln1_g"])[l]
        lnp[1 + 2 * l, 1] = np.asarray(inp["ln1_b"])[l]
        lnp[2 + 2 * l, 0] = np.asarray(inp["ln2_g"])[l]
        lnp[2 + 2 * l, 1] = np.asarray(inp["ln2_b"])[l]
    lnp = np.ascontiguousarray(lnp.reshape(9, 2, 8, 128).transpose(3, 0, 1, 2))
    qn = np.ascontiguousarray(np.asarray(inp["mla_q_norm"]).reshape(NLAYERS, 3, 128).transpose(2, 0, 1))
    kvn = np.ascontiguousarray(np.asarray(inp["mla_kv_norm"]).reshape(NLAYERS, 2, 128).transpose(2, 0, 1))
    shared = {
        "meta": np.ascontiguousarray(np.asarray(inp["meta_tokens"], dtype=np.float32)),
        "lnp": lnp.astype(np.float32), "qn": qn.astype(np.float32), "kvn": kvn.astype(np.float32),
        "wa": wa.astype(np.float32), "wuq": wuq.astype(np.float32), "wukv": wukv.astype(np.float32),
        "wo": np.ascontiguousarray(np.asarray(inp["w_out"], dtype=np.float32)[:NL]),
        "w1": np.ascontiguousarray(np.asarray(inp["w_ff1"], dtype=np.float32)[:NL]),
        "w2": np.ascontiguousarray(np.asarray(inp["w_ff2"], dtype=np.float32)[:NL]),
    }
    shared.update(_consts())
    return shared


def run(inp, NL=NLAYERS, cores=8):
    shared = _prep_weights(inp, NL)
    x = np.asarray(inp["x"], dtype=np.float32)
    nc = build_nc(NL)
    in_maps = []
    for b in range(cores):
        m = dict(shared)
        m["x"] = np.ascontiguousarray(x[b])
        in_maps.append(m)
    res = run_bass_kernel_spmd(nc, in_maps, core_ids=list(range(cores)))
    return np.stack([np.asarray(r["out"]) for r in res.results], axis=0).astype(np.float32)


def kernel(**inputs):
    return run(inputs, NLAYERS, 8)
```

```python
import numpy as np
import ml_dtypes
from contextlib import ExitStack
import concourse.bass as bass
import concourse.mybir as mybir
from concourse.bass_utils import run_bass_kernel_spmd

F32 = mybir.dt.float32
BF16 = mybir.dt.bfloat16
ALU = mybir.AluOpType
AF = mybir.ActivationFunctionType

D = 1024
SEQ = 4096
NLAYERS = 4
T = 4224
NBLK = 33
TILES = [(0, 128)] + [(128 + 512 * i, 512) for i in range(8)]
LN_EPS = 1e-5
DN_ALPHA = (2 * NLAYERS) ** 0.25
RET_GAMMA = [1.0 - 2.0 ** (-5 - h) for h in range(4)]
MLA_SCALE = 96 ** -0.5

O_SBQ, O_SBK, O_SBV, O_CQ, O_CKV, O_KR, O_KRR = 0, 512, 1024, 1536, 1920, 2176, 2208
O_RQ, O_RQR, O_RK, O_RKR, O_RV, O_RG = 2240, 2496, 2752, 3008, 3264, 3776
NA = 4288
SEM_LIM = 12000
_TRACE = False


class Builder:
    def __init__(self, nc, es):
        self.nc = nc
        self.es = es
        self.E = {"pe": nc.tensor, "act": nc.scalar, "dve": nc.vector, "pool": nc.gpsimd, "sp": nc.sync}
        self.cnt = {e: 0 for e in self.E}
        self.waited = {e: {} for e in self.E}
        self.sems = {}
        self.st = {}
        self.dq = {}
        self.flip = 0

    def sem(self, key):
        if key not in self.sems:
            self.sems[key] = self.es.enter_context(self.nc.semaphore("s%d" % len(self.sems)))
        return self.sems[key]

    def _wait(self, eng, tok):
        sk, val = tok
        w = self.waited[eng]
        if w.get(sk, 0) >= val:
            return
        w[sk] = val
        self.E[eng].wait_ge(self.sem(sk), val)

    def _deps(self, eng, r, w):
        toks = set()
        for k in r:
            s = self.st.get(k)
            if s and s[0]:
                toks.add(s[0])
        for k in w:
            s = self.st.get(k)
            if s:
                if s[0] and s[0][0][0] != eng:
                    toks.add(s[0])
                for sk, v in s[1].items():
                    if sk[0] != eng:
                        toks.add((sk, v))
        for t in toks:
            self._wait(eng, t)

    def _update(self, tok, r, w):
        sk, val = tok
        for k in r:
            s = self.st.setdefault(k, [None, {}])
            if s[1].get(sk, 0) < val:
                s[1][sk] = val
        for k in w:
            self.st[k] = [tok, {}]

    def op(self, eng, fn, r=(), w=()):
        self._deps(eng, r, w)
        ins = fn(self.E[eng])
        self.cnt[eng] += 1
        i = self.cnt[eng]
        sk = (eng, (i - 1) // SEM_LIM)
        val = (i - 1) % SEM_LIM + 1
        ins.then_inc(self.sem(sk), 1)
        self._update((sk, val), r, w)

    def dma(self, out, in_, r=(), w=(), q="sp"):
        d = self.dq.setdefault(q, {"uses": [0] * 8, "next": 0})
        slot = d["next"]
        d["next"] = (slot + 1) % 8
        sk = ("dma", q, slot)
        if d["uses"][slot] > 0:
            self._wait(q, (sk, 16 * d["uses"][slot]))
        self._deps(q, r, w)
        self.E[q].dma_start(out=out, in_=in_).then_inc(self.sem(sk), 16)
        d["uses"][slot] += 1
        self._update((sk, 16 * d["uses"][slot]), r, w)

    def all_tokens(self):
        toks = []
        for e in ("pe", "act", "dve", "pool"):
            i = self.cnt[e]
            if i > 0:
                toks.append(((e, (i - 1) // SEM_LIM), (i - 1) % SEM_LIM + 1))
        for q, d in self.dq.items():
            for slot, u in enumerate(d["uses"]):
                if u > 0:
                    toks.append((("dma", q, slot), 16 * u))
        return toks

    def barrier(self):
        toks = self.all_tokens()
        for e in self.E:
            for t in toks:
                if t[0][0] == e:
                    continue
                self._wait(e, t)

    def finish(self):
        for t in self.all_tokens():
            if t[0][0] == "dma":
                self._wait("sp", t)

    def mm(self, out, lhsT, rhs, start, stop, r, w, **kw):
        self.op("pe", lambda e: e.matmul(out, lhsT, rhs, start=start, stop=stop, **kw), r=r, w=w)

    def tr(self, out, in_, ident, r, w):
        self.op("pe", lambda e: e.transpose(out, in_, ident), r=r, w=w)

    def evac(self, out, in_, r, w, scale=None, eng=None):
        if eng is None:
            self.flip ^= 1
            eng = "act" if self.flip else "dve"
        if eng == "act":
            if scale is None:
                self.op("act", lambda e: e.copy(out=out, in_=in_), r=r, w=w)
            else:
                self.op("act", lambda e: e.mul(out=out, in_=in_, mul=scale), r=r, w=w)
        else:
            if scale is None:
                self.op("dve", lambda e: e.tensor_copy(out=out, in_=in_), r=r, w=w)
            else:
                self.op("dve", lambda e: e.tensor_scalar(out=out, in0=in_, scalar1=scale, scalar2=None,
                                                         op0=ALU.mult), r=r, w=w)


def build_nc(NL, debug=False):
    nc = bass.Bass("TRN2", target_bir_lowering=False)

    def din(name, shape, dt=F32):
        return nc.dram_tensor(name, shape, dt, kind="ExternalInput").ap()

    def dscr(name, shape, dt):
        return nc.dram_tensor(name, shape, dt, kind="Internal").ap()

    x = din("x", [SEQ, D])
    meta = din("meta", [16, D])
    lnp = din("lnp", [128, 9, 2, 8])
    qn = din("qn", [128, NLAYERS, 3])
    kvn = din("kvn", [128, NLAYERS, 2])
    wa = din("wa", [NL, D, NA])
    wuq = din("wuq", [NL, 384, 1536])
    wukv = din("wukv", [NL, 256, 1024])
    wo = din("wo", [NL, 1536, D])
    w1 = din("w1", [NL, D, 4096])
    w2 = din("w2", [NL, 4096, D])
    c_ident = din("c_ident", [128, 128])
    c_nuincl = din("c_nuincl", [128, 128])
    c_ropeR = din("c_ropeR", [128, 4, T])
    c_ropeM = din("c_ropeM", [96, 2, T])
    c_qdec = din("c_qdec", [128, 2, 128])
    c_kdecT = din("c_kdecT", [128, 256])
    c_dt = din("c_dt", [128, 512])
    c_cvec = din("c_cvec", [128, 2])
    c_msb = din("c_msb", [128, 6, 512])
    c_mmla = din("c_mmla", [128, 6, 512])
    c_valid = din("c_valid", [128, 1])
    out = nc.dram_tensor("out", [SEQ, D], F32, kind="ExternalOutput").ap()

    hres = dscr("hres", [128, 8, T], F32)
    hT = dscr("hT", [128, 8, T], BF16)
    sbq = dscr("sbq", [128, 4, T], BF16)
    sbk = dscr("sbk", [128, 4, T], BF16)
    sbv = dscr("sbv", [128, NBLK, 512], BF16)
    mq = dscr("mq", [128, 8, T], BF16)
    mk = dscr("mk", [128, 8, T], BF16)
    mv = dscr("mv", [128, NBLK, 512], BF16)
    rq = dscr("rq", [128, 2, T], BF16)
    rqd = dscr("rqd", [128, 2, T], BF16)
    rk = dscr("rk", [128, 2, T], BF16)
    rkd = dscr("rkd", [128, NBLK, 256], BF16)
    rv = dscr("rv", [128, NBLK, 512], BF16)
    rg = dscr("rg", [128, 4, T], F32)
    mix = dscr("mix", [128, 12, T], BF16)

    with ExitStack() as es:
        B = Builder(nc, es)

        uniq = {"n": 0}

        def sb(stack, name, shape, dt):
            uniq["n"] += 1
            return stack.enter_context(nc.sbuf_tensor("%s_%d" % (name, uniq["n"]), shape, dt))

        ps = [es.enter_context(nc.psum_tensor("ps%d" % i, [128, 512], F32)) for i in range(8)]
        rot = {"i": 0}

        def bank(lo=0, hi=8):
            n = hi - lo
            b = lo + rot["i"] % n
            rot["i"] += 1
            return b

        identF = sb(es, "identF", [128, 128], F32)
        identB = sb(es, "identB", [128, 128], BF16)
        onesF = sb(es, "onesF", [128, 128], F32)
        onesB = sb(es, "onesB", [128, 64], BF16)
        RS = sb(es, "RS", [128, 512], F32)
        lnps = sb(es, "lnps", [128, 9, 2, 8], F32)
        qns = sb(es, "qns", [128, NLAYERS, 3], F32)
        kvns = sb(es, "kvns", [128, NLAYERS, 2], F32)
        B.dma(identF[:], c_ident, w=["identF"])
        B.dma(identB[:], c_ident, w=["identB"], q="pool")
        B.dma(lnps[:], lnp, w=["lnps"])
        B.dma(qns[:], qn, w=["qns"])
        B.dma(kvns[:], kvn, w=["kvns"])
        B.op("pool", lambda e: e.memset(onesF[:], 1.0), w=["onesF"])
        B.op("pool", lambda e: e.memset(onesB[:], 1.0), w=["onesB"])
        with ExitStack() as z0:
            ZR = sb(z0, "ZR", [32, T], BF16)
            B.op("pool", lambda e: e.memset(ZR[:], 0.0), w=["ZR"])
            B.dma(mq[96:128, :, :], ZR[:].unsqueeze(1).broadcast_to([32, 8, T]), r=["ZR"], w=["mqz"])
            B.dma(mk[96:128, :, :], ZR[:].unsqueeze(1).broadcast_to([32, 8, T]), r=["ZR"], w=["mkz"])
            B.barrier()

        def ln_f(ph, X, kx, SQ, ksq, R, XB, W, lnidx, t0, final):
            xs = X[:, :, 0:W]
            mp = bank()
            B.op("dve", lambda e: e.reduce_sum(out=RS[:, 0:W], in_=xs.rearrange("p c n -> p n c"),
                                               axis=mybir.AxisListType.X), r=[kx], w=["RS"])
            B.mm(ps[mp][:, 0:W], onesF[:], RS[:, 0:W], True, True, r=["RS", "onesF"], w=[("ps", mp)])
            B.op("dve", lambda e: e.scalar_tensor_tensor(
                out=xs, in0=ps[mp][:, 0:W].unsqueeze(1).broadcast_to([128, 8, W]), scalar=-1.0 / D, in1=xs,
                op0=ALU.mult, op1=ALU.add), r=[("ps", mp), kx], w=[kx])
            B.op("act", lambda e: e.activation(out=SQ[:, :, 0:W], in_=xs, func=AF.Square), r=[kx], w=ksq)
            vp = bank()
            B.op("dve", lambda e: e.reduce_sum(out=RS[:, 0:W], in_=SQ[:, :, 0:W].rearrange("p c n -> p n c"),
                                               axis=mybir.AxisListType.X), r=ksq, w=["RS"])
            B.mm(ps[vp][:, 0:W], onesF[:], RS[:, 0:W], True, True, r=["RS", "onesF"], w=[("ps", vp)])
            B.op("act", lambda e: e.activation(out=R[:, 0:W], in_=ps[vp][:, 0:W], func=AF.Ln, scale=1.0 / D,
                                               bias=LN_EPS), r=[("ps", vp)], w=["R"])
            B.op("act", lambda e: e.activation(out=R[:, 0:W], in_=R[:, 0:W], func=AF.Exp, scale=-0.5),
                 r=["R"], w=["R"])
            B.op("dve", lambda e: e.tensor_tensor(out=xs, in0=xs, in1=R[:, 0:W].unsqueeze(1).broadcast_to([128, 8, W]),
                                                  op=ALU.mult), r=[kx, "R"], w=[kx])
            for c in range(8):
                B.op("act", lambda e, c=c: e.activation(out=X[:, c, 0:W], in_=X[:, c, 0:W], func=AF.Identity,
                                                        scale=lnps[:, lnidx, 0, c:c + 1],
                                                        bias=lnps[:, lnidx, 1, c:c + 1]),
                     r=[kx, "lnps"], w=[kx])
            B.op("pool", lambda e: e.tensor_copy(out=XB[:, :, 0:W], in_=xs), r=[kx], w=["XB"])
            hk = ("h", t0)
            B.dma(hres[:, :, t0:t0 + W], xs, r=[kx], w=[("hres", t0)])
            B.dma(hT[:, :, t0:t0 + W], XB[:, :, 0:W], r=["XB"], w=[("hT", t0)])
            if final and t0 >= 128:
                for s in range(W // 128):
                    OT = ph["OT"][s % 2]
                    ko = "OT%d" % (s % 2)
                    for half in range(2):
                        b = bank()
                        for cc in range(4):
                            c = half * 4 + cc
                            B.tr(ps[b][:, cc * 128:(cc + 1) * 128], X[:, c, s * 128:(s + 1) * 128], identF[:],
                                 r=[kx, "identF"], w=[("ps", b)])
                        B.evac(OT[:, half * 512:(half + 1) * 512], ps[b][:], r=[("ps", b)], w=[ko])
                    tok0 = t0 - 128 + s * 128
                    B.dma(out[tok0:tok0 + 128, :], OT[:], r=[ko], w=[("out", tok0)])

        def phase_embed():
            with ExitStack() as ph:
                xt = [sb(ph, "e_xt%d" % i, [128, 4, D], F32) for i in range(2)]
                X = [sb(ph, "e_X%d" % i, [128, 8, 512], F32) for i in range(2)]
                SQ = sb(ph, "e_SQ", [128, 8, 512], F32)
                R = sb(ph, "e_R", [128, 512], F32)
                XB = sb(ph, "e_XB", [128, 8, 512], BF16)
                for ti, (t0, W) in enumerate(TILES):
                    a = xt[ti % 2]
                    ka = "e_xt%d" % (ti % 2)
                    kx = "e_X%d" % (ti % 2)
                    nsub = W // 128
                    if ti == 0:
                        B.op("pool", lambda e: e.memset(a[:, 0, :], 0.0), w=[ka])
                        B.dma(a[112:128, 0, :], meta, w=[ka])
                    else:
                        B.dma(a[:], x[t0 - 128:t0 - 128 + 512, :].rearrange("(s p) d -> p s d", p=128), w=[ka])
                    for c in range(8):
                        b = bank()
                        for s in range(nsub):
                            B.tr(ps[b][:, s * 128:(s + 1) * 128], a[:, s, c * 128:(c + 1) * 128], identF[:],
                                 r=[ka, "identF"], w=[("ps", b)])
                        B.evac(X[ti % 2][:, c, 0:W], ps[b][:, 0:W], r=[("ps", b)], w=[kx])
                    ln_f(None, X[ti % 2], kx, SQ[:], ["SQ"], R, XB, W, 0, t0, False)
                B.barrier()

        def phase_A(l):
            with ExitStack() as ph:
                WA = sb(ph, "a_WA", [128, 8, NA], BF16)
                WUQ = sb(ph, "a_WUQ", [128, 3, 1536], BF16)
                WUKV = sb(ph, "a_WUKV", [128, 2, 1024], BF16)
                for c in range(8):
                    B.dma(WA[:, c, :], wa[l, c * 128:(c + 1) * 128, :], w=["WA"], q="pool")
                B.dma(WUQ[:], wuq[l].rearrange("(c p) n -> p c n", p=128), w=["WUQ"], q="pool")
                B.dma(WUKV[:], wukv[l].rearrange("(c p) n -> p c n", p=128), w=["WUKV"], q="pool")
                QDEC = sb(ph, "a_QDEC", [128, 2, 128], F32)
                KDEC = sb(ph, "a_KDEC", [128, 256], F32)
                B.dma(QDEC[:], c_qdec, w=["QDEC"])
                B.dma(KDEC[:], c_kdecT, w=["KDEC"])
                HT = [sb(ph, "a_HT%d" % i, [128, 8, 512], BF16) for i in range(2)]
                RT = [sb(ph, "a_RT%d" % i, [128, 4, 512], F32) for i in range(2)]
                MT = [sb(ph, "a_MT%d" % i, [96, 2, 512], F32) for i in range(2)]
                MK = [sb(ph, "a_MK%d" % i, [32, 2, 512], F32) for i in range(2)]
                CQ = sb(ph, "a_CQ", [128, 3, 512], F32)
                CKV = sb(ph, "a_CKV", [128, 2, 512], F32)
                SQ = sb(ph, "a_SQ", [128, 3, 512], F32)
                R = sb(ph, "a_R", [128, 512], F32)
                CQN = sb(ph, "a_CQN", [128, 3, 512], BF16)
                CKVN = sb(ph, "a_CKVN", [128, 2, 512], BF16)
                T1 = [sb(ph, "a_T1%d" % i, [128, 512], F32) for i in range(3)]
                T2 = [sb(ph, "a_T2%d" % i, [128, 512], F32) for i in range(3)]
                STG = [sb(ph, "a_STG%d" % i, [128, 512], BF16) for i in range(6)]
                RKB = [sb(ph, "a_RKB%d" % i, [128, 512], BF16) for i in range(2)]
                GS = [sb(ph, "a_GS0", [128, 4, 512], F32)] * 2
                KST = [sb(ph, "a_KST%d" % i, [128, 256], BF16) for i in range(2)]
                cn = {"stg": 0, "t": 0, "kst": 0}

                def stage():
                    i = cn["stg"] % 6
                    cn["stg"] += 1
                    return STG[i], "STG%d" % i

                def tt():
                    i = cn["t"] % 3
                    cn["t"] += 1
                    return T1[i], "T1%d" % i, T2[i], "T2%d" % i

                def load_tile(ti):
                    t0, W = TILES[ti]
                    p = ti % 2
                    B.dma(HT[p][:, :, 0:W], hT[:, :, t0:t0 + W], r=[("hT", t0)], w=["HT%d" % p])
                    B.dma(RT[p][:, :, 0:W], c_ropeR[:, :, t0:t0 + W], w=["RT%d" % p])
                    B.dma(MT[p][:, :, 0:W], c_ropeM[:, :, t0:t0 + W], w=["MT%d" % p])
                    B.dma(MK[p][:, :, 0:W], c_ropeM[64:96, :, t0:t0 + W], w=["MK%d" % p])

                load_tile(0)
                for ti, (t0, W) in enumerate(TILES):
                    p = ti % 2
                    if ti + 1 < len(TILES):
                        load_tile(ti + 1)
                    H = HT[p]
                    kh = "HT%d" % p
                    nsub = W // 128
                    blk0 = t0 // 128

                    def fgroup(col0, M, Wm=WA, kw="WA", rhs=None, krhs=None, nchunk=8):
                        b = bank()
                        for c in range(nchunk):
                            rr = H[:, c, 0:W] if rhs is None else rhs[:, c, 0:W]
                            B.mm(ps[b][0:M, 0:W], Wm[:, c, col0:col0 + M], rr, c == 0, c == nchunk - 1,
                                 r=[kw, kh if krhs is None else krhs], w=[("ps", b)])
                        return b

                    for m in range(4):
                        b = fgroup(O_SBQ + m * 128, 128)
                        S_, ks = stage()
                        B.evac(S_[:, 0:W], ps[b][:, 0:W], r=[("ps", b)], w=[ks], scale=0.125)
                        B.dma(sbq[:, m, t0:t0 + W], S_[:, 0:W], r=[ks], w=[("sbq", ti)])
                    for m in range(4):
                        b = fgroup(O_SBK + m * 128, 128)
                        S_, ks = stage()
                        B.evac(S_[:, 0:W], ps[b][:, 0:W], r=[("ps", b)], w=[ks])
                        B.dma(sbk[:, m, t0:t0 + W], S_[:, 0:W], r=[ks], w=[("sbk", ti)])
                    for (col0, dst, nm) in ((O_SBV, sbv, "sbv"), (O_RV, rv, "rv")):
                        for s in range(nsub):
                            b = bank()
                            for c in range(8):
                                B.mm(ps[b][:, :], H[:, c, s * 128:(s + 1) * 128], WA[:, c, col0:col0 + 512],
                                     c == 0, c == 7, r=["WA", kh], w=[("ps", b)])
                            S_, ks = stage()
                            B.evac(S_[:, :], ps[b][:, :], r=[("ps", b)], w=[ks])
                            B.dma(dst[:, blk0 + s, :], S_[:, :], r=[ks], w=[(nm, ti)])
                    for m in range(3):
                        b = fgroup(O_CQ + m * 128, 128)
                        B.evac(CQ[:, m, 0:W], ps[b][:, 0:W], r=[("ps", b)], w=["CQ"])
                    for m in range(2):
                        b = fgroup(O_CKV + m * 128, 128)
                        B.evac(CKV[:, m, 0:W], ps[b][:, 0:W], r=[("ps", b)], w=["CKV"])
                    ba = fgroup(O_KR, 32)
                    bb = fgroup(O_KRR, 32)
                    t1, k1, t2, k2 = tt()
                    B.op("dve", lambda e: e.tensor_tensor(out=t1[0:32, 0:W], in0=ps[ba][0:32, 0:W], in1=MK[p][:, 0, 0:W],
                                                          op=ALU.mult), r=[("ps", ba), "MK%d" % p], w=[k1])
                    B.op("dve", lambda e: e.tensor_tensor(out=t2[0:32, 0:W], in0=ps[bb][0:32, 0:W], in1=MK[p][:, 1, 0:W],
                                                          op=ALU.mult), r=[("ps", bb), "MK%d" % p], w=[k2])
                    S_, ks = stage()
                    B.op("pool", lambda e: e.tensor_tensor(out=S_[0:32, 0:W], in0=t1[0:32, 0:W], in1=t2[0:32, 0:W],
                                                           op=ALU.add), r=[k1, k2], w=[ks])
                    B.dma(mk[64:96, :, t0:t0 + W], S_[0:32, 0:W].unsqueeze(1).broadcast_to([32, 8, W]),
                          r=[ks], w=[("mk", ti)])
                    for pr in range(2):
                        ba = fgroup(O_RQ + pr * 128, 128)
                        bb = fgroup(O_RQR + pr * 128, 128)
                        t1, k1, t2, k2 = tt()
                        B.op("dve", lambda e: e.tensor_tensor(out=t1[:, 0:W], in0=ps[ba][:, 0:W], in1=RT[p][:, 0, 0:W],
                                                              op=ALU.mult), r=[("ps", ba), "RT%d" % p], w=[k1])
                        B.op("dve", lambda e: e.tensor_tensor(out=t2[:, 0:W], in0=ps[bb][:, 0:W], in1=RT[p][:, 1, 0:W],
                                                              op=ALU.mult), r=[("ps", bb), "RT%d" % p], w=[k2])
                        B.op("pool", lambda e: e.tensor_tensor(out=t1[:, 0:W], in0=t1[:, 0:W], in1=t2[:, 0:W],
                                                               op=ALU.add), r=[k1, k2], w=[k1])
                        S_, ks = stage()
                        B.op("act", lambda e: e.copy(out=S_[:, 0:W], in_=t1[:, 0:W]), r=[k1], w=[ks])
                        B.dma(rq[:, pr, t0:t0 + W], S_[:, 0:W], r=[ks], w=[("rq", ti)])
                        S2, ks2 = stage()
                        B.op("pool", lambda e: e.tensor_tensor(
                            out=S2[:, 0:W].rearrange("p (s n) -> p s n", n=128),
                            in0=t1[:, 0:W].rearrange("p (s n) -> p s n", n=128),
                            in1=QDEC[:, pr, :].unsqueeze(1).broadcast_to([128, nsub, 128]), op=ALU.mult),
                            r=[k1, "QDEC"], w=[ks2])
                        B.dma(rqd[:, pr, t0:t0 + W], S2[:, 0:W], r=[ks2], w=[("rqd", ti)])
                    for pr in range(2):
                        ba = fgroup(O_RK + pr * 128, 128)
                        bb = fgroup(O_RKR + pr * 128, 128)
                        t1, k1, t2, k2 = tt()
                        B.op("dve", lambda e: e.tensor_tensor(out=t1[:, 0:W], in0=ps[ba][:, 0:W], in1=RT[p][:, 2, 0:W],
                                                              op=ALU.mult), r=[("ps", ba), "RT%d" % p], w=[k1])
                        B.op("dve", lambda e: e.tensor_tensor(out=t2[:, 0:W], in0=ps[bb][:, 0:W], in1=RT[p][:, 3, 0:W],
                                                              op=ALU.mult), r=[("ps", bb), "RT%d" % p], w=[k2])
                        kb_ = "RKB%d" % pr
                        B.op("pool", lambda e: e.tensor_tensor(out=RKB[pr][:, 0:W], in0=t1[:, 0:W], in1=t2[:, 0:W],
                                                               op=ALU.add), r=[k1, k2], w=[kb_])
                        if ti == 0:
                            B.op("pool", lambda e: e.memset(RKB[pr][:, 0:112], 0.0), r=[kb_], w=[kb_])
                        B.dma(rk[:, pr, t0:t0 + W], RKB[pr][:, 0:W], r=[kb_], w=[("rk", ti)])
                    for s in range(nsub):
                        b = bank()
                        pb = ps[b][:].bitcast(BF16)
                        for pr in range(2):
                            B.tr(pb[:, pr * 128:(pr + 1) * 128], RKB[pr][:, s * 128:(s + 1) * 128], identB[:],
                                 r=["RKB%d" % pr, "identB"], w=[("ps", b)])
                        i = cn["kst"] % 2
                        cn["kst"] += 1
                        B.op("dve", lambda e: e.tensor_tensor(out=KST[i][:], in0=pb[:, 0:256], in1=KDEC[:], op=ALU.mult),
                             r=[("ps", b), "KDEC"], w=["KST%d" % i])
                        B.dma(rkd[:, blk0 + s, :], KST[i][:], r=["KST%d" % i], w=[("rkd", ti)])
                    G = GS[p]
                    for m in range(4):
                        b = fgroup(O_RG + m * 128, 128)
                        B.op("act", lambda e: e.activation(out=G[:, m, 0:W], in_=ps[b][:, 0:W], func=AF.Silu),
                             r=[("ps", b)], w=["GS"])
                    B.dma(rg[:, :, t0:t0 + W], G[:, :, 0:W], r=["GS"], w=[("rg", ti)])
                    for (src, ksrc, nch, dstn, kd, gam, inv) in ((CQ, "CQ", 3, CQN, "CQN", qns, 1.0 / 384),
                                                                   (CKV, "CKV", 2, CKVN, "CKVN", kvns, 1.0 / 256)):
                        B.op("act", lambda e: e.activation(out=SQ[:, 0:nch, 0:W], in_=src[:, :, 0:W], func=AF.Square),
                             r=[ksrc], w=["SQ"])
                        b = bank()
                        for m in range(nch):
                            B.mm(ps[b][:, 0:W], onesF[:], SQ[:, m, 0:W], m == 0, m == nch - 1, r=["SQ", "onesF"],
                                 w=[("ps", b)])
                        B.op("act", lambda e: e.activation(out=R[:, 0:W], in_=ps[b][:, 0:W], func=AF.Ln, scale=inv,
                                                           bias=LN_EPS), r=[("ps", b)], w=["R"])
                        B.op("act", lambda e: e.activation(out=R[:, 0:W], in_=R[:, 0:W], func=AF.Exp, scale=-0.5),
                             r=["R"], w=["R"])
                        for m in range(nch):
                            B.op("dve", lambda e, m=m: e.scalar_tensor_tensor(
                                out=dstn[:, m, 0:W], in0=src[:, m, 0:W], scalar=gam[:, l, m:m + 1], in1=R[:, 0:W],
                                op0=ALU.mult, op1=ALU.mult), r=[ksrc, "R", "qns", "kvns"], w=[kd])
                    for h in range(8):
                        ba = fgroup((2 * h) * 96, 96, Wm=WUQ, kw="WUQ", rhs=CQN, krhs="CQN", nchunk=3)
                        bb = fgroup((2 * h + 1) * 96, 96, Wm=WUQ, kw="WUQ", rhs=CQN, krhs="CQN", nchunk=3)
                        t1, k1, t2, k2 = tt()
                        B.op("dve", lambda e: e.tensor_tensor(out=t1[0:96, 0:W], in0=ps[ba][0:96, 0:W], in1=MT[p][:, 0, 0:W],
                                                              op=ALU.mult), r=[("ps", ba), "MT%d" % p], w=[k1])
                        B.op("dve", lambda e: e.tensor_tensor(out=t2[0:96, 0:W], in0=ps[bb][0:96, 0:W], in1=MT[p][:, 1, 0:W],
                                                              op=ALU.mult), r=[("ps", bb), "MT%d" % p], w=[k2])
                        S_, ks = stage()
                        B.op("pool", lambda e: e.tensor_tensor(out=S_[0:96, 0:W], in0=t1[0:96, 0:W], in1=t2[0:96, 0:W],
                                                               op=ALU.add), r=[k1, k2], w=[ks])
                        B.dma(mq[0:96, h, t0:t0 + W], S_[0:96, 0:W], r=[ks], w=[("mq", ti)])
                    for m in range(4):
                        b = fgroup(m * 128, 128, Wm=WUKV, kw="WUKV", rhs=CKVN, krhs="CKVN", nchunk=2)
                        S_, ks = stage()
                        B.evac(S_[:, 0:W], ps[b][:, 0:W], r=[("ps", b)], w=[ks])
                        B.dma(mk[0:64, 2 * m, t0:t0 + W], S_[0:64, 0:W], r=[ks], w=[("mk", ti)])
                        B.dma(mk[0:64, 2 * m + 1, t0:t0 + W], S_[64:128, 0:W], r=[ks], w=[("mk", ti)])
                    for s in range(nsub):
                        b = bank()
                        for c in range(2):
                            B.mm(ps[b][:, :], CKVN[:, c, s * 128:(s + 1) * 128], WUKV[:, c, 512:1024], c == 0, c == 1,
                                 r=["WUKV", "CKVN"], w=[("ps", b)])
                        S_, ks = stage()
                        B.evac(S_[:, :], ps[b][:, :], r=[("ps", b)], w=[ks])
                        B.dma(mv[:, blk0 + s, :], S_[:, :], r=[ks], w=[("mv", ti)])
                B.barrier()

        def phase_B1():
            with ExitStack() as ph:
                UIb = sb(ph, "b_UIb", [128, 128], BF16)
                ONb = sb(ph, "b_ONb", [128, 128], BF16)
                MS = sb(ph, "b_MS", [128, 6, 512], F32)
                B.dma(UIb[:], c_nuincl, w=["UIb"], q="pool")
                B.op("pool", lambda e: e.memset(ONb[:], -1.0), w=["ONb"])
                B.dma(MS[:], c_msb, w=["MS"])
                QP = [sb(ph, "b_QP%d" % i, [128, T], BF16) for i in range(2)]
                KA = [sb(ph, "b_KA%d" % i, [128, T], BF16) for i in range(2)]
                KB = [sb(ph, "b_KB%d" % i, [128, T], BF16) for i in range(2)]
                VP = [sb(ph, "b_VP%d" % i, [128, NBLK, 128], BF16) for i in range(2)]
                for i in range(2):
                    B.op("pool", lambda e: e.memset(KA[i][64:128, :], 0.0), w=["KA%d" % i])
                    B.op("pool", lambda e: e.memset(KB[i][0:64, :], 0.0), w=["KB%d" % i])
                NR = 9
                EB = [sb(ph, "b_E%d" % i, [128, 512], F32) for i in range(NR)]
                LK = [sb(ph, "b_LK%d" % i, [128, 512], F32) for i in range(NR)]
                HI = [sb(ph, "b_HI%d" % i, [128, 512], BF16) for i in range(NR)]
                LO = [sb(ph, "b_LO%d" % i, [128, 512], BF16) for i in range(NR)]
                SS = [sb(ph, "b_S%d" % i, [128, 512], F32) for i in range(NR)]
                WW = [sb(ph, "b_W%d" % i, [128, 512], BF16) for i in range(NR)]
                CC = [sb(ph, "b_C%d" % i, [128, 512], F32) for i in range(2)]
                OS = [sb(ph, "b_OS%d" % i, [128, 512], BF16) for i in range(2)]
                allt = list(range(9))

                def load_pair(m):
                    p = m % 2
                    B.dma(QP[p][:], sbq[:, m, :], r=[("sbq", i) for i in allt], w=["QP%d" % p])
                    B.dma(KA[p][0:64, :], sbk[0:64, m, :], r=[("sbk", i) for i in allt], w=["KA%d" % p])
                    B.dma(KB[p][64:128, :], sbk[64:128, m, :], r=[("sbk", i) for i in allt], w=["KB%d" % p])
                    B.dma(VP[p][:], sbv[:, :, m * 128:(m + 1) * 128], r=[("sbv", i) for i in allt], w=["VP%d" % p])

                jobs = []
                for h in range(8):
                    for ti, (t0, W) in enumerate(TILES):
                        nkb = (t0 + W) // 128
                        for kb in reversed(range(nkb)):
                            jobs.append((h, ti, kb, kb == nkb - 1, kb == 0))
                njobs = len(jobs)

                def kop(h):
                    p = (h // 2) % 2
                    return (KA[p], "KA%d" % p) if h % 2 == 0 else (KB[p], "KB%d" % p)

                def mask_idx(ti, kb, t0):
                    c = kb - t0 // 128
                    if ti == 0:
                        return 4
                    if c >= 0:
                        return c
                    if kb == 0:
                        return 5
                    return None

                def stage1(ji):
                    h, ti, kb, first, last = jobs[ji]
                    t0, W = TILES[ti]
                    p = (h // 2) % 2
                    Kx, kk = kop(h)
                    i = ji % NR
                    zb = ji % 2
                    B.mm(ps[zb][:, 0:W], Kx[:, kb * 128:(kb + 1) * 128], QP[p][:, t0:t0 + W], True, True,
                         r=[kk, "QP%d" % p], w=[("ps", zb)])
                    B.op("act", lambda e: e.activation(out=EB[i][:, 0:W], in_=ps[zb][:, 0:W], func=AF.Exp),
                         r=[("ps", zb)], w=["E%d" % i])
                    B.op("act", lambda e: e.activation(out=LK[i][:, 0:W], in_=EB[i][:, 0:W], func=AF.Ln, bias=1.0),
                         r=["E%d" % i], w=["LK%d" % i])
                    mi = mask_idx(ti, kb, t0)
                    if mi is not None:
                        B.op("pool", lambda e: e.tensor_tensor(out=LK[i][:, 0:W], in0=LK[i][:, 0:W], in1=MS[:, mi, 0:W],
                                                               op=ALU.mult), r=["LK%d" % i, "MS"], w=["LK%d" % i])
                    B.op("dve", lambda e: e.tensor_copy(out=HI[i][:, 0:W], in_=LK[i][:, 0:W]),
                         r=["LK%d" % i], w=["HI%d" % i])
                    B.op("pool", lambda e: e.tensor_tensor(out=LO[i][:, 0:W], in0=LK[i][:, 0:W], in1=HI[i][:, 0:W],
                                                           op=ALU.subtract),
                         r=["LK%d" % i, "HI%d" % i], w=["LO%d" % i])

                def stage2(ji):
                    h, ti, kb, first, last = jobs[ji]
                    t0, W = TILES[ti]
                    p = (h // 2) % 2
                    Kx, kk = kop(h)
                    i = ji % NR
                    eb = 2 + ji % 2
                    tb = 4 + ji % 2
                    cp = (h * 9 + ti) % 2
                    Cc, kc = CC[cp], "C%d" % cp
                    B.mm(ps[eb][:, 0:W], UIb[:], HI[i][:, 0:W], True, False, r=["UIb", "HI%d" % i], w=[("ps", eb)])
                    B.mm(ps[eb][:, 0:W], UIb[:], LO[i][:, 0:W], False, False, r=["UIb", "LO%d" % i], w=[("ps", eb)])
                    B.mm(ps[eb][:, 0:W], Kx[:, kb * 128:(kb + 1) * 128], QP[p][:, t0:t0 + W], False, True,
                         r=[kk, "QP%d" % p], w=[("ps", eb)])
                    if not last:
                        B.mm(ps[tb][:, 0:W], ONb[:], HI[i][:, 0:W], True, False, r=["ONb", "HI%d" % i], w=[("ps", tb)])
                        B.mm(ps[tb][:, 0:W], ONb[:], LO[i][:, 0:W], False, True, r=["ONb", "LO%d" % i], w=[("ps", tb)])
                    if first:
                        B.op("act", lambda e: e.activation(out=WW[i][:, 0:W], in_=ps[eb][:, 0:W], func=AF.Exp),
                             r=[("ps", eb)], w=["W%d" % i])
                        if not last:
                            B.op("dve", lambda e: e.tensor_copy(out=Cc[:, 0:W], in_=ps[tb][:, 0:W]),
                                 r=[("ps", tb)], w=[kc])
                    else:
                        B.op("dve", lambda e: e.tensor_tensor(out=SS[i][:, 0:W], in0=Cc[:, 0:W], in1=ps[eb][:, 0:W],
                                                              op=ALU.add), r=[("ps", eb), kc], w=["S%d" % i])
                        B.op("act", lambda e: e.activation(out=WW[i][:, 0:W], in_=SS[i][:, 0:W], func=AF.Exp),
                             r=["S%d" % i], w=["W%d" % i])
                        if not last:
                            B.op("dve", lambda e: e.tensor_tensor(out=Cc[:, 0:W], in0=Cc[:, 0:W], in1=ps[tb][:, 0:W],
                                                                  op=ALU.add), r=[("ps", tb), kc], w=[kc])
                    mi = mask_idx(ti, kb, t0)
                    if mi is not None:
                        B.op("pool", lambda e: e.tensor_tensor(out=WW[i][:, 0:W], in0=WW[i][:, 0:W], in1=MS[:, mi, 0:W],
                                                               op=ALU.mult), r=["W%d" % i, "MS"], w=["W%d" % i])

                def stage3(ji):
                    h, ti, kb, first, last = jobs[ji]
                    t0, W = TILES[ti]
                    p = (h // 2) % 2
                    i = ji % NR
                    ob = 6 + (h * 9 + ti) % 2
                    if ti == 0 and first and h % 2 == 0 and h + 2 < 8:
                        load_pair(h // 2 + 1)
                    B.mm(ps[ob][:, 0:W], VP[p][:, kb, :], WW[i][:, 0:W], first, last,
                         r=["VP%d" % p, "W%d" % i], w=[("ps", ob)])
                    if last:
                        oi = (h * 9 + ti) % 2
                        r0 = (h % 2) * 64
                        B.evac(OS[oi][r0:r0 + 64, 0:W], ps[ob][r0:r0 + 64, 0:W], r=[("ps", ob)], w=["OS%d" % oi])
                        B.dma(mix[r0:r0 + 64, h // 2, t0:t0 + W], OS[oi][r0:r0 + 64, 0:W], r=["OS%d" % oi],
                              w=[("mix", ti)])

                load_pair(0)
                L2, L3 = 3, 6
                for step in range(njobs + L3):
                    if step < njobs:
                        stage1(step)
                    if 0 <= step - L2 < njobs:
                        stage2(step - L2)
                    if 0 <= step - L3 < njobs:
                        stage3(step - L3)
                B.barrier()

        def phase_B2():
            with ExitStack() as ph:
                MM = sb(ph, "m_MM", [128, 6, 512], F32)
                ONb = sb(ph, "m_ONb", [128, 128], BF16)
                B.dma(MM[:], c_mmla, w=["MM"])
                B.op("pool", lambda e: e.memset(ONb[:], 1.0), w=["mONb"])
                QT = [sb(ph, "m_QT%d" % i, [128, T], BF16) for i in range(2)]
                KT = [sb(ph, "m_KT%d" % i, [128, T], BF16) for i in range(2)]
                VP = [sb(ph, "m_VP%d" % i, [128, NBLK, 128], BF16) for i in range(2)]
                NR = 5
                PP = [sb(ph, "m_P%d" % i, [128, 512], BF16) for i in range(NR)]
                RC = [sb(ph, "m_RC%d" % i, [128, 512], F32) for i in range(2)]
                OS = [sb(ph, "m_OS%d" % i, [128, 512], BF16) for i in range(2)]
                allt = list(range(9))

                def load_head(h):
                    p = h % 2
                    B.dma(QT[p][:], mq[:, h, :], r=[("mq", i) for i in allt] + ["mqz"], w=["mQT%d" % p])
                    B.dma(KT[p][:], mk[:, h, :], r=[("mk", i) for i in allt] + ["mkz"], w=["mKT%d" % p])
                    if h % 2 == 0:
                        pp = (h // 2) % 2
                        B.dma(VP[pp][:], mv[:, :, (h // 2) * 128:(h // 2 + 1) * 128], r=[("mv", i) for i in allt],
                              w=["mVP%d" % pp])

                jobs = []
                for h in range(8):
                    for ti, (t0, W) in enumerate(TILES):
                        nkb = (t0 + W) // 128
                        for kb in reversed(range(nkb)):
                            jobs.append((h, ti, kb, kb == nkb - 1, kb == 0))
                njobs = len(jobs)

                def stage1(ji):
                    h, ti, kb, first, last = jobs[ji]
                    t0, W = TILES[ti]
                    p = h % 2
                    i = ji % NR
                    sbk_ = ji % 3
                    B.mm(ps[sbk_][:, 0:W], KT[p][:, kb * 128:(kb + 1) * 128], QT[p][:, t0:t0 + W], True, True,
                         r=["mKT%d" % p, "mQT%d" % p], w=[("ps", sbk_)])
                    B.op("act", lambda e: e.activation(out=PP[i][:, 0:W], in_=ps[sbk_][:, 0:W], func=AF.Exp,
                                                       scale=MLA_SCALE), r=[("ps", sbk_)], w=["P%d" % i])
                    c = kb - t0 // 128
                    mi = 4 if ti == 0 else (c if c >= 0 else (5 if kb == 0 else None))
                    if mi is not None:
                        B.op("dve", lambda e: e.tensor_tensor(out=PP[i][:, 0:W], in0=PP[i][:, 0:W], in1=MM[:, mi, 0:W],
                                                              op=ALU.mult), r=["P%d" % i, "MM"], w=["P%d" % i])

                def stage2(ji):
                    h, ti, kb, first, last = jobs[ji]
                    t0, W = TILES[ti]
                    pp = (h // 2) % 2
                    i = ji % NR
                    par = (h * 9 + ti) % 2
                    ob = 4 + par
                    db = 6 + par
                    if ti == 0 and first and h + 1 < 8:
                        load_head(h + 1)
                    B.mm(ps[ob][:, 0:W], VP[pp][:, kb, :], PP[i][:, 0:W], first, last,
                         r=["mVP%d" % pp, "P%d" % i], w=[("ps", ob)])
                    B.mm(ps[db][:, 0:W], ONb[:], PP[i][:, 0:W], first, last,
                         r=["mONb", "P%d" % i], w=[("ps", db)])
                    if last:
                        r0 = (h % 2) * 64
                        B.op("dve", lambda e: e.reciprocal(out=RC[par][r0:r0 + 64, 0:W], in_=ps[db][r0:r0 + 64, 0:W]),
                             r=[("ps", db)], w=["RC%d" % par])
                        B.op("dve", lambda e: e.tensor_tensor(out=OS[par][r0:r0 + 64, 0:W], in0=ps[ob][r0:r0 + 64, 0:W],
                                                              in1=RC[par][r0:r0 + 64, 0:W], op=ALU.mult),
                             r=[("ps", ob), "RC%d" % par], w=["mOS%d" % par])
                        B.dma(mix[r0:r0 + 64, 4 + h // 2, t0:t0 + W], OS[par][r0:r0 + 64, 0:W], r=["mOS%d" % par],
                              w=[("mix", ti)])

                load_head(0)
                L2 = 2
                for step in range(njobs + L2):
                    if step < njobs:
                        stage1(step)
                    if 0 <= step - L2 < njobs:
                        stage2(step - L2)
                B.barrier()

        def phase_B3():
            with ExitStack() as ph:
                RQ = sb(ph, "r_RQ", [64, 4, T], BF16)
                RQD = sb(ph, "r_RQD", [64, 4, T], BF16)
                RK = sb(ph, "r_RK", [64, 4, T], BF16)
                RV = sb(ph, "r_RV", [128, NBLK, 512], BF16)
                RKD = sb(ph, "r_RKD", [128, NBLK, 256], BF16)
                DT = sb(ph, "r_DT", [128, 512], F32)
                allt = list(range(9))
                for h in range(4):
                    r0 = (h % 2) * 64
                    B.dma(RQ[:, h, :], rq[r0:r0 + 64, h // 2, :], r=[("rq", i) for i in allt], w=["RQ"])
                    B.dma(RQD[:, h, :], rqd[r0:r0 + 64, h // 2, :], r=[("rqd", i) for i in allt], w=["RQD"])
                    B.dma(RK[:, h, :], rk[r0:r0 + 64, h // 2, :], r=[("rk", i) for i in allt], w=["RK"])
                B.dma(RV[:], rv, r=[("rv", i) for i in allt], w=["RV"])
                B.dma(RKD[:], rkd, r=[("rkd", i) for i in allt], w=["RKD"])
                B.dma(DT[:], c_dt, w=["DT"])
                ST = sb(ph, "r_ST", [64, 4, 128], F32)
                STB = sb(ph, "r_STB", [64, 4, 128], BF16)
                B.op("pool", lambda e: e.memset(ST[:], 0.0), w=["ST"])
                B.op("pool", lambda e: e.memset(STB[:], 0.0), w=["STB"])
                IND = [sb(ph, "r_IND%d" % i, [128, 512], BF16) for i in range(2)]
                YS = [sb(ph, "r_YS%d" % i, [128, 512], F32) for i in range(2)]
                SQ = [sb(ph, "r_SQ%d" % i, [128, 512], F32) for i in range(2)]
                R = [sb(ph, "r_R%d" % i, [128, 512], F32) for i in range(2)]
                GG = [sb(ph, "r_GG%d" % i, [128, 4, 128], F32) for i in range(2)]
                OC = [sb(ph, "r_OC%d" % i, [128, 4, 128], BF16) for i in range(2)]
                cdec = [float(np.float64(g) ** 128.0) for g in RET_GAMMA]
                for n in range(NBLK):
                    p = n % 2
                    tsl = slice(n * 128, (n + 1) * 128)
                    ti = 0 if n == 0 else 1 + (n - 1) // 4
                    B.dma(GG[p][:], rg[:, :, tsl], r=[("rg", ti)], w=["GG%d" % p])
                    bi = bank()
                    for h in range(4):
                        B.mm(ps[bi][:, h * 128:(h + 1) * 128], RK[:, h, tsl], RQ[:, h, tsl],
                             True, True, r=["RK", "RQ"], w=[("ps", bi)])
                    B.op("dve", lambda e: e.tensor_tensor(out=IND[p][:], in0=ps[bi][:], in1=DT[:], op=ALU.mult),
                         r=[("ps", bi), "DT"], w=["IND%d" % p])
                    by = bank()
                    for h in range(4):
                        B.mm(ps[by][:, h * 128:(h + 1) * 128], RV[:, n, h * 128:(h + 1) * 128],
                             IND[p][:, h * 128:(h + 1) * 128], True, False, r=["RV", "IND%d" % p], w=[("ps", by)])
                        B.mm(ps[by][:, h * 128:(h + 1) * 128], STB[:, h, :], RQD[:, h, tsl],
                             False, True, r=["STB", "RQD"], w=[("ps", by)])
                    if n + 1 < NBLK:
                        bs = bank()
                        for h in range(4):
                            B.mm(ps[bs][0:64, h * 128:(h + 1) * 128], RKD[:, n, h * 64:(h + 1) * 64],
                                 RV[:, n, h * 128:(h + 1) * 128], True, True, r=["RKD", "RV"], w=[("ps", bs)])
                        for h in range(4):
                            B.op("dve", lambda e: e.scalar_tensor_tensor(
                                out=ST[:, h, :], in0=ST[:, h, :], scalar=cdec[h],
                                in1=ps[bs][0:64, h * 128:(h + 1) * 128], op0=ALU.mult, op1=ALU.add),
                                r=["ST", ("ps", bs)], w=["ST"])
                        B.op("pool", lambda e: e.tensor_copy(out=STB[:], in_=ST[:]), r=["ST"], w=["STB"])
                    B.op("act", lambda e: e.copy(out=YS[p][:], in_=ps[by][:]), r=[("ps", by)], w=["YS%d" % p])
                    bm = bank()
                    B.mm(ps[bm][:], onesF[:], YS[p][:], True, True, r=["onesF", "YS%d" % p], w=[("ps", bm)])
                    B.op("dve", lambda e: e.scalar_tensor_tensor(out=YS[p][:], in0=ps[bm][:], scalar=-1.0 / 128,
                                                                 in1=YS[p][:], op0=ALU.mult, op1=ALU.add),
                         r=[("ps", bm), "YS%d" % p], w=["YS%d" % p])
                    B.op("act", lambda e: e.activation(out=SQ[p][:], in_=YS[p][:], func=AF.Square),
                         r=["YS%d" % p], w=["rSQ%d" % p])
                    bv = bank()
                    B.mm(ps[bv][:], onesF[:], SQ[p][:], True, True, r=["onesF", "rSQ%d" % p], w=[("ps", bv)])
                    B.op("act", lambda e: e.activation(out=R[p][:], in_=ps[bv][:], func=AF.Ln, scale=1.0 / 128,
                                                       bias=LN_EPS), r=[("ps", bv)], w=["rR%d" % p])
                    B.op("act", lambda e: e.activation(out=R[p][:], in_=R[p][:], func=AF.Exp, scale=-0.5),
                         r=["rR%d" % p], w=["rR%d" % p])
                    B.op("dve", lambda e: e.tensor_tensor(out=YS[p][:], in0=YS[p][:], in1=R[p][:], op=ALU.mult),
                         r=["YS%d" % p, "rR%d" % p], w=["YS%d" % p])
                    B.op("pool", lambda e: e.tensor_tensor(out=OC[p][:], in0=YS[p][:].rearrange("p (h n) -> p h n", h=4),
                                                           in1=GG[p][:], op=ALU.mult),
                         r=["YS%d" % p, "GG%d" % p], w=["OC%d" % p])
                    B.dma(mix[:, 8:12, tsl], OC[p][:], r=["OC%d" % p], w=[("mix", ti)])
                B.barrier()

        def phase_C(l):
            with ExitStack() as ph:
                WO = sb(ph, "c_WO", [128, 12, D], BF16)
                B.dma(WO[:], wo[l].rearrange("(c p) n -> p c n", p=128), w=["WO"], q="pool")
                MX = [sb(ph, "c_MX%d" % i, [128, 12, 512], BF16) for i in range(2)]
                X = [sb(ph, "c_X%d" % i, [128, 8, 512], F32) for i in range(2)]
                SQ = sb(ph, "c_SQ", [128, 8, 512], F32)
                R = sb(ph, "c_R", [128, 512], F32)
                XB = sb(ph, "c_XB", [128, 8, 512], BF16)

                def load_tile(ti):
                    t0, W = TILES[ti]
                    p = ti % 2
                    B.dma(MX[p][:, :, 0:W], mix[:, :, t0:t0 + W], r=[("mix", ti)], w=["MX%d" % p])
                    B.dma(X[p][:, :, 0:W], hres[:, :, t0:t0 + W], r=[("hres", t0)], w=["cX%d" % p])

                load_tile(0)
                for ti, (t0, W) in enumerate(TILES):
                    p = ti % 2
                    if ti + 1 < len(TILES):
                        load_tile(ti + 1)
                    for dc in range(8):
                        b = bank()
                        for m in range(12):
                            B.mm(ps[b][:, 0:W], WO[:, m, dc * 128:(dc + 1) * 128], MX[p][:, m, 0:W], m == 0, m == 11,
                                 r=["WO", "MX%d" % p], w=[("ps", b)])
                        B.op("dve", lambda e, dc=dc, b=b: e.scalar_tensor_tensor(
                            out=X[p][:, dc, 0:W], in0=X[p][:, dc, 0:W], scalar=DN_ALPHA, in1=ps[b][:, 0:W],
                            op0=ALU.mult, op1=ALU.add), r=["cX%d" % p, ("ps", b)], w=["cX%d" % p])
                    ln_f(None, X[p], "cX%d" % p, SQ[:], ["SQ"], R, XB, W, 1 + 2 * l, t0, False)
                B.barrier()

        def phase_D(l, final):
            with ExitStack() as ph:
                W1 = sb(ph, "d_W1", [128, 8, 4096], BF16)
                W2 = sb(ph, "d_W2", [128, 32, D], BF16)
                for c in range(8):
                    B.dma(W1[:, c, :], w1[l, c * 128:(c + 1) * 128, :], w=["W1"], q="pool")
                for c in range(4):
                    B.dma(W2[:, c * 8:(c + 1) * 8, :], w2[l, c * 1024:(c + 1) * 1024, :].rearrange("(c p) n -> p c n", p=128),
                          w=["W2"], q="pool")
                HB = [sb(ph, "d_HB%d" % i, [128, 8, 512], BF16) for i in range(2)]
                X = sb(ph, "d_X", [128, 8, 512], F32)
                R = sb(ph, "d_R", [128, 512], F32)
                XB = sb(ph, "d_XB", [128, 8, 512], BF16)
                UT = sb(ph, "d_UT", [128, 16, 512], BF16)
                SQ = UT[:].rearrange("p a n -> p (a n)").bitcast(F32).rearrange("p (c n) -> p c n", c=8)
                kut = [("UT", fc) for fc in range(16)]
                RL = [sb(ph, "d_RL%d" % i, [128, 512], F32) for i in range(3)]
                phd = {}
                if final:
                    phd["OT"] = [sb(ph, "d_OT%d" % i, [128, D], F32) for i in range(2)]

                def load_tile(ti):
                    t0, W = TILES[ti]
                    B.dma(HB[ti % 2][:, :, 0:W], hT[:, :, t0:t0 + W], r=[("hT", t0)], w=["HB%d" % (ti % 2)])

                load_tile(0)
                for ti, (t0, W) in enumerate(TILES):
                    p = ti % 2
                    if ti + 1 < len(TILES):
                        load_tile(ti + 1)
                    B.dma(X[:, :, 0:W], hres[:, :, t0:t0 + W], r=[("hres", t0)], w=["dX"])
                    for half in range(2):
                        for f in range(16):
                            fc = half * 16 + f
                            b = bank()
                            for c in range(8):
                                B.mm(ps[b][:, 0:W], W1[:, c, fc * 128:(fc + 1) * 128], HB[p][:, c, 0:W], c == 0, c == 7,
                                     r=["W1", "HB%d" % p], w=[("ps", b)])
                            i = fc % 3
                            B.op("act", lambda e: e.activation(out=RL[i][:, 0:W], in_=ps[b][:, 0:W], func=AF.Relu),
                                 r=[("ps", b)], w=["RL%d" % i])
                            eng = "pool" if fc % 2 == 0 else "dve"
                            B.op(eng, lambda e: e.tensor_tensor(out=UT[:, f, 0:W], in0=RL[i][:, 0:W],
                                                                in1=RL[i][:, 0:W], op=ALU.mult),
                                 r=["RL%d" % i], w=[("UT", f)])
                        for dc in range(8):
                            b = bank()
                            for f in range(16):
                                fc = half * 16 + f
                                B.mm(ps[b][:, 0:W], W2[:, fc, dc * 128:(dc + 1) * 128], UT[:, f, 0:W], f == 0, f == 15,
                                     r=["W2", ("UT", f)], w=[("ps", b)])
                            B.op("dve", lambda e: e.scalar_tensor_tensor(
                                out=X[:, dc, 0:W], in0=X[:, dc, 0:W], scalar=(DN_ALPHA if half == 0 else 1.0),
                                in1=ps[b][:, 0:W], op0=ALU.mult, op1=ALU.add), r=["dX", ("ps", b)], w=["dX"])
                    ln_f(phd, X, "dX", SQ, kut, R, XB, W, 2 + 2 * l, t0, final)
                B.barrier()

        import os as _os
        PH = _os.environ.get("KPH", "E,A,B1,B2,B3,C,D").split(",")
        if "E" in PH:
            phase_embed()
        for l in range(NL):
            if "A" in PH:
                phase_A(l)
            if "B1" in PH:
                phase_B1()
            if "B2" in PH:
                phase_B2()
            if "B3" in PH:
                phase_B3()
            if "C" in PH:
                phase_C(l)
            if "D" in PH:
                phase_D(l, l == NL - 1)
        B.finish()
        print("instr counts", B.cnt, "sems", len(B.sems), flush=True)
    return nc


def _consts():
    c = {}
    c["c_ident"] = np.eye(128, dtype=np.float32)
    j = np.arange(128)
    c["c_nuincl"] = -(j[:, None] >= j[None, :]).astype(np.float32)
    pos = (np.arange(T) - 112).astype(np.float32)

    def tables(half):
        inv = (np.float32(10000.0) ** (-np.arange(half, dtype=np.float32) / np.float32(half))).astype(np.float32)
        ang = (pos[None, :] * inv[:, None]).astype(np.float32)
        return np.cos(ang).astype(np.float32), np.sin(ang).astype(np.float32)

    c32, s32 = tables(32)
    C = np.concatenate([c32, c32, c32, c32], 0)
    S = np.concatenate([-s32, s32, -s32, s32], 0)
    c["c_ropeR"] = np.ascontiguousarray(np.stack([C, S, 0.125 * C, 0.125 * S], 1).astype(np.float32))
    c16, s16 = tables(16)
    CM = np.concatenate([np.ones((64, T), np.float32), c16, c16], 0)
    SM = np.concatenate([np.zeros((64, T), np.float32), -s16, s16], 0)
    c["c_ropeM"] = np.ascontiguousarray(np.stack([CM, SM], 1).astype(np.float32))
    g = np.array(RET_GAMMA, np.float64)
    n = np.arange(128, dtype=np.float64)
    qd = np.zeros((128, 2, 128), np.float64)
    for h in range(4):
        qd[(h % 2) * 64:(h % 2) * 64 + 64, h // 2, :] = (g[h] ** (n + 1.0))[None, :]
    c["c_qdec"] = qd.astype(np.float32)
    kd = np.zeros((128, 256), np.float64)
    for h in range(4):
        kd[:, h * 64:(h + 1) * 64] = (g[h] ** (127.0 - n))[:, None]
    c["c_kdecT"] = kd.astype(np.float32)
    dt = np.zeros((128, 4, 128), np.float64)
    diff = n[None, :] - n[:, None]
    for h in range(4):
        dt[:, h, :] = np.where(diff >= 0, g[h] ** np.maximum(diff, 0.0), 0.0)
    c["c_dt"] = dt.reshape(128, 512).astype(np.float32)
    cv = np.zeros((128, 2), np.float64)
    for h in range(4):
        cv[(h % 2) * 64:(h % 2) * 64 + 64, h // 2] = g[h] ** 128.0
    c["c_cvec"] = cv.astype(np.float32)
    s_ = np.arange(128)[:, None]
    t_ = np.arange(512)[None, :]
    msb = np.zeros((128, 6, 512), np.float32)
    mml = np.zeros((128, 6, 512), np.float32)
    for cc in range(4):
        msb[:, cc, :] = ((cc * 128 + s_) < t_)
        mml[:, cc, :] = ((cc * 128 + s_) <= t_)
    valid = (s_ >= 112)
    msb[:, 4, :] = ((s_ < t_) & valid)
    msb[:, 5, :] = np.broadcast_to(valid, (128, 512))
    mml[:, 4, :] = ((s_ <= t_) & (valid | (s_ == t_)))
    mml[:, 5, :] = np.broadcast_to(valid, (128, 512))
    c["c_msb"] = msb
    c["c_mmla"] = mml
    c["c_valid"] = valid.astype(np.float32).reshape(128, 1)
    return c


def _rot_idx(nheads, d):
    half = d // 2
    idx = []
    for h in range(nheads):
        idx += [h * d + ((i + half) % d) for i in range(d)]
    return np.array(idx)


def _prep_weights(inp, NL):
    w_in = np.asarray(inp["w_in"])[:NL]
    b_sbq, b_sbk, b_sbv, b_cq, b_ckv, b_kr, b_rq, b_rk, b_rv, b_rg = 0, 512, 1024, 1536, 1920, 2176, 2208, 2464, 2720, 3232
    cols = np.concatenate([
        np.arange(b_sbq, b_sbq + 512), np.arange(b_sbk, b_sbk + 512), np.arange(b_sbv, b_sbv + 512),
        np.arange(b_cq, b_cq + 384), np.arange(b_ckv, b_ckv + 256),
        np.arange(b_kr, b_kr + 32), b_kr + _rot_idx(1, 32),
        np.arange(b_rq, b_rq + 256), b_rq + _rot_idx(4, 64),
        np.arange(b_rk, b_rk + 256), b_rk + _rot_idx(4, 64),
        np.arange(b_rv, b_rv + 512), np.arange(b_rg, b_rg + 512)])
    assert cols.size == NA
    wa = np.ascontiguousarray(np.take(w_in, cols, axis=2))
    w_uq = np.asarray(inp["w_uq"])[:NL]
    ucols = []
    for h in range(8):
        base = h * 96
        xc = list(range(base, base + 96))
        rc = list(range(base, base + 64)) + [base + 64 + ((i + 16) % 32) for i in range(32)]
        ucols += xc + rc
    wuq = np.ascontiguousarray(np.take(w_uq, np.array(ucols), axis=2))
    w_ukv = np.asarray(inp["w_ukv"])[:NL]
    kc = np.concatenate([np.arange(h * 128, h * 128 + 64) for h in range(8)] +
                        [np.arange(h * 128 + 64, h * 128 + 128) for h in range(8)])
    wukv = np.ascontiguousarray(np.take(w_ukv, kc, axis=2))
    lnp = np.zeros((9, 2, 1024), np.float32)
    lnp[0, 0] = np.asarray(inp["ln_emb_g"])
    lnp[0, 1] = np.asarray(inp["ln_emb_b"])
    for l in range(NLAYERS):
        lnp[1 + 2 * l, 0] = np.asarray(inp["ln1_g"])[l]
        lnp[1 + 2 * l, 1] = np.asarray(inp["ln1_b"])[l]
        lnp[2 + 2 * l, 0] = np.asarray(inp["ln2_g"])[l]
        lnp[2 + 2 * l, 1] = np.asarray(inp["ln2_b"])[l]
    lnp = np.ascontiguousarray(lnp.reshape(9, 2, 8, 128).transpose(3, 0, 1, 2))
    qn = np.ascontiguousarray(np.asarray(inp["mla_q_norm"]).reshape(NLAYERS, 3, 128).transpose(2, 0, 1))
    kvn = np.ascontiguousarray(np.asarray(inp["mla_kv_norm"]).reshape(NLAYERS, 2, 128).transpose(2, 0, 1))
    shared = {
        "meta": np.ascontiguousarray(np.asarray(inp["meta_tokens"], dtype=np.float32)),
        "lnp": lnp.astype(np.float32), "qn": qn.astype(np.float32), "kvn": kvn.astype(np.float32),
        "wa": wa.astype(np.float32), "wuq": wuq.astype(np.float32), "wukv": wukv.astype(np.float32),
        "wo": np.ascontiguousarray(np.asarray(inp["w_out"], dtype=np.float32)[:NL]),
        "w1": np.ascontiguousarray(np.asarray(inp["w_ff1"], dtype=np.float32)[:NL]),
        "w2": np.ascontiguousarray(np.asarray(inp["w_ff2"], dtype=np.float32)[:NL]),
    }
    shared.update(_consts())
    return shared


def run(inp, NL=NLAYERS, cores=8):
    shared = _prep_weights(inp, NL)
    x = np.asarray(inp["x"], dtype=np.float32)
    nc = build_nc(NL)
    in_maps = []
    for b in range(cores):
        m = dict(shared)
        m["x"] = np.ascontiguousarray(x[b])
        in_maps.append(m)
    if _TRACE:
        res = run_bass_kernel_spmd(nc, in_maps, core_ids=list(range(cores)), trace=True)
        print("TRACE exec_time_ns", res.exec_time_ns, flush=True)
    else:
        res = run_bass_kernel_spmd(nc, in_maps, core_ids=list(range(cores)))
    return np.stack([np.asarray(r["out"]) for r in res.results], axis=0).astype(np.float32)


def kernel(**inputs):
    return run(inputs, NLAYERS, 8)
```

```python
import numpy as np
import ml_dtypes
from contextlib import ExitStack
import concourse.bass as bass
import concourse.mybir as mybir
from concourse.bass_utils import run_bass_kernel_spmd

F32 = mybir.dt.float32
BF16 = mybir.dt.bfloat16
ALU = mybir.AluOpType
AF = mybir.ActivationFunctionType

D = 1024
SEQ = 4096
NLAYERS = 4
T = 4224
NBLK = 33
TILES = [(0, 128)] + [(128 + 512 * i, 512) for i in range(8)]
LN_EPS = 1e-5
DN_ALPHA = (2 * NLAYERS) ** 0.25
RET_GAMMA = [1.0 - 2.0 ** (-5 - h) for h in range(4)]
MLA_SCALE = 96 ** -0.5

O_SBQ, O_SBK, O_SBV, O_CQ, O_CKV, O_KR, O_KRR = 0, 512, 1024, 1536, 1920, 2176, 2208
O_RQ, O_RQR, O_RK, O_RKR, O_RV, O_RG = 2240, 2496, 2752, 3008, 3264, 3776
NA = 4288
SEM_LIM = 12000
_TRACE = False


class Builder:
    def __init__(self, nc, es):
        self.nc = nc
        self.es = es
        self.E = {"pe": nc.tensor, "act": nc.scalar, "dve": nc.vector, "pool": nc.gpsimd, "sp": nc.sync}
        self.cnt = {e: 0 for e in self.E}
        self.waited = {e: {} for e in self.E}
        self.sems = {}
        self.st = {}
        self.dq = {}
        self.flip = 0

    def sem(self, key):
        if key not in self.sems:
            self.sems[key] = self.es.enter_context(self.nc.semaphore("s%d" % len(self.sems)))
        return self.sems[key]

    def _wait(self, eng, tok):
        sk, val = tok
        w = self.waited[eng]
        if w.get(sk, 0) >= val:
            return
        w[sk] = val
        self.E[eng].wait_ge(self.sem(sk), val)

    def _deps(self, eng, r, w):
        toks = set()
        for k in r:
            s = self.st.get(k)
            if s and s[0]:
                toks.add(s[0])
        for k in w:
            s = self.st.get(k)
            if s:
                if s[0] and s[0][0][0] != eng:
                    toks.add(s[0])
                for sk, v in s[1].items():
                    if sk[0] != eng:
                        toks.add((sk, v))
        for t in toks:
            self._wait(eng, t)

    def _update(self, tok, r, w):
        sk, val = tok
        for k in r:
            s = self.st.setdefault(k, [None, {}])
            if s[1].get(sk, 0) < val:
                s[1][sk] = val
        for k in w:
            self.st[k] = [tok, {}]

    def op(self, eng, fn, r=(), w=()):
        self._deps(eng, r, w)
        ins = fn(self.E[eng])
        self.cnt[eng] += 1
        i = self.cnt[eng]
        sk = (eng, (i - 1) // SEM_LIM)
        val = (i - 1) % SEM_LIM + 1
        ins.then_inc(self.sem(sk), 1)
        self._update((sk, val), r, w)

    def dma(self, out, in_, r=(), w=(), q="sp"):
        d = self.dq.setdefault(q, {"uses": [0] * 8, "next": 0})
        slot = d["next"]
        d["next"] = (slot + 1) % 8
        sk = ("dma", q, slot)
        if d["uses"][slot] > 0:
            self._wait(q, (sk, 16 * d["uses"][slot]))
        self._deps(q, r, w)
        self.E[q].dma_start(out=out, in_=in_).then_inc(self.sem(sk), 16)
        d["uses"][slot] += 1
        self._update((sk, 16 * d["uses"][slot]), r, w)

    def all_tokens(self):
        toks = []
        for e in ("pe", "act", "dve", "pool"):
            i = self.cnt[e]
            if i > 0:
                toks.append(((e, (i - 1) // SEM_LIM), (i - 1) % SEM_LIM + 1))
        for q, d in self.dq.items():
            for slot, u in enumerate(d["uses"]):
                if u > 0:
                    toks.append((("dma", q, slot), 16 * u))
        return toks

    def barrier(self):
        toks = self.all_tokens()
        for e in self.E:
            for t in toks:
                if t[0][0] == e:
                    continue
                self._wait(e, t)

    def finish(self):
        for t in self.all_tokens():
            if t[0][0] == "dma":
                self._wait("sp", t)

    def mm(self, out, lhsT, rhs, start, stop, r, w, **kw):
        self.op("pe", lambda e: e.matmul(out, lhsT, rhs, start=start, stop=stop, **kw), r=r, w=w)

    def tr(self, out, in_, ident, r, w):
        self.op("pe", lambda e: e.transpose(out, in_, ident), r=r, w=w)

    def evac(self, out, in_, r, w, scale=None, eng=None):
        if eng is None:
            self.flip ^= 1
            eng = "act" if self.flip else "dve"
        if eng == "act":
            if scale is None:
                self.op("act", lambda e: e.copy(out=out, in_=in_), r=r, w=w)
            else:
                self.op("act", lambda e: e.mul(out=out, in_=in_, mul=scale), r=r, w=w)
        else:
            if scale is None:
                self.op("dve", lambda e: e.tensor_copy(out=out, in_=in_), r=r, w=w)
            else:
                self.op("dve", lambda e: e.tensor_scalar(out=out, in0=in_, scalar1=scale, scalar2=None,
                                                         op0=ALU.mult), r=r, w=w)


def build_nc(NL, debug=False):
    nc = bass.Bass("TRN2", target_bir_lowering=False)

    def din(name, shape, dt=F32):
        return nc.dram_tensor(name, shape, dt, kind="ExternalInput").ap()

    def dscr(name, shape, dt):
        return nc.dram_tensor(name, shape, dt, kind="Internal").ap()

    x = din("x", [SEQ, D])
    meta = din("meta", [16, D])
    lnp = din("lnp", [128, 9, 2, 8])
    qn = din("qn", [128, NLAYERS, 3])
    kvn = din("kvn", [128, NLAYERS, 2])
    wa = din("wa", [NL, D, NA])
    wuq = din("wuq", [NL, 384, 1536])
    wukv = din("wukv", [NL, 256, 1024])
    wo = din("wo", [NL, 1536, D])
    w1 = din("w1", [NL, D, 4096])
    w2 = din("w2", [NL, 4096, D])
    c_ident = din("c_ident", [128, 128])
    c_nuincl = din("c_nuincl", [128, 128])
    c_ropeR = din("c_ropeR", [128, 4, T])
    c_ropeM = din("c_ropeM", [96, 2, T])
    c_qdec = din("c_qdec", [128, 2, 128])
    c_kdecT = din("c_kdecT", [128, 256])
    c_dt = din("c_dt", [128, 512])
    c_cvec = din("c_cvec", [128, 2])
    c_msb = din("c_msb", [128, 6, 512])
    c_mmla = din("c_mmla", [128, 6, 512])
    c_valid = din("c_valid", [128, 1])
    out = nc.dram_tensor("out", [SEQ, D], F32, kind="ExternalOutput").ap()

    hres = dscr("hres", [128, 8, T], F32)
    hT = dscr("hT", [128, 8, T], BF16)
    sbq = dscr("sbq", [128, 4, T], BF16)
    sbk = dscr("sbk", [128, 4, T], BF16)
    sbv = dscr("sbv", [128, NBLK, 512], BF16)
    mq = dscr("mq", [128, 8, T], BF16)
    mk = dscr("mk", [128, 8, T], BF16)
    mv = dscr("mv", [128, NBLK, 512], BF16)
    rq = dscr("rq", [128, 2, T], BF16)
    rqd = dscr("rqd", [128, 2, T], BF16)
    rk = dscr("rk", [128, 2, T], BF16)
    rkd = dscr("rkd", [128, NBLK, 256], BF16)
    rv = dscr("rv", [128, NBLK, 512], BF16)
    rg = dscr("rg", [128, 4, T], F32)
    mix = dscr("mix", [128, 12, T], BF16)

    with ExitStack() as es:
        B = Builder(nc, es)

        uniq = {"n": 0}

        def sb(stack, name, shape, dt):
            uniq["n"] += 1
            return stack.enter_context(nc.sbuf_tensor("%s_%d" % (name, uniq["n"]), shape, dt))

        ps = [es.enter_context(nc.psum_tensor("ps%d" % i, [128, 512], F32)) for i in range(8)]
        rot = {"i": 0}

        def bank(lo=0, hi=8):
            n = hi - lo
            b = lo + rot["i"] % n
            rot["i"] += 1
            return b

        identF = sb(es, "identF", [128, 128], F32)
        identB = sb(es, "identB", [128, 128], BF16)
        onesF = sb(es, "onesF", [128, 128], F32)
        onesB = sb(es, "onesB", [128, 64], BF16)
        RS = sb(es, "RS", [128, 512], F32)
        lnps = sb(es, "lnps", [128, 9, 2, 8], F32)
        qns = sb(es, "qns", [128, NLAYERS, 3], F32)
        kvns = sb(es, "kvns", [128, NLAYERS, 2], F32)
        B.dma(identF[:], c_ident, w=["identF"])
        B.dma(identB[:], c_ident, w=["identB"], q="pool")
        B.dma(lnps[:], lnp, w=["lnps"])
        B.dma(qns[:], qn, w=["qns"])
        B.dma(kvns[:], kvn, w=["kvns"])
        B.op("pool", lambda e: e.memset(onesF[:], 1.0), w=["onesF"])
        B.op("pool", lambda e: e.memset(onesB[:], 1.0), w=["onesB"])
        with ExitStack() as z0:
            ZR = sb(z0, "ZR", [32, T], BF16)
            B.op("pool", lambda e: e.memset(ZR[:], 0.0), w=["ZR"])
            B.dma(mq[96:128, :, :], ZR[:].unsqueeze(1).broadcast_to([32, 8, T]), r=["ZR"], w=["mqz"])
            B.dma(mk[96:128, :, :], ZR[:].unsqueeze(1).broadcast_to([32, 8, T]), r=["ZR"], w=["mkz"])
            B.barrier()

        def ln_f(ph, X, kx, SQ, ksq, R, XB, W, lnidx, t0, final):
            xs = X[:, :, 0:W]
            mp = bank()
            B.op("dve", lambda e: e.reduce_sum(out=RS[:, 0:W], in_=xs.rearrange("p c n -> p n c"),
                                               axis=mybir.AxisListType.X), r=[kx], w=["RS"])
            B.mm(ps[mp][:, 0:W], onesF[:], RS[:, 0:W], True, True, r=["RS", "onesF"], w=[("ps", mp)])
            B.op("dve", lambda e: e.scalar_tensor_tensor(
                out=xs, in0=ps[mp][:, 0:W].unsqueeze(1).broadcast_to([128, 8, W]), scalar=-1.0 / D, in1=xs,
                op0=ALU.mult, op1=ALU.add), r=[("ps", mp), kx], w=[kx])
            B.op("act", lambda e: e.activation(out=SQ[:, :, 0:W], in_=xs, func=AF.Square), r=[kx], w=ksq)
            vp = bank()
            B.op("dve", lambda e: e.reduce_sum(out=RS[:, 0:W], in_=SQ[:, :, 0:W].rearrange("p c n -> p n c"),
                                               axis=mybir.AxisListType.X), r=ksq, w=["RS"])
            B.mm(ps[vp][:, 0:W], onesF[:], RS[:, 0:W], True, True, r=["RS", "onesF"], w=[("ps", vp)])
            B.op("act", lambda e: e.activation(out=R[:, 0:W], in_=ps[vp][:, 0:W], func=AF.Ln, scale=1.0 / D,
                                               bias=LN_EPS), r=[("ps", vp)], w=["R"])
            B.op("act", lambda e: e.activation(out=R[:, 0:W], in_=R[:, 0:W], func=AF.Exp, scale=-0.5),
                 r=["R"], w=["R"])
            B.op("dve", lambda e: e.tensor_tensor(out=xs, in0=xs, in1=R[:, 0:W].unsqueeze(1).broadcast_to([128, 8, W]),
                                                  op=ALU.mult), r=[kx, "R"], w=[kx])
            for c in range(8):
                B.op("act", lambda e, c=c: e.activation(out=X[:, c, 0:W], in_=X[:, c, 0:W], func=AF.Identity,
                                                        scale=lnps[:, lnidx, 0, c:c + 1],
                                                        bias=lnps[:, lnidx, 1, c:c + 1]),
                     r=[kx, "lnps"], w=[kx])
            B.op("pool", lambda e: e.tensor_copy(out=XB[:, :, 0:W], in_=xs), r=[kx], w=["XB"])
            hk = ("h", t0)
            B.dma(hres[:, :, t0:t0 + W], xs, r=[kx], w=[("hres", t0)])
            B.dma(hT[:, :, t0:t0 + W], XB[:, :, 0:W], r=["XB"], w=[("hT", t0)])
            if final and t0 >= 128:
                for s in range(W // 128):
                    OT = ph["OT"][s % 2]
                    ko = "OT%d" % (s % 2)
                    for half in range(2):
                        b = bank()
                        for cc in range(4):
                            c = half * 4 + cc
                            B.tr(ps[b][:, cc * 128:(cc + 1) * 128], X[:, c, s * 128:(s + 1) * 128], identF[:],
                                 r=[kx, "identF"], w=[("ps", b)])
                        B.evac(OT[:, half * 512:(half + 1) * 512], ps[b][:], r=[("ps", b)], w=[ko])
                    tok0 = t0 - 128 + s * 128
                    B.dma(out[tok0:tok0 + 128, :], OT[:], r=[ko], w=[("out", tok0)])

        def phase_embed():
            with ExitStack() as ph:
                xt = [sb(ph, "e_xt%d" % i, [128, 4, D], F32) for i in range(2)]
                X = [sb(ph, "e_X%d" % i, [128, 8, 512], F32) for i in range(2)]
                SQ = sb(ph, "e_SQ", [128, 8, 512], F32)
                R = sb(ph, "e_R", [128, 512], F32)
                XB = sb(ph, "e_XB", [128, 8, 512], BF16)
                for ti, (t0, W) in enumerate(TILES):
                    a = xt[ti % 2]
                    ka = "e_xt%d" % (ti % 2)
                    kx = "e_X%d" % (ti % 2)
                    nsub = W // 128
                    if ti == 0:
                        B.op("pool", lambda e: e.memset(a[:, 0, :], 0.0), w=[ka])
                        B.dma(a[112:128, 0, :], meta, w=[ka])
                    else:
                        B.dma(a[:], x[t0 - 128:t0 - 128 + 512, :].rearrange("(s p) d -> p s d", p=128), w=[ka])
                    for c in range(8):
                        b = bank()
                        for s in range(nsub):
                            B.tr(ps[b][:, s * 128:(s + 1) * 128], a[:, s, c * 128:(c + 1) * 128], identF[:],
                                 r=[ka, "identF"], w=[("ps", b)])
                        B.evac(X[ti % 2][:, c, 0:W], ps[b][:, 0:W], r=[("ps", b)], w=[kx])
                    ln_f(None, X[ti % 2], kx, SQ[:], ["SQ"], R, XB, W, 0, t0, False)
                B.barrier()

        def phase_A(l):
            with ExitStack() as ph:
                WA = sb(ph, "a_WA", [128, 8, NA], BF16)
                WUQ = sb(ph, "a_WUQ", [128, 3, 1536], BF16)
                WUKV = sb(ph, "a_WUKV", [128, 2, 1024], BF16)
                for c in range(8):
                    B.dma(WA[:, c, :], wa[l, c * 128:(c + 1) * 128, :], w=["WA"], q="pool")
                B.dma(WUQ[:], wuq[l].rearrange("(c p) n -> p c n", p=128), w=["WUQ"], q="pool")
                B.dma(WUKV[:], wukv[l].rearrange("(c p) n -> p c n", p=128), w=["WUKV"], q="pool")
                QDEC = sb(ph, "a_QDEC", [128, 2, 128], F32)
                KDEC = sb(ph, "a_KDEC", [128, 256], F32)
                B.dma(QDEC[:], c_qdec, w=["QDEC"])
                B.dma(KDEC[:], c_kdecT, w=["KDEC"])
                HT = [sb(ph, "a_HT%d" % i, [128, 8, 512], BF16) for i in range(2)]
                RT = [sb(ph, "a_RT%d" % i, [128, 4, 512], F32) for i in range(2)]
                MT = [sb(ph, "a_MT%d" % i, [96, 2, 512], F32) for i in range(2)]
                MK = [sb(ph, "a_MK%d" % i, [32, 2, 512], F32) for i in range(2)]
                CQ = sb(ph, "a_CQ", [128, 3, 512], F32)
                CKV = sb(ph, "a_CKV", [128, 2, 512], F32)
                SQ = sb(ph, "a_SQ", [128, 3, 512], F32)
                R = sb(ph, "a_R", [128, 512], F32)
                CQN = sb(ph, "a_CQN", [128, 3, 512], BF16)
                CKVN = sb(ph, "a_CKVN", [128, 2, 512], BF16)
                T1 = [sb(ph, "a_T1%d" % i, [128, 512], F32) for i in range(3)]
                T2 = [sb(ph, "a_T2%d" % i, [128, 512], F32) for i in range(3)]
                STG = [sb(ph, "a_STG%d" % i, [128, 512], BF16) for i in range(6)]
                RKB = [sb(ph, "a_RKB%d" % i, [128, 512], BF16) for i in range(2)]
                GS = [sb(ph, "a_GS0", [128, 4, 512], F32)] * 2
                KST = [sb(ph, "a_KST%d" % i, [128, 256], BF16) for i in range(2)]
                cn = {"stg": 0, "t": 0, "kst": 0}

                def stage():
                    i = cn["stg"] % 6
                    cn["stg"] += 1
                    return STG[i], "STG%d" % i

                def tt():
                    i = cn["t"] % 3
                    cn["t"] += 1
                    return T1[i], "T1%d" % i, T2[i], "T2%d" % i

                def load_tile(ti):
                    t0, W = TILES[ti]
                    p = ti % 2
                    B.dma(HT[p][:, :, 0:W], hT[:, :, t0:t0 + W], r=[("hT", t0)], w=["HT%d" % p])
                    B.dma(RT[p][:, :, 0:W], c_ropeR[:, :, t0:t0 + W], w=["RT%d" % p])
                    B.dma(MT[p][:, :, 0:W], c_ropeM[:, :, t0:t0 + W], w=["MT%d" % p])
                    B.dma(MK[p][:, :, 0:W], c_ropeM[64:96, :, t0:t0 + W], w=["MK%d" % p])

                load_tile(0)
                for ti, (t0, W) in enumerate(TILES):
                    p = ti % 2
                    if ti + 1 < len(TILES):
                        load_tile(ti + 1)
                    H = HT[p]
                    kh = "HT%d" % p
                    nsub = W // 128
                    blk0 = t0 // 128

                    def fgroup(col0, M, Wm=WA, kw="WA", rhs=None, krhs=None, nchunk=8):
                        b = bank()
                        for c in range(nchunk):
                            rr = H[:, c, 0:W] if rhs is None else rhs[:, c, 0:W]
                            B.mm(ps[b][0:M, 0:W], Wm[:, c, col0:col0 + M], rr, c == 0, c == nchunk - 1,
                                 r=[kw, kh if krhs is None else krhs], w=[("ps", b)])
                        return b

                    for m in range(4):
                        b = fgroup(O_SBQ + m * 128, 128)
                        S_, ks = stage()
                        B.evac(S_[:, 0:W], ps[b][:, 0:W], r=[("ps", b)], w=[ks], scale=0.125)
                        B.dma(sbq[:, m, t0:t0 + W], S_[:, 0:W], r=[ks], w=[("sbq", ti)])
                    for m in range(4):
                        b = fgroup(O_SBK + m * 128, 128)
                        S_, ks = stage()
                        B.evac(S_[:, 0:W], ps[b][:, 0:W], r=[("ps", b)], w=[ks])
                        B.dma(sbk[:, m, t0:t0 + W], S_[:, 0:W], r=[ks], w=[("sbk", ti)])
                    for (col0, dst, nm) in ((O_SBV, sbv, "sbv"), (O_RV, rv, "rv")):
                        for s in range(nsub):
                            b = bank()
                            for c in range(8):
                                B.mm(ps[b][:, :], H[:, c, s * 128:(s + 1) * 128], WA[:, c, col0:col0 + 512],
                                     c == 0, c == 7, r=["WA", kh], w=[("ps", b)])
                            S_, ks = stage()
                            B.evac(S_[:, :], ps[b][:, :], r=[("ps", b)], w=[ks])
                            B.dma(dst[:, blk0 + s, :], S_[:, :], r=[ks], w=[(nm, ti)])
                    for m in range(3):
                        b = fgroup(O_CQ + m * 128, 128)
                        B.evac(CQ[:, m, 0:W], ps[b][:, 0:W], r=[("ps", b)], w=["CQ"])
                    for m in range(2):
                        b = fgroup(O_CKV + m * 128, 128)
                        B.evac(CKV[:, m, 0:W], ps[b][:, 0:W], r=[("ps", b)], w=["CKV"])
                    ba = fgroup(O_KR, 32)
                    bb = fgroup(O_KRR, 32)
                    t1, k1, t2, k2 = tt()
                    B.op("dve", lambda e: e.tensor_tensor(out=t1[0:32, 0:W], in0=ps[ba][0:32, 0:W], in1=MK[p][:, 0, 0:W],
                                                          op=ALU.mult), r=[("ps", ba), "MK%d" % p], w=[k1])
                    B.op("dve", lambda e: e.tensor_tensor(out=t2[0:32, 0:W], in0=ps[bb][0:32, 0:W], in1=MK[p][:, 1, 0:W],
                                                          op=ALU.mult), r=[("ps", bb), "MK%d" % p], w=[k2])
                    S_, ks = stage()
                    B.op("pool", lambda e: e.tensor_tensor(out=S_[0:32, 0:W], in0=t1[0:32, 0:W], in1=t2[0:32, 0:W],
                                                           op=ALU.add), r=[k1, k2], w=[ks])
                    B.dma(mk[64:96, :, t0:t0 + W], S_[0:32, 0:W].unsqueeze(1).broadcast_to([32, 8, W]),
                          r=[ks], w=[("mk", ti)])
                    for pr in range(2):
                        ba = fgroup(O_RQ + pr * 128, 128)
                        bb = fgroup(O_RQR + pr * 128, 128)
                        t1, k1, t2, k2 = tt()
                        B.op("dve", lambda e: e.tensor_tensor(out=t1[:, 0:W], in0=ps[ba][:, 0:W], in1=RT[p][:, 0, 0:W],
                                                              op=ALU.mult), r=[("ps", ba), "RT%d" % p], w=[k1])
                        B.op("dve", lambda e: e.tensor_tensor(out=t2[:, 0:W], in0=ps[bb][:, 0:W], in1=RT[p][:, 1, 0:W],
                                                              op=ALU.mult), r=[("ps", bb), "RT%d" % p], w=[k2])
                        B.op("pool", lambda e: e.tensor_tensor(out=t1[:, 0:W], in0=t1[:, 0:W], in1=t2[:, 0:W],
                                                               op=ALU.add), r=[k1, k2], w=[k1])
                        S_, ks = stage()
                        B.op("act", lambda e: e.copy(out=S_[:, 0:W], in_=t1[:, 0:W]), r=[k1], w=[ks])
                        B.dma(rq[:, pr, t0:t0 + W], S_[:, 0:W], r=[ks], w=[("rq", ti)])
                        S2, ks2 = stage()
                        B.op("pool", lambda e: e.tensor_tensor(
                            out=S2[:, 0:W].rearrange("p (s n) -> p s n", n=128),
                            in0=t1[:, 0:W].rearrange("p (s n) -> p s n", n=128),
                            in1=QDEC[:, pr, :].unsqueeze(1).broadcast_to([128, nsub, 128]), op=ALU.mult),
                            r=[k1, "QDEC"], w=[ks2])
                        B.dma(rqd[:, pr, t0:t0 + W], S2[:, 0:W], r=[ks2], w=[("rqd", ti)])
                    for pr in range(2):
                        ba = fgroup(O_RK + pr * 128, 128)
                        bb = fgroup(O_RKR + pr * 128, 128)
                        t1, k1, t2, k2 = tt()
                        B.op("dve", lambda e: e.tensor_tensor(out=t1[:, 0:W], in0=ps[ba][:, 0:W], in1=RT[p][:, 2, 0:W],
                                                              op=ALU.mult), r=[("ps", ba), "RT%d" % p], w=[k1])
                        B.op("dve", lambda e: e.tensor_tensor(out=t2[:, 0:W], in0=ps[bb][:, 0:W], in1=RT[p][:, 3, 0:W],
                                                              op=ALU.mult), r=[("ps", bb), "RT%d" % p], w=[k2])
                        kb_ = "RKB%d" % pr
                        B.op("pool", lambda e: e.tensor_tensor(out=RKB[pr][:, 0:W], in0=t1[:, 0:W], in1=t2[:, 0:W],
                                                               op=ALU.add), r=[k1, k2], w=[kb_])
                        if ti == 0:
                            B.op("pool", lambda e: e.memset(RKB[pr][:, 0:112], 0.0), r=[kb_], w=[kb_])
                        B.dma(rk[:, pr, t0:t0 + W], RKB[pr][:, 0:W], r=[kb_], w=[("rk", ti)])
                    for s in range(nsub):
                        b = bank()
                        pb = ps[b][:].bitcast(BF16)
                        for pr in range(2):
                            B.tr(pb[:, pr * 128:(pr + 1) * 128], RKB[pr][:, s * 128:(s + 1) * 128], identB[:],
                                 r=["RKB%d" % pr, "identB"], w=[("ps", b)])
                        i = cn["kst"] % 2
                        cn["kst"] += 1
                        B.op("dve", lambda e: e.tensor_tensor(out=KST[i][:], in0=pb[:, 0:256], in1=KDEC[:], op=ALU.mult),
                             r=[("ps", b), "KDEC"], w=["KST%d" % i])
                        B.dma(rkd[:, blk0 + s, :], KST[i][:], r=["KST%d" % i], w=[("rkd", ti)])
                    G = GS[p]
                    for m in range(4):
                        b = fgroup(O_RG + m * 128, 128)
                        B.op("act", lambda e: e.activation(out=G[:, m, 0:W], in_=ps[b][:, 0:W], func=AF.Silu),
                             r=[("ps", b)], w=["GS"])
                    B.dma(rg[:, :, t0:t0 + W], G[:, :, 0:W], r=["GS"], w=[("rg", ti)])
                    for (src, ksrc, nch, dstn, kd, gam, inv) in ((CQ, "CQ", 3, CQN, "CQN", qns, 1.0 / 384),
                                                                   (CKV, "CKV", 2, CKVN, "CKVN", kvns, 1.0 / 256)):
                        B.op("act", lambda e: e.activation(out=SQ[:, 0:nch, 0:W], in_=src[:, :, 0:W], func=AF.Square),
                             r=[ksrc], w=["SQ"])
                        b = bank()
                        for m in range(nch):
                            B.mm(ps[b][:, 0:W], onesF[:], SQ[:, m, 0:W], m == 0, m == nch - 1, r=["SQ", "onesF"],
                                 w=[("ps", b)])
                        B.op("act", lambda e: e.activation(out=R[:, 0:W], in_=ps[b][:, 0:W], func=AF.Ln, scale=inv,
                                                           bias=LN_EPS), r=[("ps", b)], w=["R"])
                        B.op("act", lambda e: e.activation(out=R[:, 0:W], in_=R[:, 0:W], func=AF.Exp, scale=-0.5),
                             r=["R"], w=["R"])
                        for m in range(nch):
                            B.op("dve", lambda e, m=m: e.scalar_tensor_tensor(
                                out=dstn[:, m, 0:W], in0=src[:, m, 0:W], scalar=gam[:, l, m:m + 1], in1=R[:, 0:W],
                                op0=ALU.mult, op1=ALU.mult), r=[ksrc, "R", "qns", "kvns"], w=[kd])
                    for h in range(8):
                        ba = fgroup((2 * h) * 96, 96, Wm=WUQ, kw="WUQ", rhs=CQN, krhs="CQN", nchunk=3)
                        bb = fgroup((2 * h + 1) * 96, 96, Wm=WUQ, kw="WUQ", rhs=CQN, krhs="CQN", nchunk=3)
                        t1, k1, t2, k2 = tt()
                        B.op("dve", lambda e: e.tensor_tensor(out=t1[0:96, 0:W], in0=ps[ba][0:96, 0:W], in1=MT[p][:, 0, 0:W],
                                                              op=ALU.mult), r=[("ps", ba), "MT%d" % p], w=[k1])
                        B.op("dve", lambda e: e.tensor_tensor(out=t2[0:96, 0:W], in0=ps[bb][0:96, 0:W], in1=MT[p][:, 1, 0:W],
                                                              op=ALU.mult), r=[("ps", bb), "MT%d" % p], w=[k2])
                        S_, ks = stage()
                        B.op("pool", lambda e: e.tensor_tensor(out=S_[0:96, 0:W], in0=t1[0:96, 0:W], in1=t2[0:96, 0:W],
                                                               op=ALU.add), r=[k1, k2], w=[ks])
                        B.dma(mq[0:96, h, t0:t0 + W], S_[0:96, 0:W], r=[ks], w=[("mq", ti)])
                    for m in range(4):
                        b = fgroup(m * 128, 128, Wm=WUKV, kw="WUKV", rhs=CKVN, krhs="CKVN", nchunk=2)
                        S_, ks = stage()
                        B.evac(S_[:, 0:W], ps[b][:, 0:W], r=[("ps", b)], w=[ks])
                        B.dma(mk[0:64, 2 * m, t0:t0 + W], S_[0:64, 0:W], r=[ks], w=[("mk", ti)])
                        B.dma(mk[0:64, 2 * m + 1, t0:t0 + W], S_[64:128, 0:W], r=[ks], w=[("mk", ti)])
                    for s in range(nsub):
                        b = bank()
                        for c in range(2):
                            B.mm(ps[b][:, :], CKVN[:, c, s * 128:(s + 1) * 128], WUKV[:, c, 512:1024], c == 0, c == 1,
                                 r=["WUKV", "CKVN"], w=[("ps", b)])
                        S_, ks = stage()
                        B.evac(S_[:, :], ps[b][:, :], r=[("ps", b)], w=[ks])
                        B.dma(mv[:, blk0 + s, :], S_[:, :], r=[ks], w=[("mv", ti)])
                B.barrier()

        def phase_B1():
            with ExitStack() as ph:
                UIb = sb(ph, "b_UIb", [128, 128], BF16)
                ONb = sb(ph, "b_ONb", [128, 128], BF16)
                MS = sb(ph, "b_MS", [128, 6, 512], F32)
                B.dma(UIb[:], c_nuincl, w=["UIb"], q="pool")
                B.op("pool", lambda e: e.memset(ONb[:], -1.0), w=["ONb"])
                B.dma(MS[:], c_msb, w=["MS"])
                QP = [sb(ph, "b_QP%d" % i, [128, T], BF16) for i in range(2)]
                KA = [sb(ph, "b_KA%d" % i, [128, T], BF16) for i in range(2)]
                KB = [sb(ph, "b_KB%d" % i, [128, T], BF16) for i in range(2)]
                VP = [sb(ph, "b_VP%d" % i, [128, NBLK, 128], BF16) for i in range(2)]
                for i in range(2):
                    B.op("pool", lambda e: e.memset(KA[i][64:128, :], 0.0), w=["KA%d" % i])
                    B.op("pool", lambda e: e.memset(KB[i][0:64, :], 0.0), w=["KB%d" % i])
                NR = 10
                EB = [sb(ph, "b_E%d" % i, [128, 512], F32) for i in range(NR)]
                LK = [sb(ph, "b_LK%d" % i, [128, 512], F32) for i in range(NR)]
                HI = [sb(ph, "b_HI%d" % i, [128, 512], BF16) for i in range(NR)]
                LO = [sb(ph, "b_LO%d" % i, [128, 512], BF16) for i in range(NR)]
                SS = [sb(ph, "b_S%d" % i, [128, 512], F32) for i in range(NR)]
                WW = [sb(ph, "b_W%d" % i, [128, 512], BF16) for i in range(NR)]
                CC = [sb(ph, "b_C%d" % i, [128, 512], F32) for i in range(2)]
                OS = [sb(ph, "b_OS%d" % i, [128, 512], BF16) for i in range(2)]
                allt = list(range(9))

                def load_pair(m):
                    p = m % 2
                    B.dma(QP[p][:], sbq[:, m, :], r=[("sbq", i) for i in allt], w=["QP%d" % p])
                    B.dma(KA[p][0:64, :], sbk[0:64, m, :], r=[("sbk", i) for i in allt], w=["KA%d" % p])
                    B.dma(KB[p][64:128, :], sbk[64:128, m, :], r=[("sbk", i) for i in allt], w=["KB%d" % p])
                    B.dma(VP[p][:], sbv[:, :, m * 128:(m + 1) * 128], r=[("sbv", i) for i in allt], w=["VP%d" % p])

                jobs = []
                for h in range(8):
                    for ti, (t0, W) in enumerate(TILES):
                        nkb = (t0 + W) // 128
                        for kb in reversed(range(nkb)):
                            jobs.append((h, ti, kb, kb == nkb - 1, kb == 0))
                njobs = len(jobs)

                def kop(h):
                    p = (h // 2) % 2
                    return (KA[p], "KA%d" % p) if h % 2 == 0 else (KB[p], "KB%d" % p)

                def mask_idx(ti, kb, t0):
                    c = kb - t0 // 128
                    if ti == 0:
                        return 4
                    if c >= 0:
                        return c
                    if kb == 0:
                        return 5
                    return None

                def stage1(ji):
                    h, ti, kb, first, last = jobs[ji]
                    t0, W = TILES[ti]
                    p = (h // 2) % 2
                    Kx, kk = kop(h)
                    i = ji % NR
                    zb = ji % 2
                    B.mm(ps[zb][:, 0:W], Kx[:, kb * 128:(kb + 1) * 128], QP[p][:, t0:t0 + W], True, True,
                         r=[kk, "QP%d" % p], w=[("ps", zb)])
                    B.op("act", lambda e: e.activation(out=EB[i][:, 0:W], in_=ps[zb][:, 0:W], func=AF.Exp),
                         r=[("ps", zb)], w=["E%d" % i])
                    B.op("act", lambda e: e.activation(out=LK[i][:, 0:W], in_=EB[i][:, 0:W], func=AF.Ln, bias=1.0),
                         r=["E%d" % i], w=["LK%d" % i])

                def stage1b(ji):
                    h, ti, kb, first, last = jobs[ji]
                    t0, W = TILES[ti]
                    i = ji % NR
                    mi = mask_idx(ti, kb, t0)
                    if mi is not None:
                        B.op("pool", lambda e: e.tensor_tensor(out=LK[i][:, 0:W], in0=LK[i][:, 0:W], in1=MS[:, mi, 0:W],
                                                               op=ALU.mult), r=["LK%d" % i, "MS"], w=["LK%d" % i])
                    B.op("dve", lambda e: e.tensor_copy(out=HI[i][:, 0:W], in_=LK[i][:, 0:W]),
                         r=["LK%d" % i], w=["HI%d" % i])
                    B.op("pool", lambda e: e.tensor_tensor(out=LO[i][:, 0:W], in0=LK[i][:, 0:W], in1=HI[i][:, 0:W],
                                                           op=ALU.subtract),
                         r=["LK%d" % i, "HI%d" % i], w=["LO%d" % i])

                def stage2(ji):
                    h, ti, kb, first, last = jobs[ji]
                    t0, W = TILES[ti]
                    p = (h // 2) % 2
                    Kx, kk = kop(h)
                    i = ji % NR
                    eb = 2 + ji % 2
                    tb = 4 + ji % 2
                    cp = (h * 9 + ti) % 2
                    Cc, kc = CC[cp], "C%d" % cp
                    B.mm(ps[eb][:, 0:W], UIb[:], HI[i][:, 0:W], True, False, r=["UIb", "HI%d" % i], w=[("ps", eb)])
                    B.mm(ps[eb][:, 0:W], UIb[:], LO[i][:, 0:W], False, False, r=["UIb", "LO%d" % i], w=[("ps", eb)])
                    B.mm(ps[eb][:, 0:W], Kx[:, kb * 128:(kb + 1) * 128], QP[p][:, t0:t0 + W], False, True,
                         r=[kk, "QP%d" % p], w=[("ps", eb)])
                    if not last:
                        B.mm(ps[tb][:, 0:W], ONb[:], HI[i][:, 0:W], True, False, r=["ONb", "HI%d" % i], w=[("ps", tb)])
                        B.mm(ps[tb][:, 0:W], ONb[:], LO[i][:, 0:W], False, True, r=["ONb", "LO%d" % i], w=[("ps", tb)])

                def stage2d(ji):
                    h, ti, kb, first, last = jobs[ji]
                    t0, W = TILES[ti]
                    i = ji % NR
                    eb = 2 + ji % 2
                    tb = 4 + ji % 2
                    cp = (h * 9 + ti) % 2
                    Cc, kc = CC[cp], "C%d" % cp
                    if first:
                        B.op("dve", lambda e: e.tensor_copy(out=SS[i][:, 0:W], in_=ps[eb][:, 0:W]),
                             r=[("ps", eb)], w=["S%d" % i])
                        if not last:
                            B.op("dve", lambda e: e.tensor_copy(out=Cc[:, 0:W], in_=ps[tb][:, 0:W]),
                                 r=[("ps", tb)], w=[kc])
                    else:
                        B.op("dve", lambda e: e.tensor_tensor(out=SS[i][:, 0:W], in0=Cc[:, 0:W], in1=ps[eb][:, 0:W],
                                                              op=ALU.add), r=[("ps", eb), kc], w=["S%d" % i])
                        if not last:
                            B.op("dve", lambda e: e.tensor_tensor(out=Cc[:, 0:W], in0=Cc[:, 0:W], in1=ps[tb][:, 0:W],
                                                                  op=ALU.add), r=[("ps", tb), kc], w=[kc])

                def stage2a(ji):
                    h, ti, kb, first, last = jobs[ji]
                    t0, W = TILES[ti]
                    i = ji % NR
                    B.op("act", lambda e: e.activation(out=WW[i][:, 0:W], in_=SS[i][:, 0:W], func=AF.Exp),
                         r=["S%d" % i], w=["W%d" % i])
                    mi = mask_idx(ti, kb, t0)
                    if mi is not None:
                        B.op("pool", lambda e: e.tensor_tensor(out=WW[i][:, 0:W], in0=WW[i][:, 0:W], in1=MS[:, mi, 0:W],
                                                               op=ALU.mult), r=["W%d" % i, "MS"], w=["W%d" % i])

                def stage3(ji):
                    h, ti, kb, first, last = jobs[ji]
                    t0, W = TILES[ti]
                    p = (h // 2) % 2
                    i = ji % NR
                    ob = 6 + (h * 9 + ti) % 2
                    if ti == 0 and first and h % 2 == 0 and h + 2 < 8:
                        load_pair(h // 2 + 1)
                    B.mm(ps[ob][:, 0:W], VP[p][:, kb, :], WW[i][:, 0:W], first, last,
                         r=["VP%d" % p, "W%d" % i], w=[("ps", ob)])
                    if last:
                        oi = (h * 9 + ti) % 2
                        r0 = (h % 2) * 64
                        B.evac(OS[oi][r0:r0 + 64, 0:W], ps[ob][r0:r0 + 64, 0:W], r=[("ps", ob)], w=["OS%d" % oi])
                        B.dma(mix[r0:r0 + 64, h // 2, t0:t0 + W], OS[oi][r0:r0 + 64, 0:W], r=["OS%d" % oi],
                              w=[("mix", ti)])

                load_pair(0)
                stages = [(stage1, 0), (stage1b, 1), (stage2, 3), (stage2d, 4), (stage2a, 5), (stage3, 7)]
                for step in range(njobs + 7):
                    for fn, lag in stages:
                        if 0 <= step - lag < njobs:
                            fn(step - lag)
                B.barrier()

        def phase_B2():
            with ExitStack() as ph:
                MM = sb(ph, "m_MM", [128, 6, 512], F32)
                ONb = sb(ph, "m_ONb", [128, 128], BF16)
                B.dma(MM[:], c_mmla, w=["MM"])
                B.op("pool", lambda e: e.memset(ONb[:], 1.0), w=["mONb"])
                QT = [sb(ph, "m_QT%d" % i, [128, T], BF16) for i in range(2)]
                KT = [sb(ph, "m_KT%d" % i, [128, T], BF16) for i in range(2)]
                VP = [sb(ph, "m_VP%d" % i, [128, NBLK, 128], BF16) for i in range(2)]
                NR = 5
                PP = [sb(ph, "m_P%d" % i, [128, 512], BF16) for i in range(NR)]
                RC = [sb(ph, "m_RC%d" % i, [128, 512], F32) for i in range(2)]
                OS = [sb(ph, "m_OS%d" % i, [128, 512], BF16) for i in range(2)]
                allt = list(range(9))

                def load_head(h):
                    p = h % 2
                    B.dma(QT[p][:], mq[:, h, :], r=[("mq", i) for i in allt] + ["mqz"], w=["mQT%d" % p])
                    B.dma(KT[p][:], mk[:, h, :], r=[("mk", i) for i in allt] + ["mkz"], w=["mKT%d" % p])
                    if h % 2 == 0:
                        pp = (h // 2) % 2
                        B.dma(VP[pp][:], mv[:, :, (h // 2) * 128:(h // 2 + 1) * 128], r=[("mv", i) for i in allt],
                              w=["mVP%d" % pp])

                jobs = []
                for h in range(8):
                    for ti, (t0, W) in enumerate(TILES):
                        nkb = (t0 + W) // 128
                        for kb in reversed(range(nkb)):
                            jobs.append((h, ti, kb, kb == nkb - 1, kb == 0))
                njobs = len(jobs)

                def stage1(ji):
                    h, ti, kb, first, last = jobs[ji]
                    t0, W = TILES[ti]
                    p = h % 2
                    i = ji % NR
                    sbk_ = ji % 3
                    B.mm(ps[sbk_][:, 0:W], KT[p][:, kb * 128:(kb + 1) * 128], QT[p][:, t0:t0 + W], True, True,
                         r=["mKT%d" % p, "mQT%d" % p], w=[("ps", sbk_)])
                    B.op("act", lambda e: e.activation(out=PP[i][:, 0:W], in_=ps[sbk_][:, 0:W], func=AF.Exp,
                                                       scale=MLA_SCALE), r=[("ps", sbk_)], w=["P%d" % i])
                    c = kb - t0 // 128
                    mi = 4 if ti == 0 else (c if c >= 0 else (5 if kb == 0 else None))
                    if mi is not None:
                        B.op("dve", lambda e: e.tensor_tensor(out=PP[i][:, 0:W], in0=PP[i][:, 0:W], in1=MM[:, mi, 0:W],
                                                              op=ALU.mult), r=["P%d" % i, "MM"], w=["P%d" % i])

                def stage2(ji):
                    h, ti, kb, first, last = jobs[ji]
                    t0, W = TILES[ti]
                    pp = (h // 2) % 2
                    i = ji % NR
                    par = (h * 9 + ti) % 2
                    ob = 4 + par
                    db = 6 + par
                    if ti == 0 and first and h + 1 < 8:
                        load_head(h + 1)
                    B.mm(ps[ob][:, 0:W], VP[pp][:, kb, :], PP[i][:, 0:W], first, last,
                         r=["mVP%d" % pp, "P%d" % i], w=[("ps", ob)])
                    B.mm(ps[db][:, 0:W], ONb[:], PP[i][:, 0:W], first, last,
                         r=["mONb", "P%d" % i], w=[("ps", db)])
                    if last:
                        r0 = (h % 2) * 64
                        B.op("dve", lambda e: e.reciprocal(out=RC[par][r0:r0 + 64, 0:W], in_=ps[db][r0:r0 + 64, 0:W]),
                             r=[("ps", db)], w=["RC%d" % par])
                        B.op("dve", lambda e: e.tensor_tensor(out=OS[par][r0:r0 + 64, 0:W], in0=ps[ob][r0:r0 + 64, 0:W],
                                                              in1=RC[par][r0:r0 + 64, 0:W], op=ALU.mult),
                             r=[("ps", ob), "RC%d" % par], w=["mOS%d" % par])
                        B.dma(mix[r0:r0 + 64, 4 + h // 2, t0:t0 + W], OS[par][r0:r0 + 64, 0:W], r=["mOS%d" % par],
                              w=[("mix", ti)])

                load_head(0)
                L2 = 2
                for step in range(njobs + L2):
                    if step < njobs:
                        stage1(step)
                    if 0 <= step - L2 < njobs:
                        stage2(step - L2)
                B.barrier()

        def phase_B3():
            with ExitStack() as ph:
                RQ = sb(ph, "r_RQ", [64, 4, T], BF16)
                RQD = sb(ph, "r_RQD", [64, 4, T], BF16)
                RK = sb(ph, "r_RK", [64, 4, T], BF16)
                RV = sb(ph, "r_RV", [128, NBLK, 512], BF16)
                RKD = sb(ph, "r_RKD", [128, NBLK, 256], BF16)
                DT = sb(ph, "r_DT", [128, 512], F32)
                allt = list(range(9))
                for h in range(4):
                    r0 = (h % 2) * 64
                    B.dma(RQ[:, h, :], rq[r0:r0 + 64, h // 2, :], r=[("rq", i) for i in allt], w=["RQ"])
                    B.dma(RQD[:, h, :], rqd[r0:r0 + 64, h // 2, :], r=[("rqd", i) for i in allt], w=["RQD"])
                    B.dma(RK[:, h, :], rk[r0:r0 + 64, h // 2, :], r=[("rk", i) for i in allt], w=["RK"])
                B.dma(RV[:], rv, r=[("rv", i) for i in allt], w=["RV"])
                B.dma(RKD[:], rkd, r=[("rkd", i) for i in allt], w=["RKD"])
                B.dma(DT[:], c_dt, w=["DT"])
                ST = sb(ph, "r_ST", [64, 4, 128], F32)
                STB = sb(ph, "r_STB", [64, 4, 128], BF16)
                B.op("pool", lambda e: e.memset(ST[:], 0.0), w=["ST"])
                B.op("pool", lambda e: e.memset(STB[:], 0.0), w=["STB"])
                IND = [sb(ph, "r_IND%d" % i, [128, 512], BF16) for i in range(2)]
                YS = [sb(ph, "r_YS%d" % i, [128, 512], F32) for i in range(2)]
                SQ = [sb(ph, "r_SQ%d" % i, [128, 512], F32) for i in range(2)]
                R = [sb(ph, "r_R%d" % i, [128, 512], F32) for i in range(2)]
                GG = [sb(ph, "r_GG%d" % i, [128, 4, 128], F32) for i in range(2)]
                OC = [sb(ph, "r_OC%d" % i, [128, 4, 128], BF16) for i in range(2)]
                cdec = [float(np.float64(g) ** 128.0) for g in RET_GAMMA]
                for n in range(NBLK):
                    p = n % 2
                    tsl = slice(n * 128, (n + 1) * 128)
                    ti = 0 if n == 0 else 1 + (n - 1) // 4
                    B.dma(GG[p][:], rg[:, :, tsl], r=[("rg", ti)], w=["GG%d" % p])
                    bi = bank()
                    for h in range(4):
                        B.mm(ps[bi][:, h * 128:(h + 1) * 128], RK[:, h, tsl], RQ[:, h, tsl],
                             True, True, r=["RK", "RQ"], w=[("ps", bi)])
                    B.op("dve", lambda e: e.tensor_tensor(out=IND[p][:], in0=ps[bi][:], in1=DT[:], op=ALU.mult),
                         r=[("ps", bi), "DT"], w=["IND%d" % p])
                    by = bank()
                    for h in range(4):
                        B.mm(ps[by][:, h * 128:(h + 1) * 128], RV[:, n, h * 128:(h + 1) * 128],
                             IND[p][:, h * 128:(h + 1) * 128], True, False, r=["RV", "IND%d" % p], w=[("ps", by)])
                        B.mm(ps[by][:, h * 128:(h + 1) * 128], STB[:, h, :], RQD[:, h, tsl],
                             False, True, r=["STB", "RQD"], w=[("ps", by)])
                    if n + 1 < NBLK:
                        bs = bank()
                        for h in range(4):
                            B.mm(ps[bs][0:64, h * 128:(h + 1) * 128], RKD[:, n, h * 64:(h + 1) * 64],
                                 RV[:, n, h * 128:(h + 1) * 128], True, True, r=["RKD", "RV"], w=[("ps", bs)])
                        for h in range(4):
                            B.op("dve", lambda e: e.scalar_tensor_tensor(
                                out=ST[:, h, :], in0=ST[:, h, :], scalar=cdec[h],
                                in1=ps[bs][0:64, h * 128:(h + 1) * 128], op0=ALU.mult, op1=ALU.add),
                                r=["ST", ("ps", bs)], w=["ST"])
                        B.op("pool", lambda e: e.tensor_copy(out=STB[:], in_=ST[:]), r=["ST"], w=["STB"])
                    B.op("act", lambda e: e.copy(out=YS[p][:], in_=ps[by][:]), r=[("ps", by)], w=["YS%d" % p])
                    bm = bank()
                    B.mm(ps[bm][:], onesF[:], YS[p][:], True, True, r=["onesF", "YS%d" % p], w=[("ps", bm)])
                    B.op("dve", lambda e: e.scalar_tensor_tensor(out=YS[p][:], in0=ps[bm][:], scalar=-1.0 / 128,
                                                                 in1=YS[p][:], op0=ALU.mult, op1=ALU.add),
                         r=[("ps", bm), "YS%d" % p], w=["YS%d" % p])
                    B.op("act", lambda e: e.activation(out=SQ[p][:], in_=YS[p][:], func=AF.Square),
                         r=["YS%d" % p], w=["rSQ%d" % p])
                    bv = bank()
                    B.mm(ps[bv][:], onesF[:], SQ[p][:], True, True, r=["onesF", "rSQ%d" % p], w=[("ps", bv)])
                    B.op("act", lambda e: e.activation(out=R[p][:], in_=ps[bv][:], func=AF.Ln, scale=1.0 / 128,
                                                       bias=LN_EPS), r=[("ps", bv)], w=["rR%d" % p])
                    B.op("act", lambda e: e.activation(out=R[p][:], in_=R[p][:], func=AF.Exp, scale=-0.5),
                         r=["rR%d" % p], w=["rR%d" % p])
                    B.op("dve", lambda e: e.tensor_tensor(out=YS[p][:], in0=YS[p][:], in1=R[p][:], op=ALU.mult),
                         r=["YS%d" % p, "rR%d" % p], w=["YS%d" % p])
                    B.op("pool", lambda e: e.tensor_tensor(out=OC[p][:], in0=YS[p][:].rearrange("p (h n) -> p h n", h=4),
                                                           in1=GG[p][:], op=ALU.mult),
                         r=["YS%d" % p, "GG%d" % p], w=["OC%d" % p])
                    B.dma(mix[:, 8:12, tsl], OC[p][:], r=["OC%d" % p], w=[("mix", ti)])
                B.barrier()

        def phase_C(l):
            with ExitStack() as ph:
                WO = sb(ph, "c_WO", [128, 12, D], BF16)
                B.dma(WO[:], wo[l].rearrange("(c p) n -> p c n", p=128), w=["WO"], q="pool")
                MX = [sb(ph, "c_MX%d" % i, [128, 12, 512], BF16) for i in range(2)]
                X = [sb(ph, "c_X%d" % i, [128, 8, 512], F32) for i in range(2)]
                SQ = sb(ph, "c_SQ", [128, 8, 512], F32)
                R = sb(ph, "c_R", [128, 512], F32)
                XB = sb(ph, "c_XB", [128, 8, 512], BF16)

                def load_tile(ti):
                    t0, W = TILES[ti]
                    p = ti % 2
                    B.dma(MX[p][:, :, 0:W], mix[:, :, t0:t0 + W], r=[("mix", ti)], w=["MX%d" % p])
                    B.dma(X[p][:, :, 0:W], hres[:, :, t0:t0 + W], r=[("hres", t0)], w=["cX%d" % p])

                load_tile(0)
                for ti, (t0, W) in enumerate(TILES):
                    p = ti % 2
                    if ti + 1 < len(TILES):
                        load_tile(ti + 1)
                    for dc in range(8):
                        b = bank()
                        for m in range(12):
                            B.mm(ps[b][:, 0:W], WO[:, m, dc * 128:(dc + 1) * 128], MX[p][:, m, 0:W], m == 0, m == 11,
                                 r=["WO", "MX%d" % p], w=[("ps", b)])
                        B.op("dve", lambda e, dc=dc, b=b: e.scalar_tensor_tensor(
                            out=X[p][:, dc, 0:W], in0=X[p][:, dc, 0:W], scalar=DN_ALPHA, in1=ps[b][:, 0:W],
                            op0=ALU.mult, op1=ALU.add), r=["cX%d" % p, ("ps", b)], w=["cX%d" % p])
                    ln_f(None, X[p], "cX%d" % p, SQ[:], ["SQ"], R, XB, W, 1 + 2 * l, t0, False)
                B.barrier()

        def phase_D(l, final):
            with ExitStack() as ph:
                W1 = sb(ph, "d_W1", [128, 8, 4096], BF16)
                W2 = sb(ph, "d_W2", [128, 32, D], BF16)
                for c in range(8):
                    B.dma(W1[:, c, :], w1[l, c * 128:(c + 1) * 128, :], w=["W1"], q="pool")
                for c in range(4):
                    B.dma(W2[:, c * 8:(c + 1) * 8, :], w2[l, c * 1024:(c + 1) * 1024, :].rearrange("(c p) n -> p c n", p=128),
                          w=["W2"], q="pool")
                HB = [sb(ph, "d_HB%d" % i, [128, 8, 512], BF16) for i in range(2)]
                X = sb(ph, "d_X", [128, 8, 512], F32)
                R = sb(ph, "d_R", [128, 512], F32)
                XB = sb(ph, "d_XB", [128, 8, 512], BF16)
                UT = sb(ph, "d_UT", [128, 16, 512], BF16)
                SQ = UT[:].rearrange("p a n -> p (a n)").bitcast(F32).rearrange("p (c n) -> p c n", c=8)
                kut = [("UT", fc) for fc in range(16)]
                RL = [sb(ph, "d_RL%d" % i, [128, 512], F32) for i in range(3)]
                phd = {}
                if final:
                    phd["OT"] = [sb(ph, "d_OT%d" % i, [128, D], F32) for i in range(2)]

                def load_tile(ti):
                    t0, W = TILES[ti]
                    B.dma(HB[ti % 2][:, :, 0:W], hT[:, :, t0:t0 + W], r=[("hT", t0)], w=["HB%d" % (ti % 2)])

                load_tile(0)
                for ti, (t0, W) in enumerate(TILES):
                    p = ti % 2
                    if ti + 1 < len(TILES):
                        load_tile(ti + 1)
                    B.dma(X[:, :, 0:W], hres[:, :, t0:t0 + W], r=[("hres", t0)], w=["dX"])
                    for half in range(2):
                        for f in range(16):
                            fc = half * 16 + f
                            b = bank()
                            for c in range(8):
                                B.mm(ps[b][:, 0:W], W1[:, c, fc * 128:(fc + 1) * 128], HB[p][:, c, 0:W], c == 0, c == 7,
                                     r=["W1", "HB%d" % p], w=[("ps", b)])
                            i = fc % 3
                            B.op("act", lambda e: e.activation(out=RL[i][:, 0:W], in_=ps[b][:, 0:W], func=AF.Relu),
                                 r=[("ps", b)], w=["RL%d" % i])
                            eng = "pool" if fc % 2 == 0 else "dve"
                            B.op(eng, lambda e: e.tensor_tensor(out=UT[:, f, 0:W], in0=RL[i][:, 0:W],
                                                                in1=RL[i][:, 0:W], op=ALU.mult),
                                 r=["RL%d" % i], w=[("UT", f)])
                        for dc in range(8):
                            b = bank()
                            for f in range(16):
                                fc = half * 16 + f
                                B.mm(ps[b][:, 0:W], W2[:, fc, dc * 128:(dc + 1) * 128], UT[:, f, 0:W], f == 0, f == 15,
                                     r=["W2", ("UT", f)], w=[("ps", b)])
                            B.op("dve", lambda e: e.scalar_tensor_tensor(
                                out=X[:, dc, 0:W], in0=X[:, dc, 0:W], scalar=(DN_ALPHA if half == 0 else 1.0),
                                in1=ps[b][:, 0:W], op0=ALU.mult, op1=ALU.add), r=["dX", ("ps", b)], w=["dX"])
                    ln_f(phd, X, "dX", SQ, kut, R, XB, W, 2 + 2 * l, t0, final)
                B.barrier()

        import os as _os
        PH = _os.environ.get("KPH", "E,A,B1,B2,B3,C,D").split(",")
        if "E" in PH:
            phase_embed()
        for l in range(NL):
            if "A" in PH:
                phase_A(l)
            if "B1" in PH:
                phase_B1()
            if "B2" in PH:
                phase_B2()
            if "B3" in PH:
                phase_B3()
            if "C" in PH:
                phase_C(l)
            if "D" in PH:
                phase_D(l, l == NL - 1)
        B.finish()
        print("instr counts", B.cnt, "sems", len(B.sems), flush=True)
    return nc


def _consts():
    c = {}
    c["c_ident"] = np.eye(128, dtype=np.float32)
    j = np.arange(128)
    c["c_nuincl"] = -(j[:, None] >= j[None, :]).astype(np.float32)
    pos = (np.arange(T) - 112).astype(np.float32)

    def tables(half):
        inv = (np.float32(10000.0) ** (-np.arange(half, dtype=np.float32) / np.float32(half))).astype(np.float32)
        ang = (pos[None, :] * inv[:, None]).astype(np.float32)
        return np.cos(ang).astype(np.float32), np.sin(ang).astype(np.float32)

    c32, s32 = tables(32)
    C = np.concatenate([c32, c32, c32, c32], 0)
    S = np.concatenate([-s32, s32, -s32, s32], 0)
    c["c_ropeR"] = np.ascontiguousarray(np.stack([C, S, 0.125 * C, 0.125 * S], 1).astype(np.float32))
    c16, s16 = tables(16)
    CM = np.concatenate([np.ones((64, T), np.float32), c16, c16], 0)
    SM = np.concatenate([np.zeros((64, T), np.float32), -s16, s16], 0)
    c["c_ropeM"] = np.ascontiguousarray(np.stack([CM, SM], 1).astype(np.float32))
    g = np.array(RET_GAMMA, np.float64)
    n = np.arange(128, dtype=np.float64)
    qd = np.zeros((128, 2, 128), np.float64)
    for h in range(4):
        qd[(h % 2) * 64:(h % 2) * 64 + 64, h // 2, :] = (g[h] ** (n + 1.0))[None, :]
    c["c_qdec"] = qd.astype(np.float32)
    kd = np.zeros((128, 256), np.float64)
    for h in range(4):
        kd[:, h * 64:(h + 1) * 64] = (g[h] ** (127.0 - n))[:, None]
    c["c_kdecT"] = kd.astype(np.float32)
    dt = np.zeros((128, 4, 128), np.float64)
    diff = n[None, :] - n[:, None]
    for h in range(4):
        dt[:, h, :] = np.where(diff >= 0, g[h] ** np.maximum(diff, 0.0), 0.0)
    c["c_dt"] = dt.reshape(128, 512).astype(np.float32)
    cv = np.zeros((128, 2), np.float64)
    for h in range(4):
        cv[(h % 2) * 64:(h % 2) * 64 + 64, h // 2] = g[h] ** 128.0
    c["c_cvec"] = cv.astype(np.float32)
    s_ = np.arange(128)[:, None]
    t_ = np.arange(512)[None, :]
    msb = np.zeros((128, 6, 512), np.float32)
    mml = np.zeros((128, 6, 512), np.float32)
    for cc in range(4):
        msb[:, cc, :] = ((cc * 128 + s_) < t_)
        mml[:, cc, :] = ((cc * 128 + s_) <= t_)
    valid = (s_ >= 112)
    msb[:, 4, :] = ((s_ < t_) & valid)
    msb[:, 5, :] = np.broadcast_to(valid, (128, 512))
    mml[:, 4, :] = ((s_ <= t_) & (valid | (s_ == t_)))
    mml[:, 5, :] = np.broadcast_to(valid, (128, 512))
    c["c_msb"] = msb
    c["c_mmla"] = mml
    c["c_valid"] = valid.astype(np.float32).reshape(128, 1)
    return c


def _rot_idx(nheads, d):
    half = d // 2
    idx = []
    for h in range(nheads):
        idx += [h * d + ((i + half) % d) for i in range(d)]
    return np.array(idx)


def _prep_weights(inp, NL):
    w_in = np.asarray(inp["w_in"])[:NL]
    b_sbq, b_sbk, b_sbv, b_cq, b_ckv, b_kr, b_rq, b_rk, b_rv, b_rg = 0, 512, 1024, 1536, 1920, 2176, 2208, 2464, 2720, 3232
    cols = np.concatenate([
        np.arange(b_sbq, b_sbq + 512), np.arange(b_sbk, b_sbk + 512), np.arange(b_sbv, b_sbv + 512),
        np.arange(b_cq, b_cq + 384), np.arange(b_ckv, b_ckv + 256),
        np.arange(b_kr, b_kr + 32), b_kr + _rot_idx(1, 32),
        np.arange(b_rq, b_rq + 256), b_rq + _rot_idx(4, 64),
        np.arange(b_rk, b_rk + 256), b_rk + _rot_idx(4, 64),
        np.arange(b_rv, b_rv + 512), np.arange(b_rg, b_rg + 512)])
    assert cols.size == NA
    wa = np.ascontiguousarray(np.take(w_in, cols, axis=2))
    w_uq = np.asarray(inp["w_uq"])[:NL]
    ucols = []
    for h in range(8):
        base = h * 96
        xc = list(range(base, base + 96))
        rc = list(range(base, base + 64)) + [base + 64 + ((i + 16) % 32) for i in range(32)]
        ucols += xc + rc
    wuq = np.ascontiguousarray(np.take(w_uq, np.array(ucols), axis=2))
    w_ukv = np.asarray(inp["w_ukv"])[:NL]
    kc = np.concatenate([np.arange(h * 128, h * 128 + 64) for h in range(8)] +
                        [np.arange(h * 128 + 64, h * 128 + 128) for h in range(8)])
    wukv = np.ascontiguousarray(np.take(w_ukv, kc, axis=2))
    lnp = np.zeros((9, 2, 1024), np.float32)
    lnp[0, 0] = np.asarray(inp["ln_emb_g"])
    lnp[0, 1] = np.asarray(inp["ln_emb_b"])
    for l in range(NLAYERS):
        lnp[1 + 2 * l, 0] = np.asarray(inp["ln1_g"])[l]
        lnp[1 + 2 * l, 1] = np.asarray(inp["ln1_b"])[l]
        lnp[2 + 2 * l, 0] = np.asarray(inp["ln2_g"])[l]
        lnp[2 + 2 * l, 1] = np.asarray(inp["ln2_b"])[l]
    lnp = np.ascontiguousarray(lnp.reshape(9, 2, 8, 128).transpose(3, 0, 1, 2))
    qn = np.ascontiguousarray(np.asarray(inp["mla_q_norm"]).reshape(NLAYERS, 3, 128).transpose(2, 0, 1))
    kvn = np.ascontiguousarray(np.asarray(inp["mla_kv_norm"]).reshape(NLAYERS, 2, 128).transpose(2, 0, 1))
    shared = {
        "meta": np.ascontiguousarray(np.asarray(inp["meta_tokens"], dtype=np.float32)),
        "lnp": lnp.astype(np.float32), "qn": qn.astype(np.float32), "kvn": kvn.astype(np.float32),
        "wa": wa.astype(np.float32), "wuq": wuq.astype(np.float32), "wukv": wukv.astype(np.float32),
        "wo": np.ascontiguousarray(np.asarray(inp["w_out"], dtype=np.float32)[:NL]),
        "w1": np.ascontiguousarray(np.asarray(inp["w_ff1"], dtype=np.float32)[:NL]),
        "w2": np.ascontiguousarray(np.asarray(inp["w_ff2"], dtype=np.float32)[:NL]),
    }
    shared.update(_consts())
    return shared


def run(inp, NL=NLAYERS, cores=8):
    shared = _prep_weights(inp, NL)
    x = np.asarray(inp["x"], dtype=np.float32)
    nc = build_nc(NL)
    in_maps = []
    for b in range(cores):
        m = dict(shared)
        m["x"] = np.ascontiguousarray(x[b])
        in_maps.append(m)
    if _TRACE:
        res = run_bass_kernel_spmd(nc, in_maps, core_ids=list(range(cores)), trace=True)
        print("TRACE exec_time_ns", res.exec_time_ns, flush=True)
    else:
        res = run_bass_kernel_spmd(nc, in_maps, core_ids=list(range(cores)))
    return np.stack([np.asarray(r["out"]) for r in res.results], axis=0).astype(np.float32)


def kernel(**inputs):
    return run(inputs, NLAYERS, 8)
```

```python
import numpy as np
import ml_dtypes
from contextlib import ExitStack
import concourse.bass as bass
import concourse.mybir as mybir
from concourse.bass_utils import run_bass_kernel_spmd

F32 = mybir.dt.float32
BF16 = mybir.dt.bfloat16
ALU = mybir.AluOpType
AF = mybir.ActivationFunctionType

D = 1024
SEQ = 4096
NLAYERS = 4
T = 4224
NBLK = 33
TILES = [(0, 128)] + [(128 + 512 * i, 512) for i in range(8)]
LN_EPS = 1e-5
DN_ALPHA = (2 * NLAYERS) ** 0.25
RET_GAMMA = [1.0 - 2.0 ** (-5 - h) for h in range(4)]
MLA_SCALE = 96 ** -0.5

O_SBQ, O_SBK, O_SBV, O_CQ, O_CKV, O_KR, O_KRR = 0, 512, 1024, 1536, 1920, 2176, 2208
O_RQ, O_RQR, O_RK, O_RKR, O_RV, O_RG = 2240, 2496, 2752, 3008, 3264, 3776
NA = 4288
SEM_LIM = 12000
_TRACE = False


class Builder:
    def __init__(self, nc, es):
        self.nc = nc
        self.es = es
        self.E = {"pe": nc.tensor, "act": nc.scalar, "dve": nc.vector, "pool": nc.gpsimd, "sp": nc.sync}
        self.cnt = {e: 0 for e in self.E}
        self.waited = {e: {} for e in self.E}
        self.sems = {}
        self.st = {}
        self.dq = {}
        self.flip = 0

    def sem(self, key):
        if key not in self.sems:
            self.sems[key] = self.es.enter_context(self.nc.semaphore("s%d" % len(self.sems)))
        return self.sems[key]

    def _wait(self, eng, tok):
        sk, val = tok
        w = self.waited[eng]
        if w.get(sk, 0) >= val:
            return
        w[sk] = val
        self.E[eng].wait_ge(self.sem(sk), val)

    def _deps(self, eng, r, w):
        toks = set()
        for k in r:
            s = self.st.get(k)
            if s and s[0]:
                toks.add(s[0])
        for k in w:
            s = self.st.get(k)
            if s:
                if s[0] and s[0][0][0] != eng:
                    toks.add(s[0])
                for sk, v in s[1].items():
                    if sk[0] != eng:
                        toks.add((sk, v))
        for t in toks:
            self._wait(eng, t)

    def _update(self, tok, r, w):
        sk, val = tok
        for k in r:
            s = self.st.setdefault(k, [None, {}])
            if s[1].get(sk, 0) < val:
                s[1][sk] = val
        for k in w:
            self.st[k] = [tok, {}]

    def op(self, eng, fn, r=(), w=()):
        self._deps(eng, r, w)
        ins = fn(self.E[eng])
        self.cnt[eng] += 1
        i = self.cnt[eng]
        sk = (eng, (i - 1) // SEM_LIM)
        val = (i - 1) % SEM_LIM + 1
        ins.then_inc(self.sem(sk), 1)
        self._update((sk, val), r, w)

    def dma(self, out, in_, r=(), w=(), q="sp"):
        d = self.dq.setdefault(q, {"uses": [0] * 8, "next": 0})
        slot = d["next"]
        d["next"] = (slot + 1) % 8
        sk = ("dma", q, slot)
        if d["uses"][slot] > 0:
            self._wait(q, (sk, 16 * d["uses"][slot]))
        self._deps(q, r, w)
        self.E[q].dma_start(out=out, in_=in_).then_inc(self.sem(sk), 16)
        d["uses"][slot] += 1
        self._update((sk, 16 * d["uses"][slot]), r, w)

    def all_tokens(self):
        toks = []
        for e in ("pe", "act", "dve", "pool"):
            i = self.cnt[e]
            if i > 0:
                toks.append(((e, (i - 1) // SEM_LIM), (i - 1) % SEM_LIM + 1))
        for q, d in self.dq.items():
            for slot, u in enumerate(d["uses"]):
                if u > 0:
                    toks.append((("dma", q, slot), 16 * u))
        return toks

    def barrier(self):
        toks = self.all_tokens()
        for e in self.E:
            for t in toks:
                if t[0][0] == e:
                    continue
                self._wait(e, t)

    def finish(self):
        for t in self.all_tokens():
            if t[0][0] == "dma":
                self._wait("sp", t)

    def mm(self, out, lhsT, rhs, start, stop, r, w, **kw):
        self.op("pe", lambda e: e.matmul(out, lhsT, rhs, start=start, stop=stop, **kw), r=r, w=w)

    def tr(self, out, in_, ident, r, w):
        self.op("pe", lambda e: e.transpose(out, in_, ident), r=r, w=w)

    def evac(self, out, in_, r, w, scale=None, eng=None):
        if eng is None:
            self.flip ^= 1
            eng = "act" if self.flip else "dve"
        if eng == "act":
            if scale is None:
                self.op("act", lambda e: e.copy(out=out, in_=in_), r=r, w=w)
            else:
                self.op("act", lambda e: e.mul(out=out, in_=in_, mul=scale), r=r, w=w)
        else:
            if scale is None:
                self.op("dve", lambda e: e.tensor_copy(out=out, in_=in_), r=r, w=w)
            else:
                self.op("dve", lambda e: e.tensor_scalar(out=out, in0=in_, scalar1=scale, scalar2=None,
                                                         op0=ALU.mult), r=r, w=w)


def build_nc(NL, debug=False):
    nc = bass.Bass("TRN2", target_bir_lowering=False)

    def din(name, shape, dt=F32):
        return nc.dram_tensor(name, shape, dt, kind="ExternalInput").ap()

    def dscr(name, shape, dt):
        return nc.dram_tensor(name, shape, dt, kind="Internal").ap()

    x = din("x", [SEQ, D])
    meta = din("meta", [16, D])
    lnp = din("lnp", [128, 9, 2, 8])
    qn = din("qn", [128, NLAYERS, 3])
    kvn = din("kvn", [128, NLAYERS, 2])
    wa = din("wa", [NL, D, NA])
    wuq = din("wuq", [NL, 384, 1536])
    wukv = din("wukv", [NL, 256, 1024])
    wo = din("wo", [NL, 1536, D])
    w1 = din("w1", [NL, D, 4096])
    w2 = din("w2", [NL, 4096, D])
    c_ident = din("c_ident", [128, 128])
    c_nuincl = din("c_nuincl", [128, 128])
    c_ropeR = din("c_ropeR", [128, 4, T])
    c_ropeM = din("c_ropeM", [96, 2, T])
    c_qdec = din("c_qdec", [128, 2, 128])
    c_kdecT = din("c_kdecT", [128, 256])
    c_dt = din("c_dt", [128, 512])
    c_cvec = din("c_cvec", [128, 2])
    c_msb = din("c_msb", [128, 6, 512])
    c_mmla = din("c_mmla", [128, 6, 512])
    c_valid = din("c_valid", [128, 1])
    out = nc.dram_tensor("out", [SEQ, D], F32, kind="ExternalOutput").ap()

    hres = dscr("hres", [128, 8, T], F32)
    hT = dscr("hT", [128, 8, T], BF16)
    sbq = dscr("sbq", [128, 4, T], BF16)
    sbk = dscr("sbk", [128, 4, T], BF16)
    sbv = dscr("sbv", [128, NBLK, 512], BF16)
    mq = dscr("mq", [128, 8, T], BF16)
    mk = dscr("mk", [128, 8, T], BF16)
    mv = dscr("mv", [128, NBLK, 512], BF16)
    rq = dscr("rq", [128, 2, T], BF16)
    rqd = dscr("rqd", [128, 2, T], BF16)
    rk = dscr("rk", [128, 2, T], BF16)
    rkd = dscr("rkd", [128, NBLK, 256], BF16)
    rv = dscr("rv", [128, NBLK, 512], BF16)
    rg = dscr("rg", [128, 4, T], F32)
    mix = dscr("mix", [128, 12, T], BF16)

    with ExitStack() as es:
        B = Builder(nc, es)

        uniq = {"n": 0}

        def sb(stack, name, shape, dt):
            uniq["n"] += 1
            return stack.enter_context(nc.sbuf_tensor("%s_%d" % (name, uniq["n"]), shape, dt))

        ps = [es.enter_context(nc.psum_tensor("ps%d" % i, [128, 512], F32)) for i in range(8)]
        rot = {"i": 0}

        def bank(lo=0, hi=8):
            n = hi - lo
            b = lo + rot["i"] % n
            rot["i"] += 1
            return b

        identF = sb(es, "identF", [128, 128], F32)
        identB = sb(es, "identB", [128, 128], BF16)
        onesF = sb(es, "onesF", [128, 128], F32)
        onesB = sb(es, "onesB", [128, 64], BF16)
        RS = sb(es, "RS", [128, 512], F32)
        lnps = sb(es, "lnps", [128, 9, 2, 8], F32)
        qns = sb(es, "qns", [128, NLAYERS, 3], F32)
        kvns = sb(es, "kvns", [128, NLAYERS, 2], F32)
        B.dma(identF[:], c_ident, w=["identF"])
        B.dma(identB[:], c_ident, w=["identB"], q="pool")
        B.dma(lnps[:], lnp, w=["lnps"])
        B.dma(qns[:], qn, w=["qns"])
        B.dma(kvns[:], kvn, w=["kvns"])
        B.op("pool", lambda e: e.memset(onesF[:], 1.0), w=["onesF"])
        B.op("pool", lambda e: e.memset(onesB[:], 1.0), w=["onesB"])
        with ExitStack() as z0:
            ZR = sb(z0, "ZR", [32, T], BF16)
            B.op("pool", lambda e: e.memset(ZR[:], 0.0), w=["ZR"])
            B.dma(mq[96:128, :, :], ZR[:].unsqueeze(1).broadcast_to([32, 8, T]), r=["ZR"], w=["mqz"])
            B.dma(mk[96:128, :, :], ZR[:].unsqueeze(1).broadcast_to([32, 8, T]), r=["ZR"], w=["mkz"])
            B.barrier()

        def ln_f(ph, X, kx, SQ, ksq, R, XB, W, lnidx, t0, final):
            xs = X[:, :, 0:W]
            mp = bank()
            B.op("dve", lambda e: e.reduce_sum(out=RS[:, 0:W], in_=xs.rearrange("p c n -> p n c"),
                                               axis=mybir.AxisListType.X), r=[kx], w=["RS"])
            B.mm(ps[mp][:, 0:W], onesF[:], RS[:, 0:W], True, True, r=["RS", "onesF"], w=[("ps", mp)])
            B.op("dve", lambda e: e.scalar_tensor_tensor(
                out=xs, in0=ps[mp][:, 0:W].unsqueeze(1).broadcast_to([128, 8, W]), scalar=-1.0 / D, in1=xs,
                op0=ALU.mult, op1=ALU.add), r=[("ps", mp), kx], w=[kx])
            B.op("act", lambda e: e.activation(out=SQ[:, :, 0:W], in_=xs, func=AF.Square), r=[kx], w=ksq)
            vp = bank()
            B.op("dve", lambda e: e.reduce_sum(out=RS[:, 0:W], in_=SQ[:, :, 0:W].rearrange("p c n -> p n c"),
                                               axis=mybir.AxisListType.X), r=ksq, w=["RS"])
            B.mm(ps[vp][:, 0:W], onesF[:], RS[:, 0:W], True, True, r=["RS", "onesF"], w=[("ps", vp)])
            B.op("act", lambda e: e.activation(out=R[:, 0:W], in_=ps[vp][:, 0:W], func=AF.Ln, scale=1.0 / D,
                                               bias=LN_EPS), r=[("ps", vp)], w=["R"])
            B.op("act", lambda e: e.activation(out=R[:, 0:W], in_=R[:, 0:W], func=AF.Exp, scale=-0.5),
                 r=["R"], w=["R"])
            B.op("dve", lambda e: e.tensor_tensor(out=xs, in0=xs, in1=R[:, 0:W].unsqueeze(1).broadcast_to([128, 8, W]),
                                                  op=ALU.mult), r=[kx, "R"], w=[kx])
            for c in range(8):
                B.op("act", lambda e, c=c: e.activation(out=X[:, c, 0:W], in_=X[:, c, 0:W], func=AF.Identity,
                                                        scale=lnps[:, lnidx, 0, c:c + 1],
                                                        bias=lnps[:, lnidx, 1, c:c + 1]),
                     r=[kx, "lnps"], w=[kx])
            B.op("pool", lambda e: e.tensor_copy(out=XB[:, :, 0:W], in_=xs), r=[kx], w=["XB"])
            hk = ("h", t0)
            B.dma(hres[:, :, t0:t0 + W], xs, r=[kx], w=[("hres", t0)])
            B.dma(hT[:, :, t0:t0 + W], XB[:, :, 0:W], r=["XB"], w=[("hT", t0)])
            if final and t0 >= 128:
                for s in range(W // 128):
                    OT = ph["OT"][s % 2]
                    ko = "OT%d" % (s % 2)
                    for half in range(2):
                        b = bank()
                        for cc in range(4):
                            c = half * 4 + cc
                            B.tr(ps[b][:, cc * 128:(cc + 1) * 128], X[:, c, s * 128:(s + 1) * 128], identF[:],
                                 r=[kx, "identF"], w=[("ps", b)])
                        B.evac(OT[:, half * 512:(half + 1) * 512], ps[b][:], r=[("ps", b)], w=[ko])
                    tok0 = t0 - 128 + s * 128
                    B.dma(out[tok0:tok0 + 128, :], OT[:], r=[ko], w=[("out", tok0)])

        def phase_embed():
            with ExitStack() as ph:
                xt = [sb(ph, "e_xt%d" % i, [128, 4, D], F32) for i in range(2)]
                X = [sb(ph, "e_X%d" % i, [128, 8, 512], F32) for i in range(2)]
                SQ = sb(ph, "e_SQ", [128, 8, 512], F32)
                R = sb(ph, "e_R", [128, 512], F32)
                XB = sb(ph, "e_XB", [128, 8, 512], BF16)
                for ti, (t0, W) in enumerate(TILES):
                    a = xt[ti % 2]
                    ka = "e_xt%d" % (ti % 2)
                    kx = "e_X%d" % (ti % 2)
                    nsub = W // 128
                    if ti == 0:
                        B.op("pool", lambda e: e.memset(a[:, 0, :], 0.0), w=[ka])
                        B.dma(a[112:128, 0, :], meta, w=[ka])
                    else:
                        B.dma(a[:], x[t0 - 128:t0 - 128 + 512, :].rearrange("(s p) d -> p s d", p=128), w=[ka])
                    for c in range(8):
                        b = bank()
                        for s in range(nsub):
                            B.tr(ps[b][:, s * 128:(s + 1) * 128], a[:, s, c * 128:(c + 1) * 128], identF[:],
                                 r=[ka, "identF"], w=[("ps", b)])
                        B.evac(X[ti % 2][:, c, 0:W], ps[b][:, 0:W], r=[("ps", b)], w=[kx])
                    ln_f(None, X[ti % 2], kx, SQ[:], ["SQ"], R, XB, W, 0, t0, False)
                B.barrier()

        def phase_A(l):
            with ExitStack() as ph:
                WA = sb(ph, "a_WA", [128, 8, NA], BF16)
                WUQ = sb(ph, "a_WUQ", [128, 3, 1536], BF16)
                WUKV = sb(ph, "a_WUKV", [128, 2, 1024], BF16)
                for c in range(8):
                    B.dma(WA[:, c, :], wa[l, c * 128:(c + 1) * 128, :], w=["WA"], q="pool")
                B.dma(WUQ[:], wuq[l].rearrange("(c p) n -> p c n", p=128), w=["WUQ"], q="pool")
                B.dma(WUKV[:], wukv[l].rearrange("(c p) n -> p c n", p=128), w=["WUKV"], q="pool")
                QDEC = sb(ph, "a_QDEC", [128, 2, 128], F32)
                KDEC = sb(ph, "a_KDEC", [128, 256], F32)
                B.dma(QDEC[:], c_qdec, w=["QDEC"])
                B.dma(KDEC[:], c_kdecT, w=["KDEC"])
                HT = [sb(ph, "a_HT%d" % i, [128, 8, 512], BF16) for i in range(2)]
                RT = [sb(ph, "a_RT%d" % i, [128, 4, 512], F32) for i in range(2)]
                MT = [sb(ph, "a_MT%d" % i, [96, 2, 512], F32) for i in range(2)]
                MK = [sb(ph, "a_MK%d" % i, [32, 2, 512], F32) for i in range(2)]
                CQ = sb(ph, "a_CQ", [128, 3, 512], F32)
                CKV = sb(ph, "a_CKV", [128, 2, 512], F32)
                SQ = sb(ph, "a_SQ", [128, 3, 512], F32)
                R = sb(ph, "a_R", [128, 512], F32)
                CQN = sb(ph, "a_CQN", [128, 3, 512], BF16)
                CKVN = sb(ph, "a_CKVN", [128, 2, 512], BF16)
                T1 = [sb(ph, "a_T1%d" % i, [128, 512], F32) for i in range(3)]
                T2 = [sb(ph, "a_T2%d" % i, [128, 512], F32) for i in range(3)]
                STG = [sb(ph, "a_STG%d" % i, [128, 512], BF16) for i in range(6)]
                RKB = [sb(ph, "a_RKB%d" % i, [128, 512], BF16) for i in range(2)]
                GS = [sb(ph, "a_GS0", [128, 4, 512], F32)] * 2
                KST = [sb(ph, "a_KST%d" % i, [128, 256], BF16) for i in range(2)]
                cn = {"stg": 0, "t": 0, "kst": 0}

                def stage():
                    i = cn["stg"] % 6
                    cn["stg"] += 1
                    return STG[i], "STG%d" % i

                def tt():
                    i = cn["t"] % 3
                    cn["t"] += 1
                    return T1[i], "T1%d" % i, T2[i], "T2%d" % i

                def load_tile(ti):
                    t0, W = TILES[ti]
                    p = ti % 2
                    B.dma(HT[p][:, :, 0:W], hT[:, :, t0:t0 + W], r=[("hT", t0)], w=["HT%d" % p])
                    B.dma(RT[p][:, :, 0:W], c_ropeR[:, :, t0:t0 + W], w=["RT%d" % p])
                    B.dma(MT[p][:, :, 0:W], c_ropeM[:, :, t0:t0 + W], w=["MT%d" % p])
                    B.dma(MK[p][:, :, 0:W], c_ropeM[64:96, :, t0:t0 + W], w=["MK%d" % p])

                load_tile(0)
                for ti, (t0, W) in enumerate(TILES):
                    p = ti % 2
                    if ti + 1 < len(TILES):
                        load_tile(ti + 1)
                    H = HT[p]
                    kh = "HT%d" % p
                    nsub = W // 128
                    blk0 = t0 // 128

                    def fgroup(col0, M, Wm=WA, kw="WA", rhs=None, krhs=None, nchunk=8):
                        b = bank()
                        for c in range(nchunk):
                            rr = H[:, c, 0:W] if rhs is None else rhs[:, c, 0:W]
                            B.mm(ps[b][0:M, 0:W], Wm[:, c, col0:col0 + M], rr, c == 0, c == nchunk - 1,
                                 r=[kw, kh if krhs is None else krhs], w=[("ps", b)])
                        return b

                    for m in range(4):
                        b = fgroup(O_SBQ + m * 128, 128)
                        S_, ks = stage()
                        B.evac(S_[:, 0:W], ps[b][:, 0:W], r=[("ps", b)], w=[ks], scale=0.125)
                        B.dma(sbq[:, m, t0:t0 + W], S_[:, 0:W], r=[ks], w=[("sbq", ti)])
                    for m in range(4):
                        b = fgroup(O_SBK + m * 128, 128)
                        S_, ks = stage()
                        B.evac(S_[:, 0:W], ps[b][:, 0:W], r=[("ps", b)], w=[ks])
                        B.dma(sbk[:, m, t0:t0 + W], S_[:, 0:W], r=[ks], w=[("sbk", ti)])
                    for (col0, dst, nm) in ((O_SBV, sbv, "sbv"), (O_RV, rv, "rv")):
                        for s in range(nsub):
                            b = bank()
                            for c in range(8):
                                B.mm(ps[b][:, :], H[:, c, s * 128:(s + 1) * 128], WA[:, c, col0:col0 + 512],
                                     c == 0, c == 7, r=["WA", kh], w=[("ps", b)])
                            S_, ks = stage()
                            B.evac(S_[:, :], ps[b][:, :], r=[("ps", b)], w=[ks])
                            B.dma(dst[:, blk0 + s, :], S_[:, :], r=[ks], w=[(nm, ti)])
                    for m in range(3):
                        b = fgroup(O_CQ + m * 128, 128)
                        B.evac(CQ[:, m, 0:W], ps[b][:, 0:W], r=[("ps", b)], w=["CQ"])
                    for m in range(2):
                        b = fgroup(O_CKV + m * 128, 128)
                        B.evac(CKV[:, m, 0:W], ps[b][:, 0:W], r=[("ps", b)], w=["CKV"])
                    ba = fgroup(O_KR, 32)
                    bb = fgroup(O_KRR, 32)
                    t1, k1, t2, k2 = tt()
                    B.op("dve", lambda e: e.tensor_tensor(out=t1[0:32, 0:W], in0=ps[ba][0:32, 0:W], in1=MK[p][:, 0, 0:W],
                                                          op=ALU.mult), r=[("ps", ba), "MK%d" % p], w=[k1])
                    B.op("dve", lambda e: e.tensor_tensor(out=t2[0:32, 0:W], in0=ps[bb][0:32, 0:W], in1=MK[p][:, 1, 0:W],
                                                          op=ALU.mult), r=[("ps", bb), "MK%d" % p], w=[k2])
                    S_, ks = stage()
                    B.op("pool", lambda e: e.tensor_tensor(out=S_[0:32, 0:W], in0=t1[0:32, 0:W], in1=t2[0:32, 0:W],
                                                           op=ALU.add), r=[k1, k2], w=[ks])
                    B.dma(mk[64:96, :, t0:t0 + W], S_[0:32, 0:W].unsqueeze(1).broadcast_to([32, 8, W]),
                          r=[ks], w=[("mk", ti)])
                    for pr in range(2):
                        ba = fgroup(O_RQ + pr * 128, 128)
                        bb = fgroup(O_RQR + pr * 128, 128)
                        t1, k1, t2, k2 = tt()
                        B.op("dve", lambda e: e.tensor_tensor(out=t1[:, 0:W], in0=ps[ba][:, 0:W], in1=RT[p][:, 0, 0:W],
                                                              op=ALU.mult), r=[("ps", ba), "RT%d" % p], w=[k1])
                        B.op("dve", lambda e: e.tensor_tensor(out=t2[:, 0:W], in0=ps[bb][:, 0:W], in1=RT[p][:, 1, 0:W],
                                                              op=ALU.mult), r=[("ps", bb), "RT%d" % p], w=[k2])
                        B.op("pool", lambda e: e.tensor_tensor(out=t1[:, 0:W], in0=t1[:, 0:W], in1=t2[:, 0:W],
                                                               op=ALU.add), r=[k1, k2], w=[k1])
                        S_, ks = stage()
                        B.op("act", lambda e: e.copy(out=S_[:, 0:W], in_=t1[:, 0:W]), r=[k1], w=[ks])
                        B.dma(rq[:, pr, t0:t0 + W], S_[:, 0:W], r=[ks], w=[("rq", ti)])
                        S2, ks2 = stage()
                        B.op("pool", lambda e: e.tensor_tensor(
                            out=S2[:, 0:W].rearrange("p (s n) -> p s n", n=128),
                            in0=t1[:, 0:W].rearrange("p (s n) -> p s n", n=128),
                            in1=QDEC[:, pr, :].unsqueeze(1).broadcast_to([128, nsub, 128]), op=ALU.mult),
                            r=[k1, "QDEC"], w=[ks2])
                        B.dma(rqd[:, pr, t0:t0 + W], S2[:, 0:W], r=[ks2], w=[("rqd", ti)])
                    for pr in range(2):
                        ba = fgroup(O_RK + pr * 128, 128)
                        bb = fgroup(O_RKR + pr * 128, 128)
                        t1, k1, t2, k2 = tt()
                        B.op("dve", lambda e: e.tensor_tensor(out=t1[:, 0:W], in0=ps[ba][:, 0:W], in1=RT[p][:, 2, 0:W],
                                                              op=ALU.mult), r=[("ps", ba), "RT%d" % p], w=[k1])
                        B.op("dve", lambda e: e.tensor_tensor(out=t2[:, 0:W], in0=ps[bb][:, 0:W], in1=RT[p][:, 3, 0:W],
                                                              op=ALU.mult), r=[("ps", bb), "RT%d" % p], w=[k2])
                        kb_ = "RKB%d" % pr
                        B.op("pool", lambda e: e.tensor_tensor(out=RKB[pr][:, 0:W], in0=t1[:, 0:W], in1=t2[:, 0:W],
                                                               op=ALU.add), r=[k1, k2], w=[kb_])
                        if ti == 0:
                            B.op("pool", lambda e: e.memset(RKB[pr][:, 0:112], 0.0), r=[kb_], w=[kb_])
                        B.dma(rk[:, pr, t0:t0 + W], RKB[pr][:, 0:W], r=[kb_], w=[("rk", ti)])
                    for s in range(nsub):
                        b = bank()
                        pb = ps[b][:].bitcast(BF16)
                        for pr in range(2):
                            B.tr(pb[:, pr * 128:(pr + 1) * 128], RKB[pr][:, s * 128:(s + 1) * 128], identB[:],
                                 r=["RKB%d" % pr, "identB"], w=[("ps", b)])
                        i = cn["kst"] % 2
                        cn["kst"] += 1
                        B.op("dve", lambda e: e.tensor_tensor(out=KST[i][:], in0=pb[:, 0:256], in1=KDEC[:], op=ALU.mult),
                             r=[("ps", b), "KDEC"], w=["KST%d" % i])
                        B.dma(rkd[:, blk0 + s, :], KST[i][:], r=["KST%d" % i], w=[("rkd", ti)])
                    G = GS[p]
                    for m in range(4):
                        b = fgroup(O_RG + m * 128, 128)
                        B.op("act", lambda e: e.activation(out=G[:, m, 0:W], in_=ps[b][:, 0:W], func=AF.Silu),
                             r=[("ps", b)], w=["GS"])
                    B.dma(rg[:, :, t0:t0 + W], G[:, :, 0:W], r=["GS"], w=[("rg", ti)])
                    for (src, ksrc, nch, dstn, kd, gam, inv) in ((CQ, "CQ", 3, CQN, "CQN", qns, 1.0 / 384),
                                                                   (CKV, "CKV", 2, CKVN, "CKVN", kvns, 1.0 / 256)):
                        B.op("act", lambda e: e.activation(out=SQ[:, 0:nch, 0:W], in_=src[:, :, 0:W], func=AF.Square),
                             r=[ksrc], w=["SQ"])
                        b = bank()
                        for m in range(nch):
                            B.mm(ps[b][:, 0:W], onesF[:], SQ[:, m, 0:W], m == 0, m == nch - 1, r=["SQ", "onesF"],
                                 w=[("ps", b)])
                        B.op("act", lambda e: e.activation(out=R[:, 0:W], in_=ps[b][:, 0:W], func=AF.Ln, scale=inv,
                                                           bias=LN_EPS), r=[("ps", b)], w=["R"])
                        B.op("act", lambda e: e.activation(out=R[:, 0:W], in_=R[:, 0:W], func=AF.Exp, scale=-0.5),
                             r=["R"], w=["R"])
                        for m in range(nch):
                            B.op("dve", lambda e, m=m: e.scalar_tensor_tensor(
                                out=dstn[:, m, 0:W], in0=src[:, m, 0:W], scalar=gam[:, l, m:m + 1], in1=R[:, 0:W],
                                op0=ALU.mult, op1=ALU.mult), r=[ksrc, "R", "qns", "kvns"], w=[kd])
                    for h in range(8):
                        ba = fgroup((2 * h) * 96, 96, Wm=WUQ, kw="WUQ", rhs=CQN, krhs="CQN", nchunk=3)
                        bb = fgroup((2 * h + 1) * 96, 96, Wm=WUQ, kw="WUQ", rhs=CQN, krhs="CQN", nchunk=3)
                        t1, k1, t2, k2 = tt()
                        B.op("dve", lambda e: e.tensor_tensor(out=t1[0:96, 0:W], in0=ps[ba][0:96, 0:W], in1=MT[p][:, 0, 0:W],
                                                              op=ALU.mult), r=[("ps", ba), "MT%d" % p], w=[k1])
                        B.op("dve", lambda e: e.tensor_tensor(out=t2[0:96, 0:W], in0=ps[bb][0:96, 0:W], in1=MT[p][:, 1, 0:W],
                                                              op=ALU.mult), r=[("ps", bb), "MT%d" % p], w=[k2])
                        S_, ks = stage()
                        B.op("pool", lambda e: e.tensor_tensor(out=S_[0:96, 0:W], in0=t1[0:96, 0:W], in1=t2[0:96, 0:W],
                                                               op=ALU.add), r=[k1, k2], w=[ks])
                        B.dma(mq[0:96, h, t0:t0 + W], S_[0:96, 0:W], r=[ks], w=[("mq", ti)])
                    for m in range(4):
                        b = fgroup(m * 128, 128, Wm=WUKV, kw="WUKV", rhs=CKVN, krhs="CKVN", nchunk=2)
                        S_, ks = stage()
                        B.evac(S_[:, 0:W], ps[b][:, 0:W], r=[("ps", b)], w=[ks])
                        B.dma(mk[0:64, 2 * m, t0:t0 + W], S_[0:64, 0:W], r=[ks], w=[("mk", ti)])
                        B.dma(mk[0:64, 2 * m + 1, t0:t0 + W], S_[64:128, 0:W], r=[ks], w=[("mk", ti)])
                    for s in range(nsub):
                        b = bank()
                        for c in range(2):
                            B.mm(ps[b][:, :], CKVN[:, c, s * 128:(s + 1) * 128], WUKV[:, c, 512:1024], c == 0, c == 1,
                                 r=["WUKV", "CKVN"], w=[("ps", b)])
                        S_, ks = stage()
                        B.evac(S_[:, :], ps[b][:, :], r=[("ps", b)], w=[ks])
                        B.dma(mv[:, blk0 + s, :], S_[:, :], r=[ks], w=[("mv", ti)])
                B.barrier()

        def phase_B1():
            with ExitStack() as ph:
                UIb = sb(ph, "b_UIb", [128, 128], BF16)
                ONb = sb(ph, "b_ONb", [128, 128], BF16)
                MS = sb(ph, "b_MS", [128, 6, 512], F32)
                B.dma(UIb[:], c_nuincl, w=["UIb"], q="pool")
                B.op("pool", lambda e: e.memset(ONb[:], -1.0), w=["ONb"])
                B.dma(MS[:], c_msb, w=["MS"])
                QP = [sb(ph, "b_QP%d" % i, [128, T], BF16) for i in range(2)]
                KA = [sb(ph, "b_KA%d" % i, [128, T], BF16) for i in range(2)]
                KB = [sb(ph, "b_KB%d" % i, [128, T], BF16) for i in range(2)]
                VP = [sb(ph, "b_VP%d" % i, [128, NBLK, 128], BF16) for i in range(2)]
                for i in range(2):
                    B.op("pool", lambda e: e.memset(KA[i][64:128, :], 0.0), w=["KA%d" % i])
                    B.op("pool", lambda e: e.memset(KB[i][0:64, :], 0.0), w=["KB%d" % i])
                NR = 10
                EB = [sb(ph, "b_E%d" % i, [128, 512], F32) for i in range(NR)]
                LK = [sb(ph, "b_LK%d" % i, [128, 512], F32) for i in range(NR)]
                HI = [sb(ph, "b_HI%d" % i, [128, 512], BF16) for i in range(NR)]
                LO = [sb(ph, "b_LO%d" % i, [128, 512], BF16) for i in range(NR)]
                SS = [sb(ph, "b_S%d" % i, [128, 512], F32) for i in range(NR)]
                WW = [sb(ph, "b_W%d" % i, [128, 512], BF16) for i in range(NR)]
                CC = [sb(ph, "b_C%d" % i, [128, 512], F32) for i in range(2)]
                OS = [sb(ph, "b_OS%d" % i, [128, 512], BF16) for i in range(2)]
                allt = list(range(9))

                def load_pair(m):
                    p = m % 2
                    B.dma(QP[p][:], sbq[:, m, :], r=[("sbq", i) for i in allt], w=["QP%d" % p])
                    B.dma(KA[p][0:64, :], sbk[0:64, m, :], r=[("sbk", i) for i in allt], w=["KA%d" % p])
                    B.dma(KB[p][64:128, :], sbk[64:128, m, :], r=[("sbk", i) for i in allt], w=["KB%d" % p])
                    B.dma(VP[p][:], sbv[:, :, m * 128:(m + 1) * 128], r=[("sbv", i) for i in allt], w=["VP%d" % p])

                jobs = []
                for h in range(8):
                    for ti, (t0, W) in enumerate(TILES):
                        nkb = (t0 + W) // 128
                        for kb in reversed(range(nkb)):
                            jobs.append((h, ti, kb, kb == nkb - 1, kb == 0))
                njobs = len(jobs)

                def kop(h):
                    p = (h // 2) % 2
                    return (KA[p], "KA%d" % p) if h % 2 == 0 else (KB[p], "KB%d" % p)

                def mask_idx(ti, kb, t0):
                    c = kb - t0 // 128
                    if ti == 0:
                        return 4
                    if c >= 0:
                        return c
                    if kb == 0:
                        return 5
                    return None

                def colo(ti, kb, t0):
                    c = kb - t0 // 128
                    return c * 128 if (ti > 0 and c > 0) else 0

                def stage1(ji):
                    h, ti, kb, first, last = jobs[ji]
                    t0, W = TILES[ti]
                    c0 = colo(ti, kb, t0)
                    p = (h // 2) % 2
                    Kx, kk = kop(h)
                    i = ji % NR
                    zb = ji % 2
                    B.mm(ps[zb][:, c0:W], Kx[:, kb * 128:(kb + 1) * 128], QP[p][:, t0 + c0:t0 + W], True, True,
                         r=[kk, "QP%d" % p], w=[("ps", zb)])
                    B.op("act", lambda e: e.activation(out=EB[i][:, c0:W], in_=ps[zb][:, c0:W], func=AF.Exp),
                         r=[("ps", zb)], w=["E%d" % i])
                    B.op("act", lambda e: e.activation(out=LK[i][:, c0:W], in_=EB[i][:, c0:W], func=AF.Ln, bias=1.0),
                         r=["E%d" % i], w=["LK%d" % i])

                def stage1b(ji):
                    h, ti, kb, first, last = jobs[ji]
                    t0, W = TILES[ti]
                    c0 = colo(ti, kb, t0)
                    i = ji % NR
                    mi = mask_idx(ti, kb, t0)
                    if mi is not None:
                        B.op("pool", lambda e: e.tensor_tensor(out=LK[i][:, c0:W], in0=LK[i][:, c0:W], in1=MS[:, mi, c0:W],
                                                               op=ALU.mult), r=["LK%d" % i, "MS"], w=["LK%d" % i])
                    B.op("dve", lambda e: e.tensor_copy(out=HI[i][:, c0:W], in_=LK[i][:, c0:W]),
                         r=["LK%d" % i], w=["HI%d" % i])
                    B.op("pool", lambda e: e.tensor_tensor(out=LO[i][:, c0:W], in0=LK[i][:, c0:W], in1=HI[i][:, c0:W],
                                                           op=ALU.subtract),
                         r=["LK%d" % i, "HI%d" % i], w=["LO%d" % i])

                def stage2(ji):
                    h, ti, kb, first, last = jobs[ji]
                    t0, W = TILES[ti]
                    c0 = colo(ti, kb, t0)
                    p = (h // 2) % 2
                    Kx, kk = kop(h)
                    i = ji % NR
                    eb = 2 + ji % 2
                    tb = 4 + ji % 2
                    cp = (h * 9 + ti) % 2
                    Cc, kc = CC[cp], "C%d" % cp
                    B.mm(ps[eb][:, c0:W], UIb[:], HI[i][:, c0:W], True, False, r=["UIb", "HI%d" % i], w=[("ps", eb)])
                    B.mm(ps[eb][:, c0:W], UIb[:], LO[i][:, c0:W], False, False, r=["UIb", "LO%d" % i], w=[("ps", eb)])
                    B.mm(ps[eb][:, c0:W], Kx[:, kb * 128:(kb + 1) * 128], QP[p][:, t0 + c0:t0 + W], False, True,
                         r=[kk, "QP%d" % p], w=[("ps", eb)])
                    if not last:
                        B.mm(ps[tb][:, c0:W], ONb[:], HI[i][:, c0:W], True, False, r=["ONb", "HI%d" % i], w=[("ps", tb)])
                        B.mm(ps[tb][:, c0:W], ONb[:], LO[i][:, c0:W], False, True, r=["ONb", "LO%d" % i], w=[("ps", tb)])

                def stage2d(ji):
                    h, ti, kb, first, last = jobs[ji]
                    t0, W = TILES[ti]
                    c0 = colo(ti, kb, t0)
                    i = ji % NR
                    eb = 2 + ji % 2
                    tb = 4 + ji % 2
                    cp = (h * 9 + ti) % 2
                    Cc, kc = CC[cp], "C%d" % cp
                    if first:
                        B.op("dve", lambda e: e.memset(Cc[:, 0:W], 0.0), w=[kc])
                    B.op("dve", lambda e: e.tensor_tensor(out=SS[i][:, c0:W], in0=Cc[:, c0:W], in1=ps[eb][:, c0:W],
                                                          op=ALU.add), r=[("ps", eb), kc], w=["S%d" % i])
                    if not last:
                        B.op("dve", lambda e: e.tensor_tensor(out=Cc[:, c0:W], in0=Cc[:, c0:W], in1=ps[tb][:, c0:W],
                                                              op=ALU.add), r=[("ps", tb), kc], w=[kc])

                def stage2a(ji):
                    h, ti, kb, first, last = jobs[ji]
                    t0, W = TILES[ti]
                    c0 = colo(ti, kb, t0)
                    i = ji % NR
                    B.op("act", lambda e: e.activation(out=WW[i][:, c0:W], in_=SS[i][:, c0:W], func=AF.Exp),
                         r=["S%d" % i], w=["W%d" % i])
                    mi = mask_idx(ti, kb, t0)
                    if mi is not None:
                        B.op("pool", lambda e: e.tensor_tensor(out=WW[i][:, c0:W], in0=WW[i][:, c0:W], in1=MS[:, mi, c0:W],
                                                               op=ALU.mult), r=["W%d" % i, "MS"], w=["W%d" % i])

                def stage3(ji):
                    h, ti, kb, first, last = jobs[ji]
                    t0, W = TILES[ti]
                    c0 = colo(ti, kb, t0)
                    p = (h // 2) % 2
                    i = ji % NR
                    ob = 6 + (h * 9 + ti) % 2
                    if ti == 0 and first and h % 2 == 0 and h + 2 < 8:
                        load_pair(h // 2 + 1)
                    B.mm(ps[ob][:, c0:W], VP[p][:, kb, :], WW[i][:, c0:W], first, last,
                         r=["VP%d" % p, "W%d" % i], w=[("ps", ob)], skip_group_check=True)
                    if last:
                        oi = (h * 9 + ti) % 2
                        r0 = (h % 2) * 64
                        B.evac(OS[oi][r0:r0 + 64, 0:W], ps[ob][r0:r0 + 64, 0:W], r=[("ps", ob)], w=["OS%d" % oi])
                        B.dma(mix[r0:r0 + 64, h // 2, t0:t0 + W], OS[oi][r0:r0 + 64, 0:W], r=["OS%d" % oi],
                              w=[("mix", ti)])

                load_pair(0)
                stages = [(stage1, 0), (stage1b, 1), (stage2, 3), (stage2d, 4), (stage2a, 5), (stage3, 7)]
                for step in range(njobs + 7):
                    for fn, lag in stages:
                        if 0 <= step - lag < njobs:
                            fn(step - lag)
                B.barrier()

        def phase_B2():
            with ExitStack() as ph:
                MM = sb(ph, "m_MM", [128, 6, 512], F32)
                ONb = sb(ph, "m_ONb", [128, 128], BF16)
                B.dma(MM[:], c_mmla, w=["MM"])
                B.op("pool", lambda e: e.memset(ONb[:], 1.0), w=["mONb"])
                QT = [sb(ph, "m_QT%d" % i, [128, T], BF16) for i in range(2)]
                KT = [sb(ph, "m_KT%d" % i, [128, T], BF16) for i in range(2)]
                VP = [sb(ph, "m_VP%d" % i, [128, NBLK, 128], BF16) for i in range(2)]
                NR = 5
                PP = [sb(ph, "m_P%d" % i, [128, 512], BF16) for i in range(NR)]
                RC = [sb(ph, "m_RC%d" % i, [128, 512], F32) for i in range(2)]
                OS = [sb(ph, "m_OS%d" % i, [128, 512], BF16) for i in range(2)]
                allt = list(range(9))

                def load_head(h):
                    p = h % 2
                    B.dma(QT[p][:], mq[:, h, :], r=[("mq", i) for i in allt] + ["mqz"], w=["mQT%d" % p])
                    B.dma(KT[p][:], mk[:, h, :], r=[("mk", i) for i in allt] + ["mkz"], w=["mKT%d" % p])
                    if h % 2 == 0:
                        pp = (h // 2) % 2
                        B.dma(VP[pp][:], mv[:, :, (h // 2) * 128:(h // 2 + 1) * 128], r=[("mv", i) for i in allt],
                              w=["mVP%d" % pp])

                jobs = []
                for h in range(8):
                    for ti, (t0, W) in enumerate(TILES):
                        nkb = (t0 + W) // 128
                        for kb in reversed(range(nkb)):
                            jobs.append((h, ti, kb, kb == nkb - 1, kb == 0))
                njobs = len(jobs)

                def stage1(ji):
                    h, ti, kb, first, last = jobs[ji]
                    t0, W = TILES[ti]
                    cc_ = kb - t0 // 128
                    c0 = cc_ * 128 if (ti > 0 and cc_ > 0) else 0
                    p = h % 2
                    i = ji % NR
                    sbk_ = ji % 3
                    B.mm(ps[sbk_][:, c0:W], KT[p][:, kb * 128:(kb + 1) * 128], QT[p][:, t0 + c0:t0 + W], True, True,
                         r=["mKT%d" % p, "mQT%d" % p], w=[("ps", sbk_)])
                    B.op("act", lambda e: e.activation(out=PP[i][:, c0:W], in_=ps[sbk_][:, c0:W], func=AF.Exp,
                                                       scale=MLA_SCALE), r=[("ps", sbk_)], w=["P%d" % i])
                    c = kb - t0 // 128
                    mi = 4 if ti == 0 else (c if c >= 0 else (5 if kb == 0 else None))
                    if mi is not None:
                        B.op("dve", lambda e: e.tensor_tensor(out=PP[i][:, c0:W], in0=PP[i][:, c0:W], in1=MM[:, mi, c0:W],
                                                              op=ALU.mult), r=["P%d" % i, "MM"], w=["P%d" % i])

                def stage2(ji):
                    h, ti, kb, first, last = jobs[ji]
                    t0, W = TILES[ti]
                    cc_ = kb - t0 // 128
                    c0 = cc_ * 128 if (ti > 0 and cc_ > 0) else 0
                    pp = (h // 2) % 2
                    i = ji % NR
                    par = (h * 9 + ti) % 2
                    ob = 4 + par
                    db = 6 + par
                    if ti == 0 and first and h + 1 < 8:
                        load_head(h + 1)
                    B.mm(ps[ob][:, c0:W], VP[pp][:, kb, :], PP[i][:, c0:W], first, last,
                         r=["mVP%d" % pp, "P%d" % i], w=[("ps", ob)], skip_group_check=True)
                    B.mm(ps[db][:, c0:W], ONb[:], PP[i][:, c0:W], first, last,
                         r=["mONb", "P%d" % i], w=[("ps", db)], skip_group_check=True)
                    if last:
                        r0 = (h % 2) * 64
                        B.op("dve", lambda e: e.reciprocal(out=RC[par][r0:r0 + 64, 0:W], in_=ps[db][r0:r0 + 64, 0:W]),
                             r=[("ps", db)], w=["RC%d" % par])
                        B.op("dve", lambda e: e.tensor_tensor(out=OS[par][r0:r0 + 64, 0:W], in0=ps[ob][r0:r0 + 64, 0:W],
                                                              in1=RC[par][r0:r0 + 64, 0:W], op=ALU.mult),
                             r=[("ps", ob), "RC%d" % par], w=["mOS%d" % par])
                        B.dma(mix[r0:r0 + 64, 4 + h // 2, t0:t0 + W], OS[par][r0:r0 + 64, 0:W], r=["mOS%d" % par],
                              w=[("mix", ti)])

                load_head(0)
                L2 = 2
                for step in range(njobs + L2):
                    if step < njobs:
                        stage1(step)
                    if 0 <= step - L2 < njobs:
                        stage2(step - L2)
                B.barrier()

        def phase_B3():
            with ExitStack() as ph:
                RQ = sb(ph, "r_RQ", [64, 4, T], BF16)
                RQD = sb(ph, "r_RQD", [64, 4, T], BF16)
                RK = sb(ph, "r_RK", [64, 4, T], BF16)
                RV = sb(ph, "r_RV", [128, NBLK, 512], BF16)
                RKD = sb(ph, "r_RKD", [128, NBLK, 256], BF16)
                DT = sb(ph, "r_DT", [128, 512], F32)
                allt = list(range(9))
                for h in range(4):
                    r0 = (h % 2) * 64
                    B.dma(RQ[:, h, :], rq[r0:r0 + 64, h // 2, :], r=[("rq", i) for i in allt], w=["RQ"])
                    B.dma(RQD[:, h, :], rqd[r0:r0 + 64, h // 2, :], r=[("rqd", i) for i in allt], w=["RQD"])
                    B.dma(RK[:, h, :], rk[r0:r0 + 64, h // 2, :], r=[("rk", i) for i in allt], w=["RK"])
                B.dma(RV[:], rv, r=[("rv", i) for i in allt], w=["RV"])
                B.dma(RKD[:], rkd, r=[("rkd", i) for i in allt], w=["RKD"])
                B.dma(DT[:], c_dt, w=["DT"])
                ST = sb(ph, "r_ST", [64, 4, 128], F32)
                STB = sb(ph, "r_STB", [64, 4, 128], BF16)
                B.op("pool", lambda e: e.memset(ST[:], 0.0), w=["ST"])
                B.op("pool", lambda e: e.memset(STB[:], 0.0), w=["STB"])
                IND = [sb(ph, "r_IND%d" % i, [128, 512], BF16) for i in range(2)]
                YS = [sb(ph, "r_YS%d" % i, [128, 512], F32) for i in range(2)]
                SQ = [sb(ph, "r_SQ%d" % i, [128, 512], F32) for i in range(2)]
                R = [sb(ph, "r_R%d" % i, [128, 512], F32) for i in range(2)]
                GG = [sb(ph, "r_GG%d" % i, [128, 4, 128], F32) for i in range(2)]
                OC = [sb(ph, "r_OC%d" % i, [128, 4, 128], BF16) for i in range(2)]
                cdec = [float(np.float64(g) ** 128.0) for g in RET_GAMMA]
                for n in range(NBLK):
                    p = n % 2
                    tsl = slice(n * 128, (n + 1) * 128)
                    ti = 0 if n == 0 else 1 + (n - 1) // 4
                    B.dma(GG[p][:], rg[:, :, tsl], r=[("rg", ti)], w=["GG%d" % p])
                    bi = bank()
                    for h in range(4):
                        B.mm(ps[bi][:, h * 128:(h + 1) * 128], RK[:, h, tsl], RQ[:, h, tsl],
                             True, True, r=["RK", "RQ"], w=[("ps", bi)])
                    B.op("dve", lambda e: e.tensor_tensor(out=IND[p][:], in0=ps[bi][:], in1=DT[:], op=ALU.mult),
                         r=[("ps", bi), "DT"], w=["IND%d" % p])
                    by = bank()
                    for h in range(4):
                        B.mm(ps[by][:, h * 128:(h + 1) * 128], RV[:, n, h * 128:(h + 1) * 128],
                             IND[p][:, h * 128:(h + 1) * 128], True, False, r=["RV", "IND%d" % p], w=[("ps", by)])
                        B.mm(ps[by][:, h * 128:(h + 1) * 128], STB[:, h, :], RQD[:, h, tsl],
                             False, True, r=["STB", "RQD"], w=[("ps", by)])
                    if n + 1 < NBLK:
                        bs = bank()
                        for h in range(4):
                            B.mm(ps[bs][0:64, h * 128:(h + 1) * 128], RKD[:, n, h * 64:(h + 1) * 64],
                                 RV[:, n, h * 128:(h + 1) * 128], True, True, r=["RKD", "RV"], w=[("ps", bs)])
                        for h in range(4):
                            B.op("dve", lambda e: e.scalar_tensor_tensor(
                                out=ST[:, h, :], in0=ST[:, h, :], scalar=cdec[h],
                                in1=ps[bs][0:64, h * 128:(h + 1) * 128], op0=ALU.mult, op1=ALU.add),
                                r=["ST", ("ps", bs)], w=["ST"])
                        B.op("pool", lambda e: e.tensor_copy(out=STB[:], in_=ST[:]), r=["ST"], w=["STB"])
                    B.op("act", lambda e: e.copy(out=YS[p][:], in_=ps[by][:]), r=[("ps", by)], w=["YS%d" % p])
                    bm = bank()
                    B.mm(ps[bm][:], onesF[:], YS[p][:], True, True, r=["onesF", "YS%d" % p], w=[("ps", bm)])
                    B.op("dve", lambda e: e.scalar_tensor_tensor(out=YS[p][:], in0=ps[bm][:], scalar=-1.0 / 128,
                                                                 in1=YS[p][:], op0=ALU.mult, op1=ALU.add),
                         r=[("ps", bm), "YS%d" % p], w=["YS%d" % p])
                    B.op("act", lambda e: e.activation(out=SQ[p][:], in_=YS[p][:], func=AF.Square),
                         r=["YS%d" % p], w=["rSQ%d" % p])
                    bv = bank()
                    B.mm(ps[bv][:], onesF[:], SQ[p][:], True, True, r=["onesF", "rSQ%d" % p], w=[("ps", bv)])
                    B.op("act", lambda e: e.activation(out=R[p][:], in_=ps[bv][:], func=AF.Ln, scale=1.0 / 128,
                                                       bias=LN_EPS), r=[("ps", bv)], w=["rR%d" % p])
                    B.op("act", lambda e: e.activation(out=R[p][:], in_=R[p][:], func=AF.Exp, scale=-0.5),
                         r=["rR%d" % p], w=["rR%d" % p])
                    B.op("dve", lambda e: e.tensor_tensor(out=YS[p][:], in0=YS[p][:], in1=R[p][:], op=ALU.mult),
                         r=["YS%d" % p, "rR%d" % p], w=["YS%d" % p])
                    B.op("pool", lambda e: e.tensor_tensor(out=OC[p][:], in0=YS[p][:].rearrange("p (h n) -> p h n", h=4),
                                                           in1=GG[p][:], op=ALU.mult),
                         r=["YS%d" % p, "GG%d" % p], w=["OC%d" % p])
                    B.dma(mix[:, 8:12, tsl], OC[p][:], r=["OC%d" % p], w=[("mix", ti)])
                B.barrier()

        def phase_C(l):
            with ExitStack() as ph:
                WO = sb(ph, "c_WO", [128, 12, D], BF16)
                B.dma(WO[:], wo[l].rearrange("(c p) n -> p c n", p=128), w=["WO"], q="pool")
                MX = [sb(ph, "c_MX%d" % i, [128, 12, 512], BF16) for i in range(2)]
                X = [sb(ph, "c_X%d" % i, [128, 8, 512], F32) for i in range(2)]
                SQ = sb(ph, "c_SQ", [128, 8, 512], F32)
                R = sb(ph, "c_R", [128, 512], F32)
                XB = sb(ph, "c_XB", [128, 8, 512], BF16)

                def load_tile(ti):
                    t0, W = TILES[ti]
                    p = ti % 2
                    B.dma(MX[p][:, :, 0:W], mix[:, :, t0:t0 + W], r=[("mix", ti)], w=["MX%d" % p])
                    B.dma(X[p][:, :, 0:W], hres[:, :, t0:t0 + W], r=[("hres", t0)], w=["cX%d" % p])

                load_tile(0)
                for ti, (t0, W) in enumerate(TILES):
                    p = ti % 2
                    if ti + 1 < len(TILES):
                        load_tile(ti + 1)
                    for dc in range(8):
                        b = bank()
                        for m in range(12):
                            B.mm(ps[b][:, 0:W], WO[:, m, dc * 128:(dc + 1) * 128], MX[p][:, m, 0:W], m == 0, m == 11,
                                 r=["WO", "MX%d" % p], w=[("ps", b)])
                        B.op("dve", lambda e, dc=dc, b=b: e.scalar_tensor_tensor(
                            out=X[p][:, dc, 0:W], in0=X[p][:, dc, 0:W], scalar=DN_ALPHA, in1=ps[b][:, 0:W],
                            op0=ALU.mult, op1=ALU.add), r=["cX%d" % p, ("ps", b)], w=["cX%d" % p])
                    ln_f(None, X[p], "cX%d" % p, SQ[:], ["SQ"], R, XB, W, 1 + 2 * l, t0, False)
                B.barrier()

        def phase_D(l, final):
            with ExitStack() as ph:
                W1 = sb(ph, "d_W1", [128, 8, 4096], BF16)
                W2 = sb(ph, "d_W2", [128, 32, D], BF16)
                for c in range(8):
                    B.dma(W1[:, c, :], w1[l, c * 128:(c + 1) * 128, :], w=["W1"], q="pool")
                for c in range(4):
                    B.dma(W2[:, c * 8:(c + 1) * 8, :], w2[l, c * 1024:(c + 1) * 1024, :].rearrange("(c p) n -> p c n", p=128),
                          w=["W2"], q="pool")
                HB = [sb(ph, "d_HB%d" % i, [128, 8, 512], BF16) for i in range(2)]
                X = sb(ph, "d_X", [128, 8, 512], F32)
                R = sb(ph, "d_R", [128, 512], F32)
                XB = sb(ph, "d_XB", [128, 8, 512], BF16)
                UT = sb(ph, "d_UT", [128, 16, 512], BF16)
                SQ = UT[:].rearrange("p a n -> p (a n)").bitcast(F32).rearrange("p (c n) -> p c n", c=8)
                kut = [("UT", fc) for fc in range(16)]
                RL = [sb(ph, "d_RL%d" % i, [128, 512], F32) for i in range(3)]
                phd = {}
                if final:
                    phd["OT"] = [sb(ph, "d_OT%d" % i, [128, D], F32) for i in range(2)]

                def load_tile(ti):
                    t0, W = TILES[ti]
                    B.dma(HB[ti % 2][:, :, 0:W], hT[:, :, t0:t0 + W], r=[("hT", t0)], w=["HB%d" % (ti % 2)])

                load_tile(0)
                for ti, (t0, W) in enumerate(TILES):
                    p = ti % 2
                    if ti + 1 < len(TILES):
                        load_tile(ti + 1)
                    B.dma(X[:, :, 0:W], hres[:, :, t0:t0 + W], r=[("hres", t0)], w=["dX"])
                    for half in range(2):
                        for f in range(16):
                            fc = half * 16 + f
                            b = bank()
                            for c in range(8):
                                B.mm(ps[b][:, 0:W], W1[:, c, fc * 128:(fc + 1) * 128], HB[p][:, c, 0:W], c == 0, c == 7,
                                     r=["W1", "HB%d" % p], w=[("ps", b)])
                            i = fc % 3
                            B.op("act", lambda e: e.activation(out=RL[i][:, 0:W], in_=ps[b][:, 0:W], func=AF.Relu),
                                 r=[("ps", b)], w=["RL%d" % i])
                            eng = "pool" if fc % 2 == 0 else "dve"
                            B.op(eng, lambda e: e.tensor_tensor(out=UT[:, f, 0:W], in0=RL[i][:, 0:W],
                                                                in1=RL[i][:, 0:W], op=ALU.mult),
                                 r=["RL%d" % i], w=[("UT", f)])
                        for dc in range(8):
                            b = bank()
                            for f in range(16):
                                fc = half * 16 + f
                                B.mm(ps[b][:, 0:W], W2[:, fc, dc * 128:(dc + 1) * 128], UT[:, f, 0:W], f == 0, f == 15,
                                     r=["W2", ("UT", f)], w=[("ps", b)])
                            B.op("dve", lambda e: e.scalar_tensor_tensor(
                                out=X[:, dc, 0:W], in0=X[:, dc, 0:W], scalar=(DN_ALPHA if half == 0 else 1.0),
                                in1=ps[b][:, 0:W], op0=ALU.mult, op1=ALU.add), r=["dX", ("ps", b)], w=["dX"])
                    ln_f(phd, X, "dX", SQ, kut, R, XB, W, 2 + 2 * l, t0, final)
                B.barrier()

        import os as _os
        PH = _os.environ.get("KPH", "E,A,B1,B2,B3,C,D").split(",")
        if "E" in PH:
            phase_embed()
        for l in range(NL):
            if "A" in PH:
                phase_A(l)
            if "B1" in PH:
                phase_B1()
            if "B2" in PH:
                phase_B2()
            if "B3" in PH:
                phase_B3()
            if "C" in PH:
                phase_C(l)
            if "D" in PH:
                phase_D(l, l == NL - 1)
        B.finish()
        print("instr counts", B.cnt, "sems", len(B.sems), flush=True)
    return nc


def _consts():
    c = {}
    c["c_ident"] = np.eye(128, dtype=np.float32)
    j = np.arange(128)
    c["c_nuincl"] = -(j[:, None] >= j[None, :]).astype(np.float32)
    pos = (np.arange(T) - 112).astype(np.float32)

    def tables(half):
        inv = (np.float32(10000.0) ** (-np.arange(half, dtype=np.float32) / np.float32(half))).astype(np.float32)
        ang = (pos[None, :] * inv[:, None]).astype(np.float32)
        return np.cos(ang).astype(np.float32), np.sin(ang).astype(np.float32)

    c32, s32 = tables(32)
    C = np.concatenate([c32, c32, c32, c32], 0)
    S = np.concatenate([-s32, s32, -s32, s32], 0)
    c["c_ropeR"] = np.ascontiguousarray(np.stack([C, S, 0.125 * C, 0.125 * S], 1).astype(np.float32))
    c16, s16 = tables(16)
    CM = np.concatenate([np.ones((64, T), np.float32), c16, c16], 0)
    SM = np.concatenate([np.zeros((64, T), np.float32), -s16, s16], 0)
    c["c_ropeM"] = np.ascontiguousarray(np.stack([CM, SM], 1).astype(np.float32))
    g = np.array(RET_GAMMA, np.float64)
    n = np.arange(128, dtype=np.float64)
    qd = np.zeros((128, 2, 128), np.float64)
    for h in range(4):
        qd[(h % 2) * 64:(h % 2) * 64 + 64, h // 2, :] = (g[h] ** (n + 1.0))[None, :]
    c["c_qdec"] = qd.astype(np.float32)
    kd = np.zeros((128, 256), np.float64)
    for h in range(4):
        kd[:, h * 64:(h + 1) * 64] = (g[h] ** (127.0 - n))[:, None]
    c["c_kdecT"] = kd.astype(np.float32)
    dt = np.zeros((128, 4, 128), np.float64)
    diff = n[None, :] - n[:, None]
    for h in range(4):
        dt[:, h, :] = np.where(diff >= 0, g[h] ** np.maximum(diff, 0.0), 0.0)
    c["c_dt"] = dt.reshape(128, 512).astype(np.float32)
    cv = np.zeros((128, 2), np.float64)
    for h in range(4):
        cv[(h % 2) * 64:(h % 2) * 64 + 64, h // 2] = g[h] ** 128.0
    c["c_cvec"] = cv.astype(np.float32)
    s_ = np.arange(128)[:, None]
    t_ = np.arange(512)[None, :]
    msb = np.zeros((128, 6, 512), np.float32)
    mml = np.zeros((128, 6, 512), np.float32)
    for cc in range(4):
        msb[:, cc, :] = ((cc * 128 + s_) < t_)
        mml[:, cc, :] = ((cc * 128 + s_) <= t_)
    valid = (s_ >= 112)
    msb[:, 4, :] = ((s_ < t_) & valid)
    msb[:, 5, :] = np.broadcast_to(valid, (128, 512))
    mml[:, 4, :] = ((s_ <= t_) & (valid | (s_ == t_)))
    mml[:, 5, :] = np.broadcast_to(valid, (128, 512))
    c["c_msb"] = msb
    c["c_mmla"] = mml
    c["c_valid"] = valid.astype(np.float32).reshape(128, 1)
    return c


def _rot_idx(nheads, d):
    half = d // 2
    idx = []
    for h in range(nheads):
        idx += [h * d + ((i + half) % d) for i in range(d)]
    return np.array(idx)


def _prep_weights(inp, NL):
    w_in = np.asarray(inp["w_in"])[:NL]
    b_sbq, b_sbk, b_sbv, b_cq, b_ckv, b_kr, b_rq, b_rk, b_rv, b_rg = 0, 512, 1024, 1536, 1920, 2176, 2208, 2464, 2720, 3232
    cols = np.concatenate([
        np.arange(b_sbq, b_sbq + 512), np.arange(b_sbk, b_sbk + 512), np.arange(b_sbv, b_sbv + 512),
        np.arange(b_cq, b_cq + 384), np.arange(b_ckv, b_ckv + 256),
        np.arange(b_kr, b_kr + 32), b_kr + _rot_idx(1, 32),
        np.arange(b_rq, b_rq + 256), b_rq + _rot_idx(4, 64),
        np.arange(b_rk, b_rk + 256), b_rk + _rot_idx(4, 64),
        np.arange(b_rv, b_rv + 512), np.arange(b_rg, b_rg + 512)])
    assert cols.size == NA
    wa = np.ascontiguousarray(np.take(w_in, cols, axis=2))
    w_uq = np.asarray(inp["w_uq"])[:NL]
    ucols = []
    for h in range(8):
        base = h * 96
        xc = list(range(base, base + 96))
        rc = list(range(base, base + 64)) + [base + 64 + ((i + 16) % 32) for i in range(32)]
        ucols += xc + rc
    wuq = np.ascontiguousarray(np.take(w_uq, np.array(ucols), axis=2))
    w_ukv = np.asarray(inp["w_ukv"])[:NL]
    kc = np.concatenate([np.arange(h * 128, h * 128 + 64) for h in range(8)] +
                        [np.arange(h * 128 + 64, h * 128 + 128) for h in range(8)])
    wukv = np.ascontiguousarray(np.take(w_ukv, kc, axis=2))
    lnp = np.zeros((9, 2, 1024), np.float32)
    lnp[0, 0] = np.asarray(inp["ln_emb_g"])
    lnp[0, 1] = np.asarray(inp["ln_emb_b"])
    for l in range(NLAYERS):
        lnp[1 + 2 * l, 0] = np.asarray(inp["ln1_g"])[l]
        lnp[1 + 2 * l, 1] = np.asarray(inp["ln1_b"])[l]
        lnp[2 + 2 * l, 0] = np.asarray(inp["ln2_g"])[l]
        lnp[2 + 2 * l, 1] = np.asarray(inp["ln2_b"])[l]
    lnp = np.ascontiguousarray(lnp.reshape(9, 2, 8, 128).transpose(3, 0, 1, 2))
    qn = np.ascontiguousarray(np.asarray(inp["mla_q_norm"]).reshape(NLAYERS, 3, 128).transpose(2, 0, 1))
    kvn = np.ascontiguousarray(np.asarray(inp["mla_kv_norm"]).reshape(NLAYERS, 2, 128).transpose(2, 0, 1))
    shared = {
        "meta": np.ascontiguousarray(np.asarray(inp["meta_tokens"], dtype=np.float32)),
        "lnp": lnp.astype(np.float32), "qn": qn.astype(np.float32), "kvn": kvn.astype(np.float32),
        "wa": wa.astype(np.float32), "wuq": wuq.astype(np.float32), "wukv": wukv.astype(np.float32),
        "wo": np.ascontiguousarray(np.asarray(inp["w_out"], dtype=np.float32)[:NL]),
        "w1": np.ascontiguousarray(np.asarray(inp["w_ff1"], dtype=np.float32)[:NL]),
        "w2": np.ascontiguousarray(np.asarray(inp["w_ff2"], dtype=np.float32)[:NL]),
    }
    shared.update(_consts())
    return shared


def run(inp, NL=NLAYERS, cores=8):
    shared = _prep_weights(inp, NL)
    x = np.asarray(inp["x"], dtype=np.float32)
    nc = build_nc(NL)
    in_maps = []
    for b in range(cores):
        m = dict(shared)
        m["x"] = np.ascontiguousarray(x[b])
        in_maps.append(m)
    if _TRACE:
        res = run_bass_kernel_spmd(nc, in_maps, core_ids=list(range(cores)), trace=True)
        print("TRACE exec_time_ns", res.exec_time_ns, flush=True)
    else:
        res = run_bass_kernel_spmd(nc, in_maps, core_ids=list(range(cores)))
    return np.stack([np.asarray(r["out"]) for r in res.results], axis=0).astype(np.float32)


def kernel(**inputs):
    return run(inputs, NLAYERS, 8)
```

```python
import numpy as np
import ml_dtypes
from contextlib import ExitStack
import concourse.bass as bass
import concourse.mybir as mybir
from concourse.bass_utils import run_bass_kernel_spmd

F32 = mybir.dt.float32
BF16 = mybir.dt.bfloat16
ALU = mybir.AluOpType
AF = mybir.ActivationFunctionType

D = 1024
SEQ = 4096
NLAYERS = 4
T = 4224
NBLK = 33
TILES = [(0, 128)] + [(128 + 512 * i, 512) for i in range(8)]
LN_EPS = 1e-5
DN_ALPHA = (2 * NLAYERS) ** 0.25
RET_GAMMA = [1.0 - 2.0 ** (-5 - h) for h in range(4)]
MLA_SCALE = 96 ** -0.5

O_SBQ, O_SBK, O_SBV, O_CQ, O_CKV, O_KR, O_KRR = 0, 512, 1024, 1536, 1920, 2176, 2208
O_RQ, O_RQR, O_RK, O_RKR, O_RV, O_RG = 2240, 2496, 2752, 3008, 3264, 3776
NA = 4288
SEM_LIM = 12000
_TRACE = False


class Builder:
    def __init__(self, nc, es):
        self.nc = nc
        self.es = es
        self.E = {"pe": nc.tensor, "act": nc.scalar, "dve": nc.vector, "pool": nc.gpsimd, "sp": nc.sync}
        self.cnt = {e: 0 for e in self.E}
        self.waited = {e: {} for e in self.E}
        self.sems = {}
        self.st = {}
        self.dq = {}
        self.flip = 0

    def sem(self, key):
        if key not in self.sems:
            self.sems[key] = self.es.enter_context(self.nc.semaphore("s%d" % len(self.sems)))
        return self.sems[key]

    def _wait(self, eng, tok):
        sk, val = tok
        w = self.waited[eng]
        if w.get(sk, 0) >= val:
            return
        w[sk] = val
        self.E[eng].wait_ge(self.sem(sk), val)

    def _deps(self, eng, r, w):
        toks = set()
        for k in r:
            s = self.st.get(k)
            if s and s[0]:
                toks.add(s[0])
        for k in w:
            s = self.st.get(k)
            if s:
                if s[0] and s[0][0][0] != eng:
                    toks.add(s[0])
                for sk, v in s[1].items():
                    if sk[0] != eng:
                        toks.add((sk, v))
        for t in toks:
            self._wait(eng, t)

    def _update(self, tok, r, w):
        sk, val = tok
        for k in r:
            s = self.st.setdefault(k, [None, {}])
            if s[1].get(sk, 0) < val:
                s[1][sk] = val
        for k in w:
            self.st[k] = [tok, {}]

    def op(self, eng, fn, r=(), w=()):
        self._deps(eng, r, w)
        ins = fn(self.E[eng])
        self.cnt[eng] += 1
        i = self.cnt[eng]
        sk = (eng, (i - 1) // SEM_LIM)
        val = (i - 1) % SEM_LIM + 1
        ins.then_inc(self.sem(sk), 1)
        self._update((sk, val), r, w)

    def dma(self, out, in_, r=(), w=(), q="sp"):
        d = self.dq.setdefault(q, {"uses": [0] * 8, "next": 0})
        slot = d["next"]
        d["next"] = (slot + 1) % 8
        sk = ("dma", q, slot)
        if d["uses"][slot] > 0:
            self._wait(q, (sk, 16 * d["uses"][slot]))
        self._deps(q, r, w)
        self.E[q].dma_start(out=out, in_=in_).then_inc(self.sem(sk), 16)
        d["uses"][slot] += 1
        self._update((sk, 16 * d["uses"][slot]), r, w)

    def all_tokens(self):
        toks = []
        for e in ("pe", "act", "dve", "pool"):
            i = self.cnt[e]
            if i > 0:
                toks.append(((e, (i - 1) // SEM_LIM), (i - 1) % SEM_LIM + 1))
        for q, d in self.dq.items():
            for slot, u in enumerate(d["uses"]):
                if u > 0:
                    toks.append((("dma", q, slot), 16 * u))
        return toks

    def barrier(self):
        toks = self.all_tokens()
        for e in self.E:
            for t in toks:
                if t[0][0] == e:
                    continue
                self._wait(e, t)

    def finish(self):
        for t in self.all_tokens():
            if t[0][0] == "dma":
                self._wait("sp", t)

    def mm(self, out, lhsT, rhs, start, stop, r, w, **kw):
        self.op("pe", lambda e: e.matmul(out, lhsT, rhs, start=start, stop=stop, **kw), r=r, w=w)

    def tr(self, out, in_, ident, r, w):
        self.op("pe", lambda e: e.transpose(out, in_, ident), r=r, w=w)

    def evac(self, out, in_, r, w, scale=None, eng=None):
        if eng is None:
            self.flip ^= 1
            eng = "act" if self.flip else "dve"
        if eng == "act":
            if scale is None:
                self.op("act", lambda e: e.copy(out=out, in_=in_), r=r, w=w)
            else:
                self.op("act", lambda e: e.mul(out=out, in_=in_, mul=scale), r=r, w=w)
        else:
            if scale is None:
                self.op("dve", lambda e: e.tensor_copy(out=out, in_=in_), r=r, w=w)
            else:
                self.op("dve", lambda e: e.tensor_scalar(out=out, in0=in_, scalar1=scale, scalar2=None,
                                                         op0=ALU.mult), r=r, w=w)


def build_nc(NL, debug=False):
    nc = bass.Bass("TRN2", target_bir_lowering=False)

    def din(name, shape, dt=F32):
        return nc.dram_tensor(name, shape, dt, kind="ExternalInput").ap()

    def dscr(name, shape, dt):
        return nc.dram_tensor(name, shape, dt, kind="Internal").ap()

    x = din("x", [SEQ, D])
    meta = din("meta", [16, D])
    lnp = din("lnp", [128, 9, 2, 8])
    qn = din("qn", [128, NLAYERS, 3])
    kvn = din("kvn", [128, NLAYERS, 2])
    wa = din("wa", [NL, D, NA])
    wuq = din("wuq", [NL, 384, 1536])
    wukv = din("wukv", [NL, 256, 1024])
    wo = din("wo", [NL, 1536, D])
    w1 = din("w1", [NL, D, 4096])
    w2 = din("w2", [NL, 4096, D])
    c_ident = din("c_ident", [128, 128])
    c_nuincl = din("c_nuincl", [128, 128])
    c_ropeR = din("c_ropeR", [128, 4, T])
    c_ropeM = din("c_ropeM", [96, 2, T])
    c_qdec = din("c_qdec", [128, 2, 128])
    c_kdecT = din("c_kdecT", [128, 256])
    c_dt = din("c_dt", [128, 512])
    c_cvec = din("c_cvec", [128, 2])
    c_msb = din("c_msb", [128, 6, 512])
    c_mmla = din("c_mmla", [128, 6, 512])
    c_valid = din("c_valid", [128, 1])
    out = nc.dram_tensor("out", [SEQ, D], F32, kind="ExternalOutput").ap()

    hres = dscr("hres", [128, 8, T], F32)
    hT = dscr("hT", [128, 8, T], BF16)
    sbq = dscr("sbq", [128, 4, T], BF16)
    sbk = dscr("sbk", [128, 4, T], BF16)
    sbv = dscr("sbv", [128, NBLK, 512], BF16)
    mq = dscr("mq", [128, 8, T], BF16)
    mk = dscr("mk", [128, 8, T], BF16)
    mv = dscr("mv", [128, NBLK, 512], BF16)
    rq = dscr("rq", [128, 2, T], BF16)
    rqd = dscr("rqd", [128, 2, T], BF16)
    rk = dscr("rk", [128, 2, T], BF16)
    rkd = dscr("rkd", [128, NBLK, 256], BF16)
    rv = dscr("rv", [128, NBLK, 512], BF16)
    rg = dscr("rg", [128, 4, T], F32)
    mix = dscr("mix", [128, 12, T], BF16)

    with ExitStack() as es:
        B = Builder(nc, es)

        uniq = {"n": 0}

        def sb(stack, name, shape, dt):
            uniq["n"] += 1
            return stack.enter_context(nc.sbuf_tensor("%s_%d" % (name, uniq["n"]), shape, dt))

        ps = [es.enter_context(nc.psum_tensor("ps%d" % i, [128, 512], F32)) for i in range(8)]
        rot = {"i": 0}

        def bank(lo=0, hi=8):
            n = hi - lo
            b = lo + rot["i"] % n
            rot["i"] += 1
            return b

        identF = sb(es, "identF", [128, 128], F32)
        identB = sb(es, "identB", [128, 128], BF16)
        onesF = sb(es, "onesF", [128, 128], F32)
        onesB = sb(es, "onesB", [128, 64], BF16)
        RS = sb(es, "RS", [128, 512], F32)
        lnps = sb(es, "lnps", [128, 9, 2, 8], F32)
        qns = sb(es, "qns", [128, NLAYERS, 3], F32)
        kvns = sb(es, "kvns", [128, NLAYERS, 2], F32)
        B.dma(identF[:], c_ident, w=["identF"])
        B.dma(identB[:], c_ident, w=["identB"], q="pool")
        B.dma(lnps[:], lnp, w=["lnps"])
        B.dma(qns[:], qn, w=["qns"])
        B.dma(kvns[:], kvn, w=["kvns"])
        B.op("pool", lambda e: e.memset(onesF[:], 1.0), w=["onesF"])
        B.op("pool", lambda e: e.memset(onesB[:], 1.0), w=["onesB"])
        with ExitStack() as z0:
            ZR = sb(z0, "ZR", [32, T], BF16)
            B.op("pool", lambda e: e.memset(ZR[:], 0.0), w=["ZR"])
            B.dma(mq[96:128, :, :], ZR[:].unsqueeze(1).broadcast_to([32, 8, T]), r=["ZR"], w=["mqz"])
            B.dma(mk[96:128, :, :], ZR[:].unsqueeze(1).broadcast_to([32, 8, T]), r=["ZR"], w=["mkz"])
            B.barrier()

        def ln_f(ph, X, kx, SQ, ksq, R, XB, W, lnidx, t0, final):
            xs = X[:, :, 0:W]
            mp = bank()
            B.op("dve", lambda e: e.reduce_sum(out=RS[:, 0:W], in_=xs.rearrange("p c n -> p n c"),
                                               axis=mybir.AxisListType.X), r=[kx], w=["RS"])
            B.mm(ps[mp][:, 0:W], onesF[:], RS[:, 0:W], True, True, r=["RS", "onesF"], w=[("ps", mp)])
            B.op("dve", lambda e: e.scalar_tensor_tensor(
                out=xs, in0=ps[mp][:, 0:W].unsqueeze(1).broadcast_to([128, 8, W]), scalar=-1.0 / D, in1=xs,
                op0=ALU.mult, op1=ALU.add), r=[("ps", mp), kx], w=[kx])
            B.op("act", lambda e: e.activation(out=SQ[:, :, 0:W], in_=xs, func=AF.Square), r=[kx], w=ksq)
            vp = bank()
            B.op("dve", lambda e: e.reduce_sum(out=RS[:, 0:W], in_=SQ[:, :, 0:W].rearrange("p c n -> p n c"),
                                               axis=mybir.AxisListType.X), r=ksq, w=["RS"])
            B.mm(ps[vp][:, 0:W], onesF[:], RS[:, 0:W], True, True, r=["RS", "onesF"], w=[("ps", vp)])
            B.op("act", lambda e: e.activation(out=R[:, 0:W], in_=ps[vp][:, 0:W], func=AF.Ln, scale=1.0 / D,
                                               bias=LN_EPS), r=[("ps", vp)], w=["R"])
            B.op("act", lambda e: e.activation(out=R[:, 0:W], in_=R[:, 0:W], func=AF.Exp, scale=-0.5),
                 r=["R"], w=["R"])
            B.op("dve", lambda e: e.tensor_tensor(out=xs, in0=xs, in1=R[:, 0:W].unsqueeze(1).broadcast_to([128, 8, W]),
                                                  op=ALU.mult), r=[kx, "R"], w=[kx])
            for c in range(8):
                B.op("act", lambda e, c=c: e.activation(out=X[:, c, 0:W], in_=X[:, c, 0:W], func=AF.Identity,
                                                        scale=lnps[:, lnidx, 0, c:c + 1],
                                                        bias=lnps[:, lnidx, 1, c:c + 1]),
                     r=[kx, "lnps"], w=[kx])
            B.op("pool", lambda e: e.tensor_copy(out=XB[:, :, 0:W], in_=xs), r=[kx], w=["XB"])
            hk = ("h", t0)
            B.dma(hres[:, :, t0:t0 + W], xs, r=[kx], w=[("hres", t0)])
            B.dma(hT[:, :, t0:t0 + W], XB[:, :, 0:W], r=["XB"], w=[("hT", t0)])
            if final and t0 >= 128:
                for s in range(W // 128):
                    OT = ph["OT"][s % 2]
                    ko = "OT%d" % (s % 2)
                    for half in range(2):
                        b = bank()
                        for cc in range(4):
                            c = half * 4 + cc
                            B.tr(ps[b][:, cc * 128:(cc + 1) * 128], X[:, c, s * 128:(s + 1) * 128], identF[:],
                                 r=[kx, "identF"], w=[("ps", b)])
                        B.evac(OT[:, half * 512:(half + 1) * 512], ps[b][:], r=[("ps", b)], w=[ko])
                    tok0 = t0 - 128 + s * 128
                    B.dma(out[tok0:tok0 + 128, :], OT[:], r=[ko], w=[("out", tok0)])

        def phase_embed():
            with ExitStack() as ph:
                xt = [sb(ph, "e_xt%d" % i, [128, 4, D], F32) for i in range(2)]
                X = [sb(ph, "e_X%d" % i, [128, 8, 512], F32) for i in range(2)]
                SQ = sb(ph, "e_SQ", [128, 8, 512], F32)
                R = sb(ph, "e_R", [128, 512], F32)
                XB = sb(ph, "e_XB", [128, 8, 512], BF16)
                for ti, (t0, W) in enumerate(TILES):
                    a = xt[ti % 2]
                    ka = "e_xt%d" % (ti % 2)
                    kx = "e_X%d" % (ti % 2)
                    nsub = W // 128
                    if ti == 0:
                        B.op("pool", lambda e: e.memset(a[:, 0, :], 0.0), w=[ka])
                        B.dma(a[112:128, 0, :], meta, w=[ka])
                    else:
                        B.dma(a[:], x[t0 - 128:t0 - 128 + 512, :].rearrange("(s p) d -> p s d", p=128), w=[ka])
                    for c in range(8):
                        b = bank()
                        for s in range(nsub):
                            B.tr(ps[b][:, s * 128:(s + 1) * 128], a[:, s, c * 128:(c + 1) * 128], identF[:],
                                 r=[ka, "identF"], w=[("ps", b)])
                        B.evac(X[ti % 2][:, c, 0:W], ps[b][:, 0:W], r=[("ps", b)], w=[kx])
                    ln_f(None, X[ti % 2], kx, SQ[:], ["SQ"], R, XB, W, 0, t0, False)
                B.barrier()

        def phase_A(l):
            with ExitStack() as ph:
                WA = sb(ph, "a_WA", [128, 8, NA], BF16)
                WUQ = sb(ph, "a_WUQ", [128, 3, 1536], BF16)
                WUKV = sb(ph, "a_WUKV", [128, 2, 1024], BF16)
                for c in range(8):
                    B.dma(WA[:, c, :], wa[l, c * 128:(c + 1) * 128, :], w=["WA"], q="pool")
                B.dma(WUQ[:], wuq[l].rearrange("(c p) n -> p c n", p=128), w=["WUQ"], q="pool")
                B.dma(WUKV[:], wukv[l].rearrange("(c p) n -> p c n", p=128), w=["WUKV"], q="pool")
                QDEC = sb(ph, "a_QDEC", [128, 2, 128], F32)
                KDEC = sb(ph, "a_KDEC", [128, 256], F32)
                B.dma(QDEC[:], c_qdec, w=["QDEC"])
                B.dma(KDEC[:], c_kdecT, w=["KDEC"])
                HT = [sb(ph, "a_HT%d" % i, [128, 8, 512], BF16) for i in range(2)]
                RT = [sb(ph, "a_RT%d" % i, [128, 4, 512], F32) for i in range(2)]
                MT = [sb(ph, "a_MT%d" % i, [96, 2, 512], F32) for i in range(2)]
                MK = [sb(ph, "a_MK%d" % i, [32, 2, 512], F32) for i in range(2)]
                CQ = sb(ph, "a_CQ", [128, 3, 512], F32)
                CKV = sb(ph, "a_CKV", [128, 2, 512], F32)
                SQ = sb(ph, "a_SQ", [128, 3, 512], F32)
                R = sb(ph, "a_R", [128, 512], F32)
                CQN = sb(ph, "a_CQN", [128, 3, 512], BF16)
                CKVN = sb(ph, "a_CKVN", [128, 2, 512], BF16)
                T1 = [sb(ph, "a_T1%d" % i, [128, 512], F32) for i in range(3)]
                T2 = [sb(ph, "a_T2%d" % i, [128, 512], F32) for i in range(3)]
                STG = [sb(ph, "a_STG%d" % i, [128, 512], BF16) for i in range(6)]
                RKB = [sb(ph, "a_RKB%d" % i, [128, 512], BF16) for i in range(2)]
                GS = [sb(ph, "a_GS0", [128, 4, 512], F32)] * 2
                KST = [sb(ph, "a_KST%d" % i, [128, 256], BF16) for i in range(2)]
                cn = {"stg": 0, "t": 0, "kst": 0}

                def stage():
                    i = cn["stg"] % 6
                    cn["stg"] += 1
                    return STG[i], "STG%d" % i

                def tt():
                    i = cn["t"] % 3
                    cn["t"] += 1
                    return T1[i], "T1%d" % i, T2[i], "T2%d" % i

                def load_tile(ti):
                    t0, W = TILES[ti]
                    p = ti % 2
                    B.dma(HT[p][:, :, 0:W], hT[:, :, t0:t0 + W], r=[("hT", t0)], w=["HT%d" % p])
                    B.dma(RT[p][:, :, 0:W], c_ropeR[:, :, t0:t0 + W], w=["RT%d" % p])
                    B.dma(MT[p][:, :, 0:W], c_ropeM[:, :, t0:t0 + W], w=["MT%d" % p])
                    B.dma(MK[p][:, :, 0:W], c_ropeM[64:96, :, t0:t0 + W], w=["MK%d" % p])

                load_tile(0)
                for ti, (t0, W) in enumerate(TILES):
                    p = ti % 2
                    if ti + 1 < len(TILES):
                        load_tile(ti + 1)
                    H = HT[p]
                    kh = "HT%d" % p
                    nsub = W // 128
                    blk0 = t0 // 128

                    def fgroup(col0, M, Wm=WA, kw="WA", rhs=None, krhs=None, nchunk=8):
                        b = bank()
                        for c in range(nchunk):
                            rr = H[:, c, 0:W] if rhs is None else rhs[:, c, 0:W]
                            B.mm(ps[b][0:M, 0:W], Wm[:, c, col0:col0 + M], rr, c == 0, c == nchunk - 1,
                                 r=[kw, kh if krhs is None else krhs], w=[("ps", b)])
                        return b

                    for m in range(4):
                        b = fgroup(O_SBQ + m * 128, 128)
                        S_, ks = stage()
                        B.evac(S_[:, 0:W], ps[b][:, 0:W], r=[("ps", b)], w=[ks], scale=0.125)
                        B.dma(sbq[:, m, t0:t0 + W], S_[:, 0:W], r=[ks], w=[("sbq", ti)])
                    for m in range(4):
                        b = fgroup(O_SBK + m * 128, 128)
                        S_, ks = stage()
                        B.evac(S_[:, 0:W], ps[b][:, 0:W], r=[("ps", b)], w=[ks])
                        B.dma(sbk[:, m, t0:t0 + W], S_[:, 0:W], r=[ks], w=[("sbk", ti)])
                    for (col0, dst, nm) in ((O_SBV, sbv, "sbv"), (O_RV, rv, "rv")):
                        for s in range(nsub):
                            b = bank()
                            for c in range(8):
                                B.mm(ps[b][:, :], H[:, c, s * 128:(s + 1) * 128], WA[:, c, col0:col0 + 512],
                                     c == 0, c == 7, r=["WA", kh], w=[("ps", b)])
                            S_, ks = stage()
                            B.evac(S_[:, :], ps[b][:, :], r=[("ps", b)], w=[ks])
                            B.dma(dst[:, blk0 + s, :], S_[:, :], r=[ks], w=[(nm, ti)])
                    for m in range(3):
                        b = fgroup(O_CQ + m * 128, 128)
                        B.evac(CQ[:, m, 0:W], ps[b][:, 0:W], r=[("ps", b)], w=["CQ"])
                    for m in range(2):
                        b = fgroup(O_CKV + m * 128, 128)
                        B.evac(CKV[:, m, 0:W], ps[b][:, 0:W], r=[("ps", b)], w=["CKV"])
                    ba = fgroup(O_KR, 32)
                    bb = fgroup(O_KRR, 32)
                    t1, k1, t2, k2 = tt()
                    B.op("dve", lambda e: e.tensor_tensor(out=t1[0:32, 0:W], in0=ps[ba][0:32, 0:W], in1=MK[p][:, 0, 0:W],
                                                          op=ALU.mult), r=[("ps", ba), "MK%d" % p], w=[k1])
                    B.op("dve", lambda e: e.tensor_tensor(out=t2[0:32, 0:W], in0=ps[bb][0:32, 0:W], in1=MK[p][:, 1, 0:W],
                                                          op=ALU.mult), r=[("ps", bb), "MK%d" % p], w=[k2])
                    S_, ks = stage()
                    B.op("pool", lambda e: e.tensor_tensor(out=S_[0:32, 0:W], in0=t1[0:32, 0:W], in1=t2[0:32, 0:W],
                                                           op=ALU.add), r=[k1, k2], w=[ks])
                    B.dma(mk[64:96, :, t0:t0 + W], S_[0:32, 0:W].unsqueeze(1).broadcast_to([32, 8, W]),
                          r=[ks], w=[("mk", ti)])
                    for pr in range(2):
                        ba = fgroup(O_RQ + pr * 128, 128)
                        bb = fgroup(O_RQR + pr * 128, 128)
                        t1, k1, t2, k2 = tt()
                        B.op("dve", lambda e: e.tensor_tensor(out=t1[:, 0:W], in0=ps[ba][:, 0:W], in1=RT[p][:, 0, 0:W],
                                                              op=ALU.mult), r=[("ps", ba), "RT%d" % p], w=[k1])
                        B.op("dve", lambda e: e.tensor_tensor(out=t2[:, 0:W], in0=ps[bb][:, 0:W], in1=RT[p][:, 1, 0:W],
                                                              op=ALU.mult), r=[("ps", bb), "RT%d" % p], w=[k2])
                        B.op("pool", lambda e: e.tensor_tensor(out=t1[:, 0:W], in0=t1[:, 0:W], in1=t2[:, 0:W],
                                                               op=ALU.add), r=[k1, k2], w=[k1])
                        S_, ks = stage()
                        B.op("act", lambda e: e.copy(out=S_[:, 0:W], in_=t1[:, 0:W]), r=[k1], w=[ks])
                        B.dma(rq[:, pr, t0:t0 + W], S_[:, 0:W], r=[ks], w=[("rq", ti)])
                        S2, ks2 = stage()
                        B.op("pool", lambda e: e.tensor_tensor(
                            out=S2[:, 0:W].rearrange("p (s n) -> p s n", n=128),
                            in0=t1[:, 0:W].rearrange("p (s n) -> p s n", n=128),
                            in1=QDEC[:, pr, :].unsqueeze(1).broadcast_to([128, nsub, 128]), op=ALU.mult),
                            r=[k1, "QDEC"], w=[ks2])
                        B.dma(rqd[:, pr, t0:t0 + W], S2[:, 0:W], r=[ks2], w=[("rqd", ti)])
                    for pr in range(2):
                        ba = fgroup(O_RK + pr * 128, 128)
                        bb = fgroup(O_RKR + pr * 128, 128)
                        t1, k1, t2, k2 = tt()
                        B.op("dve", lambda e: e.tensor_tensor(out=t1[:, 0:W], in0=ps[ba][:, 0:W], in1=RT[p][:, 2, 0:W],
                                                              op=ALU.mult), r=[("ps", ba), "RT%d" % p], w=[k1])
                        B.op("dve", lambda e: e.tensor_tensor(out=t2[:, 0:W], in0=ps[bb][:, 0:W], in1=RT[p][:, 3, 0:W],
                                                              op=ALU.mult), r=[("ps", bb), "RT%d" % p], w=[k2])
                        kb_ = "RKB%d" % pr
                        B.op("pool", lambda e: e.tensor_tensor(out=RKB[pr][:, 0:W], in0=t1[:, 0:W], in1=t2[:, 0:W],
                                                               op=ALU.add), r=[k1, k2], w=[kb_])
                        if ti == 0:
                            B.op("pool", lambda e: e.memset(RKB[pr][:, 0:112], 0.0), r=[kb_], w=[kb_])
                        B.dma(rk[:, pr, t0:t0 + W], RKB[pr][:, 0:W], r=[kb_], w=[("rk", ti)])
                    for s in range(nsub):
                        b = bank()
                        pb = ps[b][:].bitcast(BF16)
                        for pr in range(2):
                            B.tr(pb[:, pr * 128:(pr + 1) * 128], RKB[pr][:, s * 128:(s + 1) * 128], identB[:],
                                 r=["RKB%d" % pr, "identB"], w=[("ps", b)])
                        i = cn["kst"] % 2
                        cn["kst"] += 1
                        B.op("dve", lambda e: e.tensor_tensor(out=KST[i][:], in0=pb[:, 0:256], in1=KDEC[:], op=ALU.mult),
                             r=[("ps", b), "KDEC"], w=["KST%d" % i])
                        B.dma(rkd[:, blk0 + s, :], KST[i][:], r=["KST%d" % i], w=[("rkd", ti)])
                    G = GS[p]
                    for m in range(4):
                        b = fgroup(O_RG + m * 128, 128)
                        B.op("act", lambda e: e.activation(out=G[:, m, 0:W], in_=ps[b][:, 0:W], func=AF.Silu),
                             r=[("ps", b)], w=["GS"])
                    B.dma(rg[:, :, t0:t0 + W], G[:, :, 0:W], r=["GS"], w=[("rg", ti)])
                    for (src, ksrc, nch, dstn, kd, gam, inv) in ((CQ, "CQ", 3, CQN, "CQN", qns, 1.0 / 384),
                                                                   (CKV, "CKV", 2, CKVN, "CKVN", kvns, 1.0 / 256)):
                        B.op("act", lambda e: e.activation(out=SQ[:, 0:nch, 0:W], in_=src[:, :, 0:W], func=AF.Square),
                             r=[ksrc], w=["SQ"])
                        b = bank()
                        for m in range(nch):
                            B.mm(ps[b][:, 0:W], onesF[:], SQ[:, m, 0:W], m == 0, m == nch - 1, r=["SQ", "onesF"],
                                 w=[("ps", b)])
                        B.op("act", lambda e: e.activation(out=R[:, 0:W], in_=ps[b][:, 0:W], func=AF.Ln, scale=inv,
                                                           bias=LN_EPS), r=[("ps", b)], w=["R"])
                        B.op("act", lambda e: e.activation(out=R[:, 0:W], in_=R[:, 0:W], func=AF.Exp, scale=-0.5),
                             r=["R"], w=["R"])
                        for m in range(nch):
                            B.op("dve", lambda e, m=m: e.scalar_tensor_tensor(
                                out=dstn[:, m, 0:W], in0=src[:, m, 0:W], scalar=gam[:, l, m:m + 1], in1=R[:, 0:W],
                                op0=ALU.mult, op1=ALU.mult), r=[ksrc, "R", "qns", "kvns"], w=[kd])
                    for h in range(8):
                        ba = fgroup((2 * h) * 96, 96, Wm=WUQ, kw="WUQ", rhs=CQN, krhs="CQN", nchunk=3)
                        bb = fgroup((2 * h + 1) * 96, 96, Wm=WUQ, kw="WUQ", rhs=CQN, krhs="CQN", nchunk=3)
                        t1, k1, t2, k2 = tt()
                        B.op("dve", lambda e: e.tensor_tensor(out=t1[0:96, 0:W], in0=ps[ba][0:96, 0:W], in1=MT[p][:, 0, 0:W],
                                                              op=ALU.mult), r=[("ps", ba), "MT%d" % p], w=[k1])
                        B.op("dve", lambda e: e.tensor_tensor(out=t2[0:96, 0:W], in0=ps[bb][0:96, 0:W], in1=MT[p][:, 1, 0:W],
                                                              op=ALU.mult), r=[("ps", bb), "MT%d" % p], w=[k2])
                        S_, ks = stage()
                        B.op("pool", lambda e: e.tensor_tensor(out=S_[0:96, 0:W], in0=t1[0:96, 0:W], in1=t2[0:96, 0:W],
                                                               op=ALU.add), r=[k1, k2], w=[ks])
                        B.dma(mq[0:96, h, t0:t0 + W], S_[0:96, 0:W], r=[ks], w=[("mq", ti)])
                    for m in range(4):
                        b = fgroup(m * 128, 128, Wm=WUKV, kw="WUKV", rhs=CKVN, krhs="CKVN", nchunk=2)
                        S_, ks = stage()
                        B.evac(S_[:, 0:W], ps[b][:, 0:W], r=[("ps", b)], w=[ks])
                        B.dma(mk[0:64, 2 * m, t0:t0 + W], S_[0:64, 0:W], r=[ks], w=[("mk", ti)])
                        B.dma(mk[0:64, 2 * m + 1, t0:t0 + W], S_[64:128, 0:W], r=[ks], w=[("mk", ti)])
                    for s in range(nsub):
                        b = bank()
                        for c in range(2):
                            B.mm(ps[b][:, :], CKVN[:, c, s * 128:(s + 1) * 128], WUKV[:, c, 512:1024], c == 0, c == 1,
                                 r=["WUKV", "CKVN"], w=[("ps", b)])
                        S_, ks = stage()
                        B.evac(S_[:, :], ps[b][:, :], r=[("ps", b)], w=[ks])
                        B.dma(mv[:, blk0 + s, :], S_[:, :], r=[ks], w=[("mv", ti)])
                B.barrier()

        def phase_B1():
            with ExitStack() as ph:
                UIb = sb(ph, "b_UIb", [128, 128], BF16)
                ONb = sb(ph, "b_ONb", [128, 128], BF16)
                MS = sb(ph, "b_MS", [128, 6, 512], F32)
                B.dma(UIb[:], c_nuincl, w=["UIb"], q="pool")
                B.op("pool", lambda e: e.memset(ONb[:], -1.0), w=["ONb"])
                B.dma(MS[:], c_msb, w=["MS"])
                QP = [sb(ph, "b_QP%d" % i, [128, T], BF16) for i in range(2)]
                KA = [sb(ph, "b_KA%d" % i, [128, T], BF16) for i in range(2)]
                KB = [sb(ph, "b_KB%d" % i, [128, T], BF16) for i in range(2)]
                VP = [sb(ph, "b_VP%d" % i, [128, NBLK, 128], BF16) for i in range(2)]
                for i in range(2):
                    B.op("pool", lambda e: e.memset(KA[i][64:128, :], 0.0), w=["KA%d" % i])
                    B.op("pool", lambda e: e.memset(KB[i][0:64, :], 0.0), w=["KB%d" % i])
                NR = 10
                EB = [sb(ph, "b_E%d" % i, [128, 512], F32) for i in range(NR)]
                LK = [sb(ph, "b_LK%d" % i, [128, 512], F32) for i in range(NR)]
                HI = [sb(ph, "b_HI%d" % i, [128, 512], BF16) for i in range(NR)]
                LO = [sb(ph, "b_LO%d" % i, [128, 512], BF16) for i in range(NR)]
                SS = [sb(ph, "b_S%d" % i, [128, 512], F32) for i in range(NR)]
                WW = [sb(ph, "b_W%d" % i, [128, 512], BF16) for i in range(NR)]
                CC = [sb(ph, "b_C%d" % i, [128, 512], F32) for i in range(2)]
                OS = [sb(ph, "b_OS%d" % i, [128, 512], BF16) for i in range(2)]
                allt = list(range(9))

                def load_pair(m):
                    p = m % 2
                    B.dma(QP[p][:], sbq[:, m, :], r=[("sbq", i) for i in allt], w=["QP%d" % p])
                    B.dma(KA[p][0:64, :], sbk[0:64, m, :], r=[("sbk", i) for i in allt], w=["KA%d" % p])
                    B.dma(KB[p][64:128, :], sbk[64:128, m, :], r=[("sbk", i) for i in allt], w=["KB%d" % p])
                    B.dma(VP[p][:], sbv[:, :, m * 128:(m + 1) * 128], r=[("sbv", i) for i in allt], w=["VP%d" % p])

                jobs = []
                for h in range(8):
                    for ti, (t0, W) in enumerate(TILES):
                        nkb = (t0 + W) // 128
                        for kb in reversed(range(nkb)):
                            jobs.append((h, ti, kb, kb == nkb - 1, kb == 0))
                njobs = len(jobs)

                def kop(h):
                    p = (h // 2) % 2
                    return (KA[p], "KA%d" % p) if h % 2 == 0 else (KB[p], "KB%d" % p)

                def mask_idx(ti, kb, t0):
                    c = kb - t0 // 128
                    if ti == 0:
                        return 4
                    if c >= 0:
                        return c
                    if kb == 0:
                        return 5
                    return None

                def colo(ti, kb, t0):
                    c = kb - t0 // 128
                    return c * 128 if (ti > 0 and c > 0) else 0

                def stage1(ji):
                    h, ti, kb, first, last = jobs[ji]
                    t0, W = TILES[ti]
                    c0 = colo(ti, kb, t0)
                    p = (h // 2) % 2
                    Kx, kk = kop(h)
                    i = ji % NR
                    zb = ji % 2
                    B.mm(ps[zb][:, c0:W], Kx[:, kb * 128:(kb + 1) * 128], QP[p][:, t0 + c0:t0 + W], True, True,
                         r=[kk, "QP%d" % p], w=[("ps", zb)])
                    B.op("act", lambda e: e.activation(out=EB[i][:, c0:W], in_=ps[zb][:, c0:W], func=AF.Exp),
                         r=[("ps", zb)], w=["E%d" % i])
                    B.op("act", lambda e: e.activation(out=LK[i][:, c0:W], in_=EB[i][:, c0:W], func=AF.Ln, bias=1.0),
                         r=["E%d" % i], w=["LK%d" % i])

                def stage1b(ji):
                    h, ti, kb, first, last = jobs[ji]
                    t0, W = TILES[ti]
                    c0 = colo(ti, kb, t0)
                    i = ji % NR
                    mi = mask_idx(ti, kb, t0)
                    if mi is not None:
                        B.op("pool", lambda e: e.tensor_tensor(out=LK[i][:, c0:W], in0=LK[i][:, c0:W], in1=MS[:, mi, c0:W],
                                                               op=ALU.mult), r=["LK%d" % i, "MS"], w=["LK%d" % i])
                    B.op("dve", lambda e: e.tensor_copy(out=HI[i][:, c0:W], in_=LK[i][:, c0:W]),
                         r=["LK%d" % i], w=["HI%d" % i])
                    B.op("pool", lambda e: e.tensor_tensor(out=LO[i][:, c0:W], in0=LK[i][:, c0:W], in1=HI[i][:, c0:W],
                                                           op=ALU.subtract),
                         r=["LK%d" % i, "HI%d" % i], w=["LO%d" % i])

                def stage2(ji):
                    h, ti, kb, first, last = jobs[ji]
                    t0, W = TILES[ti]
                    c0 = colo(ti, kb, t0)
                    p = (h // 2) % 2
                    Kx, kk = kop(h)
                    i = ji % NR
                    eb = 2 + ji % 2
                    tb = 4 + ji % 2
                    cp = (h * 9 + ti) % 2
                    Cc, kc = CC[cp], "C%d" % cp
                    B.mm(ps[eb][:, c0:W], UIb[:], HI[i][:, c0:W], True, False, r=["UIb", "HI%d" % i], w=[("ps", eb)])
                    B.mm(ps[eb][:, c0:W], UIb[:], LO[i][:, c0:W], False, False, r=["UIb", "LO%d" % i], w=[("ps", eb)])
                    B.mm(ps[eb][:, c0:W], Kx[:, kb * 128:(kb + 1) * 128], QP[p][:, t0 + c0:t0 + W], False, True,
                         r=[kk, "QP%d" % p], w=[("ps", eb)])
                    if not last:
                        B.mm(ps[tb][:, c0:W], ONb[:], HI[i][:, c0:W], True, False, r=["ONb", "HI%d" % i], w=[("ps", tb)])
                        B.mm(ps[tb][:, c0:W], ONb[:], LO[i][:, c0:W], False, True, r=["ONb", "LO%d" % i], w=[("ps", tb)])

                def stage2d(ji):
                    h, ti, kb, first, last = jobs[ji]
                    t0, W = TILES[ti]
                    c0 = colo(ti, kb, t0)
                    i = ji % NR
                    eb = 2 + ji % 2
                    tb = 4 + ji % 2
                    cp = (h * 9 + ti) % 2
                    Cc, kc = CC[cp], "C%d" % cp
                    if first:
                        B.op("dve", lambda e: e.memset(Cc[:, 0:W], 0.0), w=[kc])
                    B.op("dve", lambda e: e.tensor_tensor(out=SS[i][:, c0:W], in0=Cc[:, c0:W], in1=ps[eb][:, c0:W],
                                                          op=ALU.add), r=[("ps", eb), kc], w=["S%d" % i])
                    if not last:
                        B.op("dve", lambda e: e.tensor_tensor(out=Cc[:, c0:W], in0=Cc[:, c0:W], in1=ps[tb][:, c0:W],
                                                              op=ALU.add), r=[("ps", tb), kc], w=[kc])

                def stage2a(ji):
                    h, ti, kb, first, last = jobs[ji]
                    t0, W = TILES[ti]
                    c0 = colo(ti, kb, t0)
                    i = ji % NR
                    B.op("act", lambda e: e.activation(out=WW[i][:, c0:W], in_=SS[i][:, c0:W], func=AF.Exp),
                         r=["S%d" % i], w=["W%d" % i])
                    mi = mask_idx(ti, kb, t0)
                    if mi is not None:
                        B.op("pool", lambda e: e.tensor_tensor(out=WW[i][:, c0:W], in0=WW[i][:, c0:W], in1=MS[:, mi, c0:W],
                                                               op=ALU.mult), r=["W%d" % i, "MS"], w=["W%d" % i])

                def stage3(ji):
                    h, ti, kb, first, last = jobs[ji]
                    t0, W = TILES[ti]
                    c0 = colo(ti, kb, t0)
                    p = (h // 2) % 2
                    i = ji % NR
                    ob = 6 + (h * 9 + ti) % 2
                    if ti == 0 and first and h % 2 == 0 and h + 2 < 8:
                        load_pair(h // 2 + 1)
                    B.mm(ps[ob][:, c0:W], VP[p][:, kb, :], WW[i][:, c0:W], first, last,
                         r=["VP%d" % p, "W%d" % i], w=[("ps", ob)], skip_group_check=True)
                    if last:
                        oi = (h * 9 + ti) % 2
                        r0 = (h % 2) * 64
                        B.evac(OS[oi][r0:r0 + 64, 0:W], ps[ob][r0:r0 + 64, 0:W], r=[("ps", ob)], w=["OS%d" % oi])
                        B.dma(mix[r0:r0 + 64, h // 2, t0:t0 + W], OS[oi][r0:r0 + 64, 0:W], r=["OS%d" % oi],
                              w=[("mix", ti)])

                load_pair(0)
                stages = [(stage1, 0), (stage1b, 1), (stage2, 3), (stage2d, 4), (stage2a, 5), (stage3, 7)]
                for step in range(njobs + 7):
                    for fn, lag in stages:
                        if 0 <= step - lag < njobs:
                            fn(step - lag)
                B.barrier()

        def phase_B2():
            with ExitStack() as ph:
                MM = sb(ph, "m_MM", [128, 6, 512], F32)
                ONb = sb(ph, "m_ONb", [128, 128], BF16)
                B.dma(MM[:], c_mmla, w=["MM"])
                B.op("pool", lambda e: e.memset(ONb[:], 1.0), w=["mONb"])
                QT = [sb(ph, "m_QT%d" % i, [128, T], BF16) for i in range(2)]
                KT = [sb(ph, "m_KT%d" % i, [128, T], BF16) for i in range(2)]
                VP = [sb(ph, "m_VP%d" % i, [128, NBLK, 128], BF16) for i in range(2)]
                NR = 6
                PP = [sb(ph, "m_P%d" % i, [128, 512], BF16) for i in range(NR)]
                RC = [sb(ph, "m_RC%d" % i, [128, 512], F32) for i in range(2)]
                OS = [sb(ph, "m_OS%d" % i, [128, 512], BF16) for i in range(2)]
                allt = list(range(9))

                def load_head(h):
                    p = h % 2
                    B.dma(QT[p][:], mq[:, h, :], r=[("mq", i) for i in allt] + ["mqz"], w=["mQT%d" % p])
                    B.dma(KT[p][:], mk[:, h, :], r=[("mk", i) for i in allt] + ["mkz"], w=["mKT%d" % p])
                    if h % 2 == 0:
                        pp = (h // 2) % 2
                        B.dma(VP[pp][:], mv[:, :, (h // 2) * 128:(h // 2 + 1) * 128], r=[("mv", i) for i in allt],
                              w=["mVP%d" % pp])

                jobs = []
                for h in range(8):
                    for ti, (t0, W) in enumerate(TILES):
                        nkb = (t0 + W) // 128
                        for kb in reversed(range(nkb)):
                            jobs.append((h, ti, kb, kb == nkb - 1, kb == 0))
                njobs = len(jobs)

                def stage1(ji):
                    h, ti, kb, first, last = jobs[ji]
                    t0, W = TILES[ti]
                    cc_ = kb - t0 // 128
                    c0 = cc_ * 128 if (ti > 0 and cc_ > 0) else 0
                    p = h % 2
                    i = ji % NR
                    sbk_ = ji % 3
                    B.mm(ps[sbk_][:, c0:W], KT[p][:, kb * 128:(kb + 1) * 128], QT[p][:, t0 + c0:t0 + W], True, True,
                         r=["mKT%d" % p, "mQT%d" % p], w=[("ps", sbk_)])
                    B.op("act", lambda e: e.activation(out=PP[i][:, c0:W], in_=ps[sbk_][:, c0:W], func=AF.Exp,
                                                       scale=MLA_SCALE), r=[("ps", sbk_)], w=["P%d" % i])

                def stage1m(ji):
                    h, ti, kb, first, last = jobs[ji]
                    t0, W = TILES[ti]
                    cc_ = kb - t0 // 128
                    c0 = cc_ * 128 if (ti > 0 and cc_ > 0) else 0
                    i = ji % NR
                    c = kb - t0 // 128
                    mi = 4 if ti == 0 else (c if c >= 0 else (5 if kb == 0 else None))
                    if mi is not None:
                        B.op("dve", lambda e: e.tensor_tensor(out=PP[i][:, c0:W], in0=PP[i][:, c0:W], in1=MM[:, mi, c0:W],
                                                              op=ALU.mult), r=["P%d" % i, "MM"], w=["P%d" % i])

                def stage2(ji):
                    h, ti, kb, first, last = jobs[ji]
                    t0, W = TILES[ti]
                    cc_ = kb - t0 // 128
                    c0 = cc_ * 128 if (ti > 0 and cc_ > 0) else 0
                    pp = (h // 2) % 2
                    i = ji % NR
                    par = (h * 9 + ti) % 2
                    ob = 4 + par
                    db = 6 + par
                    if ti == 0 and first and h + 1 < 8:
                        load_head(h + 1)
                    B.mm(ps[ob][:, c0:W], VP[pp][:, kb, :], PP[i][:, c0:W], first, last,
                         r=["mVP%d" % pp, "P%d" % i], w=[("ps", ob)], skip_group_check=True)
                    B.mm(ps[db][:, c0:W], ONb[:], PP[i][:, c0:W], first, last,
                         r=["mONb", "P%d" % i], w=[("ps", db)], skip_group_check=True)
                    if last:
                        r0 = (h % 2) * 64
                        B.op("dve", lambda e: e.reciprocal(out=RC[par][r0:r0 + 64, 0:W], in_=ps[db][r0:r0 + 64, 0:W]),
                             r=[("ps", db)], w=["RC%d" % par])
                        B.op("dve", lambda e: e.tensor_tensor(out=OS[par][r0:r0 + 64, 0:W], in0=ps[ob][r0:r0 + 64, 0:W],
                                                              in1=RC[par][r0:r0 + 64, 0:W], op=ALU.mult),
                             r=[("ps", ob), "RC%d" % par], w=["mOS%d" % par])
                        B.dma(mix[r0:r0 + 64, 4 + h // 2, t0:t0 + W], OS[par][r0:r0 + 64, 0:W], r=["mOS%d" % par],
                              w=[("mix", ti)])

                load_head(0)
                for step in range(njobs + 3):
                    for fn, lag in ((stage1, 0), (stage1m, 1), (stage2, 3)):
                        if 0 <= step - lag < njobs:
                            fn(step - lag)
                B.barrier()

        def phase_B3():
            with ExitStack() as ph:
                RQ = sb(ph, "r_RQ", [64, 4, T], BF16)
                RQD = sb(ph, "r_RQD", [64, 4, T], BF16)
                RK = sb(ph, "r_RK", [64, 4, T], BF16)
                RV = sb(ph, "r_RV", [128, NBLK, 512], BF16)
                RKD = sb(ph, "r_RKD", [128, NBLK, 256], BF16)
                DT = sb(ph, "r_DT", [128, 512], F32)
                allt = list(range(9))
                for h in range(4):
                    r0 = (h % 2) * 64
                    B.dma(RQ[:, h, :], rq[r0:r0 + 64, h // 2, :], r=[("rq", i) for i in allt], w=["RQ"])
                    B.dma(RQD[:, h, :], rqd[r0:r0 + 64, h // 2, :], r=[("rqd", i) for i in allt], w=["RQD"])
                    B.dma(RK[:, h, :], rk[r0:r0 + 64, h // 2, :], r=[("rk", i) for i in allt], w=["RK"])
                B.dma(RV[:], rv, r=[("rv", i) for i in allt], w=["RV"])
                B.dma(RKD[:], rkd, r=[("rkd", i) for i in allt], w=["RKD"])
                B.dma(DT[:], c_dt, w=["DT"])
                ST = sb(ph, "r_ST", [64, 4, 128], F32)
                STB = sb(ph, "r_STB", [64, 4, 128], BF16)
                B.op("pool", lambda e: e.memset(ST[:], 0.0), w=["ST"])
                B.op("pool", lambda e: e.memset(STB[:], 0.0), w=["STB"])
                IND = [sb(ph, "r_IND%d" % i, [128, 512], BF16) for i in range(2)]
                YS = [sb(ph, "r_YS%d" % i, [128, 512], F32) for i in range(2)]
                SQ = [sb(ph, "r_SQ%d" % i, [128, 512], F32) for i in range(2)]
                R = [sb(ph, "r_R%d" % i, [128, 512], F32) for i in range(2)]
                GG = [sb(ph, "r_GG%d" % i, [128, 4, 128], F32) for i in range(2)]
                OC = [sb(ph, "r_OC%d" % i, [128, 4, 128], BF16) for i in range(2)]
                cdec = [float(np.float64(g) ** 128.0) for g in RET_GAMMA]
                for n in range(NBLK):
                    p = n % 2
                    tsl = slice(n * 128, (n + 1) * 128)
                    ti = 0 if n == 0 else 1 + (n - 1) // 4
                    B.dma(GG[p][:], rg[:, :, tsl], r=[("rg", ti)], w=["GG%d" % p])
                    bi = bank()
                    for h in range(4):
                        B.mm(ps[bi][:, h * 128:(h + 1) * 128], RK[:, h, tsl], RQ[:, h, tsl],
                             True, True, r=["RK", "RQ"], w=[("ps", bi)])
                    B.op("dve", lambda e: e.tensor_tensor(out=IND[p][:], in0=ps[bi][:], in1=DT[:], op=ALU.mult),
                         r=[("ps", bi), "DT"], w=["IND%d" % p])
                    by = bank()
                    for h in range(4):
                        B.mm(ps[by][:, h * 128:(h + 1) * 128], RV[:, n, h * 128:(h + 1) * 128],
                             IND[p][:, h * 128:(h + 1) * 128], True, False, r=["RV", "IND%d" % p], w=[("ps", by)])
                        B.mm(ps[by][:, h * 128:(h + 1) * 128], STB[:, h, :], RQD[:, h, tsl],
                             False, True, r=["STB", "RQD"], w=[("ps", by)])
                    if n + 1 < NBLK:
                        bs = bank()
                        for h in range(4):
                            B.mm(ps[bs][0:64, h * 128:(h + 1) * 128], RKD[:, n, h * 64:(h + 1) * 64],
                                 RV[:, n, h * 128:(h + 1) * 128], True, True, r=["RKD", "RV"], w=[("ps", bs)])
                        for h in range(4):
                            B.op("dve", lambda e: e.scalar_tensor_tensor(
                                out=ST[:, h, :], in0=ST[:, h, :], scalar=cdec[h],
                                in1=ps[bs][0:64, h * 128:(h + 1) * 128], op0=ALU.mult, op1=ALU.add),
                                r=["ST", ("ps", bs)], w=["ST"])
                        B.op("pool", lambda e: e.tensor_copy(out=STB[:], in_=ST[:]), r=["ST"], w=["STB"])
                    B.op("act", lambda e: e.copy(out=YS[p][:], in_=ps[by][:]), r=[("ps", by)], w=["YS%d" % p])
                    bm = bank()
                    B.mm(ps[bm][:], onesF[:], YS[p][:], True, True, r=["onesF", "YS%d" % p], w=[("ps", bm)])
                    B.op("dve", lambda e: e.scalar_tensor_tensor(out=YS[p][:], in0=ps[bm][:], scalar=-1.0 / 128,
                                                                 in1=YS[p][:], op0=ALU.mult, op1=ALU.add),
                         r=[("ps", bm), "YS%d" % p], w=["YS%d" % p])
                    B.op("act", lambda e: e.activation(out=SQ[p][:], in_=YS[p][:], func=AF.Square),
                         r=["YS%d" % p], w=["rSQ%d" % p])
                    bv = bank()
                    B.mm(ps[bv][:], onesF[:], SQ[p][:], True, True, r=["onesF", "rSQ%d" % p], w=[("ps", bv)])
                    B.op("act", lambda e: e.activation(out=R[p][:], in_=ps[bv][:], func=AF.Ln, scale=1.0 / 128,
                                                       bias=LN_EPS), r=[("ps", bv)], w=["rR%d" % p])
                    B.op("act", lambda e: e.activation(out=R[p][:], in_=R[p][:], func=AF.Exp, scale=-0.5),
                         r=["rR%d" % p], w=["rR%d" % p])
                    B.op("dve", lambda e: e.tensor_tensor(out=YS[p][:], in0=YS[p][:], in1=R[p][:], op=ALU.mult),
                         r=["YS%d" % p, "rR%d" % p], w=["YS%d" % p])
                    B.op("pool", lambda e: e.tensor_tensor(out=OC[p][:], in0=YS[p][:].rearrange("p (h n) -> p h n", h=4),
                                                           in1=GG[p][:], op=ALU.mult),
                         r=["YS%d" % p, "GG%d" % p], w=["OC%d" % p])
                    B.dma(mix[:, 8:12, tsl], OC[p][:], r=["OC%d" % p], w=[("mix", ti)])
                B.barrier()

        def phase_C(l):
            with ExitStack() as ph:
                WO = sb(ph, "c_WO", [128, 12, D], BF16)
                B.dma(WO[:], wo[l].rearrange("(c p) n -> p c n", p=128), w=["WO"], q="pool")
                MX = [sb(ph, "c_MX%d" % i, [128, 12, 512], BF16) for i in range(2)]
                X = [sb(ph, "c_X%d" % i, [128, 8, 512], F32) for i in range(2)]
                SQ = sb(ph, "c_SQ", [128, 8, 512], F32)
                R = sb(ph, "c_R", [128, 512], F32)
                XB = sb(ph, "c_XB", [128, 8, 512], BF16)

                def load_tile(ti):
                    t0, W = TILES[ti]
                    p = ti % 2
                    B.dma(MX[p][:, :, 0:W], mix[:, :, t0:t0 + W], r=[("mix", ti)], w=["MX%d" % p])
                    B.dma(X[p][:, :, 0:W], hres[:, :, t0:t0 + W], r=[("hres", t0)], w=["cX%d" % p])

                load_tile(0)
                for ti, (t0, W) in enumerate(TILES):
                    p = ti % 2
                    if ti + 1 < len(TILES):
                        load_tile(ti + 1)
                    for dc in range(8):
                        b = bank()
                        for m in range(12):
                            B.mm(ps[b][:, 0:W], WO[:, m, dc * 128:(dc + 1) * 128], MX[p][:, m, 0:W], m == 0, m == 11,
                                 r=["WO", "MX%d" % p], w=[("ps", b)])
                        B.op("dve", lambda e, dc=dc, b=b: e.scalar_tensor_tensor(
                            out=X[p][:, dc, 0:W], in0=X[p][:, dc, 0:W], scalar=DN_ALPHA, in1=ps[b][:, 0:W],
                            op0=ALU.mult, op1=ALU.add), r=["cX%d" % p, ("ps", b)], w=["cX%d" % p])
                    ln_f(None, X[p], "cX%d" % p, SQ[:], ["SQ"], R, XB, W, 1 + 2 * l, t0, False)
                B.barrier()

        def phase_D(l, final):
            with ExitStack() as ph:
                W1 = sb(ph, "d_W1", [128, 8, 4096], BF16)
                W2 = sb(ph, "d_W2", [128, 32, D], BF16)
                for c in range(8):
                    B.dma(W1[:, c, :], w1[l, c * 128:(c + 1) * 128, :], w=["W1"], q="pool")
                for c in range(4):
                    B.dma(W2[:, c * 8:(c + 1) * 8, :], w2[l, c * 1024:(c + 1) * 1024, :].rearrange("(c p) n -> p c n", p=128),
                          w=["W2"], q="pool")
                HB = [sb(ph, "d_HB%d" % i, [128, 8, 512], BF16) for i in range(2)]
                X = sb(ph, "d_X", [128, 8, 512], F32)
                R = sb(ph, "d_R", [128, 512], F32)
                XB = sb(ph, "d_XB", [128, 8, 512], BF16)
                UT = sb(ph, "d_UT", [128, 16, 512], BF16)
                SQ = UT[:].rearrange("p a n -> p (a n)").bitcast(F32).rearrange("p (c n) -> p c n", c=8)
                kut = [("UT", fc) for fc in range(16)]
                RL = [sb(ph, "d_RL%d" % i, [128, 512], F32) for i in range(3)]
                phd = {}
                if final:
                    phd["OT"] = [sb(ph, "d_OT%d" % i, [128, D], F32) for i in range(2)]

                def load_tile(ti):
                    t0, W = TILES[ti]
                    B.dma(HB[ti % 2][:, :, 0:W], hT[:, :, t0:t0 + W], r=[("hT", t0)], w=["HB%d" % (ti % 2)])

                load_tile(0)
                for ti, (t0, W) in enumerate(TILES):
                    p = ti % 2
                    if ti + 1 < len(TILES):
                        load_tile(ti + 1)
                    B.dma(X[:, :, 0:W], hres[:, :, t0:t0 + W], r=[("hres", t0)], w=["dX"])
                    for half in range(2):
                        for f in range(16):
                            fc = half * 16 + f
                            b = bank()
                            for c in range(8):
                                B.mm(ps[b][:, 0:W], W1[:, c, fc * 128:(fc + 1) * 128], HB[p][:, c, 0:W], c == 0, c == 7,
                                     r=["W1", "HB%d" % p], w=[("ps", b)])
                            i = fc % 3
                            B.op("act", lambda e: e.activation(out=RL[i][:, 0:W], in_=ps[b][:, 0:W], func=AF.Relu),
                                 r=[("ps", b)], w=["RL%d" % i])
                            eng = "pool" if fc % 2 == 0 else "dve"
                            B.op(eng, lambda e: e.tensor_tensor(out=UT[:, f, 0:W], in0=RL[i][:, 0:W],
                                                                in1=RL[i][:, 0:W], op=ALU.mult),
                                 r=["RL%d" % i], w=[("UT", f)])
                        for dc in range(8):
                            b = bank()
                            for f in range(16):
                                fc = half * 16 + f
                                B.mm(ps[b][:, 0:W], W2[:, fc, dc * 128:(dc + 1) * 128], UT[:, f, 0:W], f == 0, f == 15,
                                     r=["W2", ("UT", f)], w=[("ps", b)])
                            B.op("dve", lambda e: e.scalar_tensor_tensor(
                                out=X[:, dc, 0:W], in0=X[:, dc, 0:W], scalar=(DN_ALPHA if half == 0 else 1.0),
                                in1=ps[b][:, 0:W], op0=ALU.mult, op1=ALU.add), r=["dX", ("ps", b)], w=["dX"])
                    ln_f(phd, X, "dX", SQ, kut, R, XB, W, 2 + 2 * l, t0, final)
                B.barrier()

        import os as _os
        PH = _os.environ.get("KPH", "E,A,B1,B2,B3,C,D").split(",")
        if "E" in PH:
            phase_embed()
        for l in range(NL):
            if "A" in PH:
                phase_A(l)
            if "B1" in PH:
                phase_B1()
            if "B2" in PH:
                phase_B2()
            if "B3" in PH:
                phase_B3()
            if "C" in PH:
                phase_C(l)
            if "D" in PH:
                phase_D(l, l == NL - 1)
        B.finish()
        print("instr counts", B.cnt, "sems", len(B.sems), flush=True)
    return nc


def _consts():
    c = {}
    c["c_ident"] = np.eye(128, dtype=np.float32)
    j = np.arange(128)
    c["c_nuincl"] = -(j[:, None] >= j[None, :]).astype(np.float32)
    pos = (np.arange(T) - 112).astype(np.float32)

    def tables(half):
        inv = (np.float32(10000.0) ** (-np.arange(half, dtype=np.float32) / np.float32(half))).astype(np.float32)
        ang = (pos[None, :] * inv[:, None]).astype(np.float32)
        return np.cos(ang).astype(np.float32), np.sin(ang).astype(np.float32)

    c32, s32 = tables(32)
    C = np.concatenate([c32, c32, c32, c32], 0)
    S = np.concatenate([-s32, s32, -s32, s32], 0)
    c["c_ropeR"] = np.ascontiguousarray(np.stack([C, S, 0.125 * C, 0.125 * S], 1).astype(np.float32))
    c16, s16 = tables(16)
    CM = np.concatenate([np.ones((64, T), np.float32), c16, c16], 0)
    SM = np.concatenate([np.zeros((64, T), np.float32), -s16, s16], 0)
    c["c_ropeM"] = np.ascontiguousarray(np.stack([CM, SM], 1).astype(np.float32))
    g = np.array(RET_GAMMA, np.float64)
    n = np.arange(128, dtype=np.float64)
    qd = np.zeros((128, 2, 128), np.float64)
    for h in range(4):
        qd[(h % 2) * 64:(h % 2) * 64 + 64, h // 2, :] = (g[h] ** (n + 1.0))[None, :]
    c["c_qdec"] = qd.astype(np.float32)
    kd = np.zeros((128, 256), np.float64)
    for h in range(4):
        kd[:, h * 64:(h + 1) * 64] = (g[h] ** (127.0 - n))[:, None]
    c["c_kdecT"] = kd.astype(np.float32)
    dt = np.zeros((128, 4, 128), np.float64)
    diff = n[None, :] - n[:, None]
    for h in range(4):
        dt[:, h, :] = np.where(diff >= 0, g[h] ** np.maximum(diff, 0.0), 0.0)
    c["c_dt"] = dt.reshape(128, 512).astype(np.float32)
    cv = np.zeros((128, 2), np.float64)
    for h in range(4):
        cv[(h % 2) * 64:(h % 2) * 64 + 64, h // 2] = g[h] ** 128.0
    c["c_cvec"] = cv.astype(np.float32)
    s_ = np.arange(128)[:, None]
    t_ = np.arange(512)[None, :]
    msb = np.zeros((128, 6, 512), np.float32)
    mml = np.zeros((128, 6, 512), np.float32)
    for cc in range(4):
        msb[:, cc, :] = ((cc * 128 + s_) < t_)
        mml[:, cc, :] = ((cc * 128 + s_) <= t_)
    valid = (s_ >= 112)
    msb[:, 4, :] = ((s_ < t_) & valid)
    msb[:, 5, :] = np.broadcast_to(valid, (128, 512))
    mml[:, 4, :] = ((s_ <= t_) & (valid | (s_ == t_)))
    mml[:, 5, :] = np.broadcast_to(valid, (128, 512))
    c["c_msb"] = msb
    c["c_mmla"] = mml
    c["c_valid"] = valid.astype(np.float32).reshape(128, 1)
    return c


def _rot_idx(nheads, d):
    half = d // 2
    idx = []
    for h in range(nheads):
        idx += [h * d + ((i + half) % d) for i in range(d)]
    return np.array(idx)


def _prep_weights(inp, NL):
    w_in = np.asarray(inp["w_in"])[:NL]
    b_sbq, b_sbk, b_sbv, b_cq, b_ckv, b_kr, b_rq, b_rk, b_rv, b_rg = 0, 512, 1024, 1536, 1920, 2176, 2208, 2464, 2720, 3232
    cols = np.concatenate([
        np.arange(b_sbq, b_sbq + 512), np.arange(b_sbk, b_sbk + 512), np.arange(b_sbv, b_sbv + 512),
        np.arange(b_cq, b_cq + 384), np.arange(b_ckv, b_ckv + 256),
        np.arange(b_kr, b_kr + 32), b_kr + _rot_idx(1, 32),
        np.arange(b_rq, b_rq + 256), b_rq + _rot_idx(4, 64),
        np.arange(b_rk, b_rk + 256), b_rk + _rot_idx(4, 64),
        np.arange(b_rv, b_rv + 512), np.arange(b_rg, b_rg + 512)])
    assert cols.size == NA
    wa = np.ascontiguousarray(np.take(w_in, cols, axis=2))
    w_uq = np.asarray(inp["w_uq"])[:NL]
    ucols = []
    for h in range(8):
        base = h * 96
        xc = list(range(base, base + 96))
        rc = list(range(base, base + 64)) + [base + 64 + ((i + 16) % 32) for i in range(32)]
        ucols += xc + rc
    wuq = np.ascontiguousarray(np.take(w_uq, np.array(ucols), axis=2))
    w_ukv = np.asarray(inp["w_ukv"])[:NL]
    kc = np.concatenate([np.arange(h * 128, h * 128 + 64) for h in range(8)] +
                        [np.arange(h * 128 + 64, h * 128 + 128) for h in range(8)])
    wukv = np.ascontiguousarray(np.take(w_ukv, kc, axis=2))
    lnp = np.zeros((9, 2, 1024), np.float32)
    lnp[0, 0] = np.asarray(inp["ln_emb_g"])
    lnp[0, 1] = np.asarray(inp["ln_emb_b"])
    for l in range(NLAYERS):
        lnp[1 + 2 * l, 0] = np.asarray(inp["ln1_g"])[l]
        lnp[1 + 2 * l, 1] = np.asarray(inp["ln1_b"])[l]
        lnp[2 + 2 * l, 0] = np.asarray(inp["ln2_g"])[l]
        lnp[2 + 2 * l, 1] = np.asarray(inp["ln2_b"])[l]
    lnp = np.ascontiguousarray(lnp.reshape(9, 2, 8, 128).transpose(3, 0, 1, 2))
    qn = np.ascontiguousarray(np.asarray(inp["mla_q_norm"]).reshape(NLAYERS, 3, 128).transpose(2, 0, 1))
    kvn = np.ascontiguousarray(np.asarray(inp["mla_kv_norm"]).reshape(NLAYERS, 2, 128).transpose(2, 0, 1))
    shared = {
        "meta": np.ascontiguousarray(np.asarray(inp["meta_tokens"], dtype=np.float32)),
        "lnp": lnp.astype(np.float32), "qn": qn.astype(np.float32), "kvn": kvn.astype(np.float32),
        "wa": wa.astype(np.float32), "wuq": wuq.astype(np.float32), "wukv": wukv.astype(np.float32),
        "wo": np.ascontiguousarray(np.asarray(inp["w_out"], dtype=np.float32)[:NL]),
        "w1": np.ascontiguousarray(np.asarray(inp["w_ff1"], dtype=np.float32)[:NL]),
        "w2": np.ascontiguousarray(np.asarray(inp["w_ff2"], dtype=np.float32)[:NL]),
    }
    shared.update(_consts())
    return shared


def run(inp, NL=NLAYERS, cores=8):
    shared = _prep_weights(inp, NL)
    x = np.asarray(inp["x"], dtype=np.float32)
    nc = build_nc(NL)
    in_maps = []
    for b in range(cores):
        m = dict(shared)
        m["x"] = np.ascontiguousarray(x[b])
        in_maps.append(m)
    if _TRACE:
        res = run_bass_kernel_spmd(nc, in_maps, core_ids=list(range(cores)), trace=True)
        print("TRACE exec_time_ns", res.exec_time_ns, flush=True)
    else:
        res = run_bass_kernel_spmd(nc, in_maps, core_ids=list(range(cores)))
    return np.stack([np.asarray(r["out"]) for r in res.results], axis=0).astype(np.float32)


def kernel(**inputs):
    return run(inputs, NLAYERS, 8)
```
